# Optimizing a Trainium2 kernel written in Bass

```python
import math
import jax, jax.numpy as jnp
from jax import lax
import numpy as np

D_MODEL = 1024
BATCH = 16
SEQ = 2048
DEPTH = 4

CTX_LEN = 256
GRID_W = 64
EPS = 1e-6
BRANCH_W = 512
N_BRANCH = 3
CONV_K = 4
DK_A = 128
H_A = BRANCH_W // DK_A
W_A = H_A * DK_A
CHUNK = 64
W_B = BRANCH_W
NB_B = 8
BW_B = W_B // NB_B
RGLRU_C = 8.0
DH_C = 128
H_C = BRANCH_W // DH_C
W_C = H_C * DH_C
WIN_R = 8
WIN_C = 16
QCB = 16
KCB = 32
ROPE_BASE = 10000.0
SPLIT_SIZES = (3 * W_A, W_A, 2 * H_A, 2 * H_A, W_B, W_B, 3 * W_C, W_C, N_BRANCH * D_MODEL)
N_IN = sum(SPLIT_SIZES)

kernel_name = "hybrid_gdn_rglru_natten_prefix_trunk"


def rms_norm(x, g):
    xf = x.astype(jnp.float32)
    y = xf * lax.rsqrt(jnp.mean(xf * xf, axis=-1, keepdims=True) + EPS)
    return (y * g.astype(jnp.float32)).astype(x.dtype)


def l2_normalize(x):
    xf = x.astype(jnp.float32)
    return (xf * lax.rsqrt(jnp.sum(xf * xf, axis=-1, keepdims=True) + EPS)).astype(x.dtype)


def same(t):
    return t


def flip_seq(t):
    return jnp.flip(t, axis=1)


def conv_centred(x, w):
    k = w.shape[0]
    return lax.conv_general_dilated(
        x, w[:, None, :], window_strides=(1,), padding=[(k // 2, k - 1 - k // 2)],
        dimension_numbers=("NWC", "WIO", "NWC"), feature_group_count=x.shape[-1])


def axial_rope(x, rows, cols):
    half = x.shape[-1] // 2
    quarter = half // 2
    inv_freq = ROPE_BASE ** (-jnp.arange(quarter, dtype=jnp.float32) / quarter)

    def rotate(xa, pos):
        ang = pos.astype(jnp.float32)[:, None] * inv_freq
        cos = jnp.cos(ang)[None, :, None, :].astype(x.dtype)
        sin = jnp.sin(ang)[None, :, None, :].astype(x.dtype)
        x1, x2 = xa[..., :quarter], xa[..., quarter:]
        return jnp.concatenate([x1 * cos - x2 * sin, x2 * cos + x1 * sin], axis=-1)

    return jnp.concatenate([rotate(x[..., :half], rows), rotate(x[..., half:], cols)], axis=-1)


def gated_delta_rule(q, k, v, g, beta, s0):
    out_dtype = v.dtype
    bsz, length, heads, dk = q.shape
    dv = v.shape[-1]
    n = length // CHUNK
    f32 = jnp.float32

    def blocks(t):
        t = t.astype(f32).reshape(bsz, n, CHUNK, heads, *t.shape[3:])
        return jnp.moveaxis(t, (1, 3), (0, 2))

    qb, kb, vb, gb, bb = blocks(q), blocks(k), blocks(v), blocks(g), blocks(beta)
    gc = jnp.cumsum(gb, axis=-1)
    idx = jnp.arange(CHUNK)
    incl = idx[:, None] >= idx[None, :]
    strict = idx[:, None] > idx[None, :]
    diff = gc[..., :, None] - gc[..., None, :]
    decay = jnp.where(incl, jnp.exp(jnp.where(incl, diff, 0.0)), 0.0)
    k_beta = kb * bb[..., None]
    a_mat = jnp.where(strict, jnp.einsum("nbhid,nbhjd->nbhij", k_beta, kb) * decay, 0.0)
    lhs = a_mat + jnp.eye(CHUNK, dtype=f32)
    rhs = jnp.concatenate([vb * bb[..., None], k_beta * jnp.exp(gc)[..., None]], axis=-1)
    sol = lax.linalg.triangular_solve(lhs, rhs, left_side=True, lower=True, unit_diagonal=True)
    w_v, w_k = sol[..., :dv], sol[..., dv:]
    qk = jnp.einsum("nbhid,nbhjd->nbhij", qb, kb) * decay
    q_dec = qb * jnp.exp(gc)[..., None]
    k_tail = kb * jnp.exp(gc[..., -1:] - gc)[..., None]
    g_tail = jnp.exp(gc[..., -1])

    def step(s, xs):
        wv, wk, qk_c, qd, kt, gt = xs
        u = wv - jnp.einsum("bhik,bhkv->bhiv", wk, s)
        o = jnp.einsum("bhik,bhkv->bhiv", qd, s) + jnp.einsum("bhij,bhjv->bhiv", qk_c, u)
        s = s * gt[..., None, None] + jnp.einsum("bhik,bhiv->bhkv", kt, u)
        return s, o

    s_final, o = lax.scan(step, s0.astype(f32), (w_v, w_k, qk, q_dec, k_tail, g_tail))
    o = jnp.moveaxis(o, (0, 2), (1, 3)).reshape(bsz, length, heads, dv)
    return o.astype(out_dtype), s_final


def gdn_bidirectional(lat, ctx):
    ql, kl, vl, gl, bl = lat
    qc, kc, vc, gc, bc = ctx
    s0 = jnp.zeros((ql.shape[0], H_A, DK_A, DK_A), jnp.float32)
    outs_lat, outs_ctx = [], []
    for d in range(2):
        f = flip_seq if d == 1 else same
        oc, sc = gated_delta_rule(f(qc), f(kc), f(vc), f(gc[:, :, d]), f(bc[:, :, d]), s0)
        ol, _ = gated_delta_rule(f(ql), f(kl), f(vl), f(gl[:, :, d]), f(bl[:, :, d]), sc)
        outs_lat.append(f(ol))
        outs_ctx.append(f(oc))
    return outs_lat[0] + outs_lat[1], outs_ctx[0] + outs_ctx[1]


def block_diag_linear(x, w, b):
    xb = x.reshape(*x.shape[:-1], NB_B, BW_B)
    return jnp.einsum("blnd,nde->blne", xb, w).reshape(x.shape) + b


def rglru(x, wa, ba, wx, bx, lam, h0):
    xf = x.astype(jnp.float32)
    r = jax.nn.sigmoid(block_diag_linear(xf, wa, ba))
    i = jax.nn.sigmoid(block_diag_linear(xf, wx, bx))
    log_a = -RGLRU_C * r * jax.nn.softplus(-lam.astype(jnp.float32))
    a = jnp.exp(log_a)
    b = jnp.sqrt(-jnp.expm1(2.0 * log_a)) * (i * xf)
    b = b.at[:, 0].add(a[:, 0] * h0)

    def combine(lhs, rhs):
        a1, b1 = lhs
        a2, b2 = rhs
        return a1 * a2, a2 * b1 + b2

    _, h = lax.associative_scan(combine, (a, b), axis=1)
    return h


def rglru_bidirectional(x_lat, x_ctx, wa, ba, wx, bx, lam):
    h0 = jnp.zeros((x_lat.shape[0], W_B), jnp.float32)
    outs_lat, outs_ctx = [], []
    for d in range(2):
        f = flip_seq if d == 1 else same
        h_ctx = rglru(f(x_ctx), wa[d], ba[d], wx[d], bx[d], lam[d], h0)
        h_lat = rglru(f(x_lat), wa[d], ba[d], wx[d], bx[d], lam[d], h_ctx[:, -1])
        outs_lat.append(f(h_lat))
        outs_ctx.append(f(h_ctx))
    return outs_lat[0] + outs_lat[1], outs_ctx[0] + outs_ctx[1]


def neighbourhood_attention(q, k, v, k_ctx, v_ctx, rpb):
    bsz, s_len, heads, dh = q.shape
    rows = s_len // GRID_W
    wr = min(WIN_R, rows)
    ncb = GRID_W // QCB
    scale = dh ** -0.5
    q_cols = np.arange(GRID_W).reshape(ncb, QCB)
    kc_idx = np.clip(np.arange(ncb) * QCB - WIN_C // 2, 0, GRID_W - KCB)[:, None] + np.arange(KCB)
    win_c0 = np.clip(q_cols - WIN_C // 2, 0, GRID_W - WIN_C)
    col_ok = (kc_idx[:, None, :] >= win_c0[..., None]) & (kc_idx[:, None, :] < win_c0[..., None] + WIN_C)
    dc = np.clip(kc_idx[:, None, :] - q_cols[..., None], -(WIN_C - 1), WIN_C - 1) + WIN_C - 1
    rpb_cols = rpb[:, :, dc].astype(jnp.float32)
    qg = jnp.moveaxis(q.reshape(bsz, rows, ncb, QCB, heads, dh), 1, 0)
    kg = k.reshape(bsz, rows, GRID_W, heads, dh)
    vg = v.reshape(bsz, rows, GRID_W, heads, dh)
    n_loc = wr * KCB

    def row_block(args):
        r, qr = args
        r0 = jnp.clip(r - WIN_R // 2, 0, rows - wr)
        kr = lax.dynamic_slice_in_dim(kg, r0, wr, axis=1)[:, :, kc_idx]
        vr = lax.dynamic_slice_in_dim(vg, r0, wr, axis=1)[:, :, kc_idx]
        s_loc = jnp.einsum("bnqhd,bwnkhd->bhnqwk", qr, kr).astype(jnp.float32) * scale
        dr = r0 + jnp.arange(wr) - r + WIN_R - 1
        bias = jnp.transpose(jnp.take(rpb_cols, dr, axis=1), (0, 2, 3, 1, 4))
        s_loc = jnp.where(col_ok[:, :, None, :], s_loc + bias, -jnp.inf)
        s_loc = s_loc.reshape(bsz, heads, ncb, QCB, n_loc)
        s_ctx = jnp.einsum("bnqhd,bchd->bhnqc", qr, k_ctx).astype(jnp.float32) * scale
        p = jax.nn.softmax(jnp.concatenate([s_loc, s_ctx], axis=-1), axis=-1).astype(v.dtype)
        p_loc = p[..., :n_loc].reshape(bsz, heads, ncb, QCB, wr, KCB)
        return (jnp.einsum("bhnqwk,bwnkhd->bnqhd", p_loc, vr)
                + jnp.einsum("bhnqc,bchd->bnqhd", p[..., n_loc:], v_ctx))

    o = lax.map(row_block, (jnp.arange(rows), qg))
    return jnp.moveaxis(o, 0, 1).reshape(bsz, s_len, heads, dh)


def context_attention(q, k, v):
    s = jnp.einsum("bqhd,bkhd->bhqk", q, k).astype(jnp.float32) * (q.shape[-1] ** -0.5)
    p = jax.nn.softmax(s, axis=-1).astype(v.dtype)
    return jnp.einsum("bhqk,bkhd->bqhd", p, v)


def hybrid_mixer(h_lat, h_ctx, rows_idx, cols_idx, w_in, conv_a, a_log, dt_bias, onorm_a,
                 conv_b, conv_b_bias, lru_wa, lru_ba, lru_wx, lru_bx, lru_lam, rpb,
                 w_branch, w_out, with_ctx_out):
    split_at = tuple(int(i) for i in np.cumsum(SPLIT_SIZES)[:-1])
    p_lat = jnp.split(h_lat @ w_in, split_at, axis=-1)
    p_ctx = jnp.split(h_ctx @ w_in, split_at, axis=-1)

    def gdn_inputs(p, rotate):
        qkv, beta_logit, alpha_logit = p[0], p[2], p[3]
        bsz, length, _ = qkv.shape
        qkv = jax.nn.silu(conv_centred(qkv, conv_a)).reshape(bsz, length, 3, H_A, DK_A)
        q, k, v = l2_normalize(qkv[:, :, 0]), l2_normalize(qkv[:, :, 1]), qkv[:, :, 2]
        if rotate:
            q, k = axial_rope(q, rows_idx, cols_idx), axial_rope(k, rows_idx, cols_idx)
        beta = jax.nn.sigmoid(beta_logit).reshape(bsz, length, 2, H_A)
        g = -jnp.exp(a_log) * jax.nn.softplus(alpha_logit.reshape(bsz, length, 2, H_A) + dt_bias)
        return q * DK_A ** -0.5, k, v, g, beta

    def gdn_out(o, gate):
        bsz, length = o.shape[:2]
        return rms_norm(o, onorm_a).reshape(bsz, length, W_A) * jax.nn.silu(gate)

    oa_lat, oa_ctx = gdn_bidirectional(gdn_inputs(p_lat, True), gdn_inputs(p_ctx, False))

    hb_lat, hb_ctx = rglru_bidirectional(conv_centred(p_lat[4], conv_b) + conv_b_bias,
                                         conv_centred(p_ctx[4], conv_b) + conv_b_bias,
                                         lru_wa, lru_ba, lru_wx, lru_bx, lru_lam)

    def lru_out(h, gate):
        return h.astype(gate.dtype) * jax.nn.silu(gate)

    def split_heads(t):
        bsz, length, _ = t.shape
        return t.reshape(bsz, length, 3, H_C, DH_C)

    qkv_lat, qkv_ctx = split_heads(p_lat[6]), split_heads(p_ctx[6])
    oc_lat = neighbourhood_attention(qkv_lat[:, :, 0], qkv_lat[:, :, 1], qkv_lat[:, :, 2],
                                     qkv_ctx[:, :, 1], qkv_ctx[:, :, 2], rpb)

    def attn_out(o, gate):
        bsz, length = o.shape[:2]
        return o.reshape(bsz, length, W_C) * jax.nn.silu(gate)

    def merge(ya, yb, yc, gate_logits):
        gates = jax.nn.sigmoid(gate_logits.reshape(*gate_logits.shape[:-1], N_BRANCH, D_MODEL))
        merged = (gates[..., 0, :] * (ya @ w_branch[0])
                  + gates[..., 1, :] * (yb @ w_branch[1])
                  + gates[..., 2, :] * (yc @ w_branch[2]))
        return merged @ w_out

    y_lat = merge(gdn_out(oa_lat, p_lat[1]), lru_out(hb_lat, p_lat[5]),
                  attn_out(oc_lat, p_lat[7]), p_lat[8])
    if not with_ctx_out:
        return y_lat, None
    oc_ctx = context_attention(qkv_ctx[:, :, 0], qkv_ctx[:, :, 1], qkv_ctx[:, :, 2])
    y_ctx = merge(gdn_out(oa_ctx, p_ctx[1]), lru_out(hb_ctx, p_ctx[5]),
                  attn_out(oc_ctx, p_ctx[7]), p_ctx[8])
    return y_lat, y_ctx


def setup_inputs(seed: int = 0) -> dict:
    key = jax.random.key(seed)
    ks = jax.random.split(key, 24)
    f32 = jnp.float32

    def nrm(k, shape, s):
        return jax.random.normal(k, shape, f32) * s

    dt = jnp.exp(jax.random.uniform(ks[11], (DEPTH, 2, H_A), f32, math.log(1e-3), math.log(1e-1)))
    lam_a = jax.random.uniform(ks[19], (DEPTH, 2, W_B), f32, 0.9, 0.999) ** (1.0 / RGLRU_C)
    return {
        "x": nrm(ks[0], (BATCH, SEQ, D_MODEL), 1.0),
        "c": nrm(ks[1], (BATCH, D_MODEL), 1.0),
        "ctx": nrm(ks[2], (BATCH, CTX_LEN, D_MODEL), 1.0),
        "c_ctx": nrm(ks[3], (D_MODEL,), 1.0),
        "w_mod": nrm(ks[4], (DEPTH, D_MODEL, 3 * D_MODEL), 0.5 * D_MODEL ** -0.5),
        "b_mod": nrm(ks[5], (DEPTH, 3 * D_MODEL), 0.02),
        "g_pre": 1.0 + nrm(ks[6], (DEPTH, D_MODEL), 0.02),
        "g_post": 1.0 + nrm(ks[7], (DEPTH, D_MODEL), 0.02),
        "w_in": nrm(ks[8], (DEPTH, D_MODEL, N_IN), D_MODEL ** -0.5),
        "conv_a": nrm(ks[9], (DEPTH, CONV_K, 3 * W_A), CONV_K ** -0.5),
        "a_log": jnp.log(jax.random.uniform(ks[10], (DEPTH, 2, H_A), f32, 1.0, 16.0)),
        "dt_bias": dt + jnp.log(-jnp.expm1(-dt)),
        "onorm_a": 1.0 + nrm(ks[12], (DEPTH, DK_A), 0.02),
        "conv_b": nrm(ks[13], (DEPTH, CONV_K, W_B), CONV_K ** -0.5),
        "conv_b_bias": nrm(ks[14], (DEPTH, W_B), 0.02),
        "lru_wa": nrm(ks[15], (DEPTH, 2, NB_B, BW_B, BW_B), BW_B ** -0.5),
        "lru_ba": nrm(ks[16], (DEPTH, 2, W_B), 0.02),
        "lru_wx": nrm(ks[17], (DEPTH, 2, NB_B, BW_B, BW_B), BW_B ** -0.5),
        "lru_bx": nrm(ks[18], (DEPTH, 2, W_B), 0.02),
        "lru_lam": jnp.log(lam_a) - jnp.log1p(-lam_a),
        "rpb": nrm(ks[20], (DEPTH, H_C, 2 * WIN_R - 1, 2 * WIN_C - 1), 0.02),
        "w_branch": nrm(ks[21], (DEPTH, N_BRANCH, BRANCH_W, D_MODEL), BRANCH_W ** -0.5),
        "w_out": nrm(ks[22], (DEPTH, D_MODEL, D_MODEL), D_MODEL ** -0.5),
    }


def reference(x, c, ctx, c_ctx, w_mod, b_mod, g_pre, g_post, w_in, conv_a, a_log, dt_bias,
              onorm_a, conv_b, conv_b_bias, lru_wa, lru_ba, lru_wx, lru_bx, lru_lam, rpb,
              w_branch, w_out):
    s_len = x.shape[1]
    pos = jnp.arange(s_len)
    rows_idx, cols_idx = pos // GRID_W, pos % GRID_W
    silu_c = jax.nn.silu(c)
    silu_cc = jax.nn.silu(c_ctx)
    for l in range(DEPTH):
        last = l == DEPTH - 1
        shift_l, scale_l, gate_l = jnp.split(silu_c @ w_mod[l] + b_mod[l], 3, axis=-1)
        shift_c, scale_c, gate_c = jnp.split(silu_cc @ w_mod[l] + b_mod[l], 3, axis=-1)
        h_lat = rms_norm(x, g_pre[l]) * (1.0 + scale_l[:, None]) + shift_l[:, None]
        h_ctx = rms_norm(ctx, g_pre[l]) * (1.0 + scale_c) + shift_c
        y_lat, y_ctx = hybrid_mixer(h_lat, h_ctx, rows_idx, cols_idx, w_in[l], conv_a[l], a_log[l],
                                    dt_bias[l], onorm_a[l], conv_b[l], conv_b_bias[l], lru_wa[l],
                                    lru_ba[l], lru_wx[l], lru_bx[l], lru_lam[l], rpb[l],
                                    w_branch[l], w_out[l], not last)
        x = x + gate_l[:, None] * rms_norm(y_lat, g_post[l])
        if not last:
            ctx = ctx + gate_c * rms_norm(y_ctx, g_post[l])
    return x
```

```python
import contextlib
import re
import numpy as np
import ml_dtypes
import concourse.bass as bass
import concourse.mybir as mybir
from concourse.bass_utils import run_bass_kernel_spmd

F32 = mybir.dt.float32
BF16 = mybir.dt.bfloat16
AF = mybir.ActivationFunctionType
ALU = mybir.AluOpType
AX = mybir.AxisListType

D = 1024
NB = 2
CTX = 256
LAT = 2048
S = CTX + LAT
NT = NB * S
DEPTH = 4
EPS = 1e-6
ENGS = ("pe", "act", "dve", "pool", "sp")
NDMASEM = 12


_BANK_RULES = [
    (re.compile(r"^modps$"), lambda m: [0]),
    (re.compile(r"^(?:ssps|pps|lps|mps|c0ps|cps)(\d)"), lambda m: [int(m.group(1))]),
    (re.compile(r"^nS(\d)"), lambda m: [2 * int(m.group(1)), 2 * int(m.group(1)) + 1]),
    (re.compile(r"^nPTp(\d)"), lambda m: [4 + int(m.group(1))]),
    (re.compile(r"^no(\d)"), lambda m: [6 + int(m.group(1))]),
    (re.compile(r"^mss$"), lambda m: [6]),
    (re.compile(r"^cpA(\d)_"), lambda m: [2 * int(m.group(1))]),
    (re.compile(r"^cpB(\d)_"), lambda m: [2 * int(m.group(1)) + 1]),
    (re.compile(r"^cp6_"), lambda m: [6]),
    (re.compile(r"^cp7_"), lambda m: [7]),
]
_BANK_CACHE = {}


def banks_of(name):
    b = _BANK_CACHE.get(name)
    if b is None:
        b = []
        for rx, fn in _BANK_RULES:
            m = rx.match(name)
            if m:
                b = fn(m)
                break
        _BANK_CACHE[name] = b
    return b


class Res:
    __slots__ = ("w", "r")

    def __init__(self):
        self.w = None
        self.r = {}


class Prog:
    def __init__(self, nc):
        self.nc = nc
        self.ops = {e: [] for e in ENGS}
        self.cnt = {e: 0 for e in ENGS}
        self.dcnt = {e: 0 for e in ENGS}
        self.seen = {e: {} for e in ENGS}
        self.pending = {e: [] for e in ENGS}
        self.res = {}
        self.inflight = {e: {} for e in ENGS}

    def R(self, name):
        r = self.res.get(name)
        if r is None:
            r = self.res[name] = Res()
        return r

    def _need(self, eng, tok, waits):
        if tok is None:
            return
        if tok[0] == "c":
            key = ("c", tok[1]); val = tok[2]
        else:
            key = ("d", tok[1], tok[2]); val = tok[3]
        if self.seen[eng].get(key, 0) >= val:
            return
        self.seen[eng][key] = val
        waits.append((key, val))

    def _deps(self, eng, reads, writes, waits):
        for t in self.pending[eng]:
            self._need(eng, t, waits)
        self.pending[eng] = []
        banks = set()
        for r in reads:
            banks.update(banks_of(r))
        for r in writes:
            banks.update(banks_of(r))
        self._banks = banks
        for k in banks:
            r = self.R("BANK%d" % k)
            if r.w is not None and r.w[1] != eng:
                self._need(eng, r.w, waits)
        for r in reads:
            self._need(eng, self.R(r).w, waits)
        for r in writes:
            r = self.R(r)
            self._need(eng, r.w, waits)
            for t in r.r.values():
                self._need(eng, t, waits)

    def _commit(self, tok, reads, writes):
        key = tok[:2] if tok[0] == "c" else tok[:3]
        for k in self._banks:
            self.R("BANK%d" % k).w = tok
        for r in reads:
            self.R(r).r[key] = tok
        for r in writes:
            r = self.R(r)
            r.w = tok
            r.r = {}

    limit = None
    nrec = 0

    def op(self, eng, fn, reads=(), writes=()):
        if Prog.limit is not None:
            Prog.nrec += 1
            if Prog.nrec > Prog.limit:
                return None
        waits = []
        self._deps(eng, reads, writes, waits)
        self.cnt[eng] += 1
        tok = ("c", eng, self.cnt[eng])
        self.ops[eng].append((waits, fn, tok))
        self._commit(tok, reads, writes)
        return tok

    def dma(self, eng, fn, reads=(), writes=()):
        if Prog.limit is not None:
            Prog.nrec += 1
            if Prog.nrec > Prog.limit:
                return None
        waits = []
        self._deps(eng, reads, writes, waits)
        i = self.dcnt[eng]
        self.dcnt[eng] += 1
        slot = i % NDMASEM
        val = 16 * (i // NDMASEM + 1)
        if i >= NDMASEM:
            self._need(eng, ("d", eng, slot, val - 16), waits)
        tok = ("d", eng, slot, val)
        self.inflight[eng][slot] = tok
        self.ops[eng].append((waits, fn, tok))
        self._commit(tok, reads, writes)
        return tok

    def barrier(self):
        toks = []
        for e in ("pe", "act", "dve", "pool"):
            if self.cnt[e]:
                toks.append(("c", e, self.cnt[e]))
        for e in ENGS:
            toks.extend(self.inflight[e].values())
        for e in ENGS:
            self.pending[e] = list(toks)
        self.res = {}

    def emit(self, final_tokens=()):
        nc = self.nc
        with contextlib.ExitStack() as st:
            csem = {e: st.enter_context(nc.semaphore("c_" + e)) for e in ("pe", "act", "dve", "pool")}
            dsem = {}
            for e in ENGS:
                for s in range(min(NDMASEM, self.dcnt[e])):
                    dsem[(e, s)] = st.enter_context(nc.semaphore("d_%s_%d" % (e, s)))
            fw = []
            for t in self.pending["sp"]:
                self._need("sp", t, fw)
            for t in final_tokens:
                self._need("sp", t, fw)
            block = st.enter_context(nc.Block())

            def semof(key):
                return csem[key[1]] if key[0] == "c" else dsem[(key[1], key[2])]

            def replay(ename):
                def body(e):
                    for waits, fn, tok in self.ops[ename]:
                        for key, val in waits:
                            e.wait_ge(semof(key), val)
                        ins = fn(e)
                        if tok[0] == "c":
                            ins.then_inc(csem[tok[1]], 1)
                        else:
                            ins.then_inc(dsem[(tok[1], tok[2])], 16)
                    if ename == "sp":
                        for key, val in fw:
                            e.wait_ge(semof(key), val)
                return body

            block.tensor(replay("pe"))
            block.scalar(replay("act"))
            block.vector(replay("dve"))
            block.gpsimd(replay("pool"))
            block.sync(replay("sp"))


class Arena:
    def __init__(self, ap, nbytes):
        self.ap = ap
        self.nbytes = nbytes
        self.off = 0
        self.names = {}

    def reset(self):
        self.off = 0
        self.names = {}

    def alloc(self, name, shape, dt=F32):
        esz = 4 if dt == F32 else 2
        n = int(np.prod(shape))
        nb = (n * esz + 63) // 64 * 64
        assert self.off + nb <= self.nbytes, "SBUF arena overflow at %s: %d + %d > %d" % (name, self.off, nb, self.nbytes)
        v = self.ap[:, self.off // 4:(self.off + nb) // 4]
        self.off += nb
        if dt != F32:
            v = v.bitcast(dt)
        v = v[:, 0:n]
        if len(shape) == 2:
            v = v.rearrange("p (a b) -> p a b", a=shape[0])
        elif len(shape) == 3:
            v = v.rearrange("p (a b c) -> p a b c", a=shape[0], b=shape[1])
        self.names[name] = v
        return v


class Ctx:
    pass


LASTP = None


def build(nlayers=DEPTH, dbg=(), run="CDEF"):
    nc = bass.Bass("TRN2", target_bir_lowering=False)
    g = Ctx()
    g.run = run
    g.nc = nc
    g.dbg = dbg

    def din(name, shape, dt=F32):
        return nc.dram_tensor(name, list(shape), dt, kind="ExternalInput").ap()

    def dscr(name, shape, dt=F32):
        kind = "ExternalOutput" if name in dbg else "ExternalInput" if (name + "_in") in dbg else "Internal"
        return nc.dram_tensor(name, list(shape), dt, kind=kind).ap()

    g.xT_in = din("xT", [D, NT])
    g.cT = din("cT", [128, 8 * 3])
    ND = nlayers
    g.w_mod = din("w_mod", [ND, D, 3 * D])
    g.w_big = din("w_big", [ND, D, 8192])
    g.w_small = din("w_small", [ND, D, 16])
    g.prm = din("prm", [ND, 128, PRM_N])
    g.lruw = din("lruw", [ND, 128, 16 * 128])
    g.rpbias = din("rpbias", [ND, 4, 128, 5 * 640])
    g.w_br = din("w_br", [ND, 3, 512, D])
    g.w_out = din("w_out", [ND, D, D])
    g.cst = din("cst", [128, CST_N])
    g.rope = din("rope", [128, 2 * LAT])
    g.outT = nc.dram_tensor("outT", [D, NB * LAT], F32, kind="ExternalOutput").ap()

    g.xs = dscr("xs", [D, NT])
    g.PT = dscr("PT", [64, 128, NT], BF16)
    g.VC = dscr("VC", [NT, 512], BF16)
    g.BA = dscr("BA", [NT, 16])
    g.Y3 = dscr("Y3", [3, 4, 128, NT], BF16)

    with contextlib.ExitStack() as st:
        arena_t = st.enter_context(nc.sbuf_tensor("arena", [128, ARENA_BYTES // 4], F32))
        pers_t = st.enter_context(nc.sbuf_tensor("pers", [128, PERS_BYTES // 4], F32))
        g.psall = st.enter_context(nc.psum_tensor("psall", [128, 4096], F32))
        g.ps = [g.psall[:, i * 512:(i + 1) * 512] for i in range(8)]
        g.A = Arena(arena_t[:], ARENA_BYTES)
        g.PA = Arena(pers_t[:], PERS_BYTES)
        P = g.P = Prog(nc)
        global LASTP
        LASTP = P
        phase_setup(g)
        for l in range(nlayers):
            last = (l == DEPTH - 1)
            phase_mod(g, l)
            if "skipB" not in dbg:
                phase_norm_proj(g, l)
            if "stopB" in dbg:
                break
            if "C" in g.run:
                phase_gdn(g, l, last)
            if "D" in g.run:
                phase_lru(g, l)
            if "E" in g.run:
                phase_natten(g, l, last)
            if "F" in g.run:
                phase_merge(g, l, last)
        P.barrier()
        P.emit()
    return nc


def _cst_layout():
    lay = {}
    off = 0
    names = ["ident", "ones", "tri0", "tri1", "mit0", "mit1", "mst0", "mst1"] + ["lm%d" % i for i in range(7)] + ["prot"]
    for name, n in [(nm, 128) for nm in names]:
        lay[name] = (off, n)
        off += n
    return lay, off


CST_LAY, CST_N = _cst_layout()


def _prm_layout():
    lay = {}
    off = 0
    for name, n in (("bmodT", 24), ("g_pre", 8), ("g_post", 8), ("conv_b", 16), ("conv_b_bias", 4),
                    ("lru_ba", 8), ("lru_bx", 8), ("lru_lam", 8), ("conv_a", 48), ("onorm_a", 1),
                    ("dt_bias", 8), ("a_log", 8)):
        lay[name] = (off, n)
        off += n
    return lay, off


PRM_LAY, PRM_N = _prm_layout()

ARENA_BYTES = 190 * 1024
PERS_BYTES = 16 * 1024


def host_consts():
    c = np.zeros((128, CST_N), np.float32)
    o, n = CST_LAY["ident"]; c[:, o:o + n] = np.eye(128, dtype=np.float32)
    o, n = CST_LAY["ones"]; c[:, o:o + n] = 1.0
    p = np.arange(128)[:, None]; f = np.arange(128)[None, :]

    def put(name, m):
        o, n = CST_LAY[name]; c[:, o:o + n] = m.astype(np.float32)
    put("tri0", p <= f); put("tri1", p >= f)
    put("mit0", f >= p); put("mit1", f <= p)
    put("mst0", f > p); put("mst1", f < p)
    for lv in range(7):
        put("lm%d" % lv, ((p >> (lv + 1)) == (f >> (lv + 1))) & ((p >> lv) != (f >> lv)))
    pr = np.zeros((128, 128), np.float32)
    for m in range(128):
        if (m % 64) < 32:
            pr[m + 32, m] = -1.0
        else:
            pr[m - 32, m] = 1.0
    put("prot", pr)
    return c


def host_rope():
    t = np.arange(LAT)
    rows = (t // 64).astype(np.float32); cols = (t % 64).astype(np.float32)
    inv = (10000.0 ** (-np.arange(32, dtype=np.float32) / 32)).astype(np.float32)
    out = np.zeros((128, 2, LAT), np.float32)
    for p in range(128):
        pos = rows if p < 64 else cols
        ang = (pos * inv[p % 32]).astype(np.float32)
        out[p, 0] = np.cos(ang); out[p, 1] = np.sin(ang)
    return out


def host_params(inp):
    p = np.zeros((DEPTH, 128, PRM_N), np.float32)
    for l in range(DEPTH):
        def put(name, arr):
            o, n = PRM_LAY[name]
            p[l, :, o:o + n] = arr.reshape(128, n)
        put("bmodT", inp["b_mod"][l].reshape(24, 128).T)
        put("g_pre", inp["g_pre"][l].reshape(8, 128).T)
        put("g_post", inp["g_post"][l].reshape(8, 128).T)
        put("conv_b", inp["conv_b"][l].reshape(4, 4, 128).transpose(2, 1, 0))
        put("conv_b_bias", inp["conv_b_bias"][l].reshape(4, 128).T)
        put("lru_ba", inp["lru_ba"][l].reshape(2, 4, 128).transpose(2, 0, 1))
        put("lru_bx", inp["lru_bx"][l].reshape(2, 4, 128).transpose(2, 0, 1))
        put("lru_lam", inp["lru_lam"][l].reshape(2, 4, 128).transpose(2, 0, 1))
        put("conv_a", inp["conv_a"][l].reshape(4, 12, 128).transpose(2, 1, 0))
        put("onorm_a", inp["onorm_a"][l].reshape(128, 1))
        put("dt_bias", np.broadcast_to(inp["dt_bias"][l].reshape(1, 8), (128, 8)))
        put("a_log", np.broadcast_to(inp["a_log"][l].reshape(1, 8), (128, 8)))
    return p


def rsqrt_act(P, out, in_, bias, reads, writes):
    P.op("act", lambda e: e.activation(out=out, in_=in_, func=AF.Ln, bias=bias, scale=1.0), reads=reads, writes=writes)
    P.op("act", lambda e: e.activation(out=out, in_=out, func=AF.Exp, scale=-0.5), reads=writes, writes=writes)


def phase_setup(g):
    nc, P, PA = g.nc, g.P, g.PA
    g.cst_sb = PA.alloc("cst", [CST_N])
    P.dma("sp", lambda e: e.dma_start(out=g.cst_sb, in_=g.cst[:, :]), writes=["cst"])
    g.cT_sb = PA.alloc("cT", [8, 3])
    P.dma("sp", lambda e: e.dma_start(out=g.cT_sb, in_=g.cT.rearrange("p (a b) -> p a b", a=8)), writes=["cT"])
    g.sc = PA.alloc("sc", [8, 3])
    P.op("act", lambda e: e.activation(out=g.sc, in_=g.cT_sb, func=AF.Silu), reads=["cT"], writes=["sc"])
    o, n = CST_LAY["ident"]; g.ident = g.cst_sb[:, o:o + n]
    o, n = CST_LAY["ones"]; g.ones = g.cst_sb[:, o:o + n]
    g.ident_bf = PA.alloc("ident_bf", [128], BF16)
    g.ones_bf = PA.alloc("ones_bf", [128], BF16)
    P.op("dve", lambda e: e.tensor_copy(out=g.ident_bf, in_=g.ident), reads=["cst"], writes=["ident_bf"])
    P.op("dve", lambda e: e.tensor_copy(out=g.ones_bf, in_=g.ones), reads=["cst"], writes=["ones_bf"])
    g.prm_sb = PA.alloc("prm", [PRM_N])
    g.mod = PA.alloc("mod", [24, 3])
    g.modA = PA.alloc("modA", [8, 3])
    g.modG = PA.alloc("modG", [8, 3])
    g.gpre32 = PA.alloc("gpre32", [8])
    g.gpost32 = PA.alloc("gpost32", [8])
    P.barrier()


def prm(g, name):
    o, n = PRM_LAY[name]
    return g.prm_sb[:, o:o + n]


def phase_mod(g, l):
    nc, P, A = g.nc, g.P, g.A
    P.barrier()
    A.reset()
    P.dma("sp", lambda e: e.dma_start(out=g.prm_sb, in_=g.prm[l, :, :]), writes=["prm"])
    wv = g.w_mod[l].rearrange("(kc p) n -> p kc n", p=128)
    wbuf = [A.alloc("wm%d" % i, [8, 512]) for i in range(2)]
    ps = g.ps[0][:, 0:72]
    for grp in range(6):
        wb = wbuf[grp % 2]
        P.dma("sp", lambda e, wb=wb, grp=grp: e.dma_start(out=wb, in_=wv[:, :, grp * 512:(grp + 1) * 512]),
              writes=["wm%d" % (grp % 2)])
        for ci in range(4):
            ct = grp * 4 + ci
            for kc in range(8):
                P.op("pe", lambda e, wb=wb, ci=ci, kc=kc, ct=ct: e.matmul(
                    ps[:, ct * 3:(ct + 1) * 3], lhsT=wb[:, kc, ci * 128:(ci + 1) * 128], rhs=g.sc[:, kc, :],
                    start=(kc == 0), stop=(kc == 7)),
                    reads=["wm%d" % (grp % 2), "sc"], writes=["modps"])
    ps3 = ps.rearrange("p (a b) -> p a b", a=24)
    bm = prm(g, "bmodT").unsqueeze(2).to_broadcast([128, 24, 3])
    P.op("dve", lambda e: e.tensor_tensor(out=g.mod, in0=ps3, in1=bm, op=ALU.add), reads=["modps", "prm"], writes=["mod"])
    P.op("dve", lambda e: e.tensor_scalar_mul(out=g.gpre32, in0=prm(g, "g_pre"), scalar1=32.0), reads=["prm"], writes=["gpre32"])
    P.op("dve", lambda e: e.tensor_scalar_mul(out=g.gpost32, in0=prm(g, "g_post"), scalar1=32.0), reads=["prm"], writes=["gpost32"])
    P.op("dve", lambda e: e.scalar_tensor_tensor(out=g.modA, in0=g.mod[:, 8:16, :], scalar=1.0,
                                                 in1=g.gpre32.unsqueeze(2).to_broadcast([128, 8, 3]),
                                                 op0=ALU.add, op1=ALU.mult), reads=["mod", "gpre32"], writes=["modA"])
    P.op("dve", lambda e: e.tensor_tensor(out=g.modG, in0=g.mod[:, 16:24, :],
                                          in1=g.gpost32.unsqueeze(2).to_broadcast([128, 8, 3]), op=ALU.mult),
         reads=["mod", "gpost32"], writes=["modG"])


def seg_j(pc):
    return 2 if pc % 9 == 0 else pc // 9


def phase_norm_proj(g, l):
    nc, P, A = g.nc, g.P, g.A
    P.barrier()
    A.reset()
    xsrc = (g.xT_in if l == 0 else g.xs).rearrange("(kc p) t -> p kc t", p=128)
    hT = A.alloc("hT", [8, NT], BF16)
    xp = [A.alloc("xp%d" % i, [8, 256]) for i in range(2)]
    sq = [A.alloc("sq%d" % i, [8, 256], BF16) for i in range(2)]
    xn = [A.alloc("xn%d" % i, [8, 256]) for i in range(2)]
    rstd = [A.alloc("rstd%d" % i, [256]) for i in range(2)]
    NPC = NT // 256

    def load(pc):
        i = pc % 2
        P.dma("sp", lambda e: e.dma_start(out=xp[i], in_=xsrc[:, :, pc * 256:(pc + 1) * 256]), writes=["xp%d" % i])

    load(0)
    for pc in range(NPC):
        i = pc % 2
        j = seg_j(pc)
        if pc + 1 < NPC:
            load(pc + 1)
        P.op("act", lambda e, i=i: e.activation(out=sq[i], in_=xp[i], func=AF.Square), reads=["xp%d" % i], writes=["sq%d" % i])
        ps = g.ps[i][:, 0:256]
        for kc in range(8):
            P.op("pe", lambda e, i=i, kc=kc, ps=ps: e.matmul(ps, lhsT=g.ones_bf, rhs=sq[i][:, kc, :], start=(kc == 0), stop=(kc == 7)),
                 reads=["sq%d" % i, "ones_bf"], writes=["ssps%d" % i])
        rsqrt_act(P, rstd[i], ps, float(D * EPS), ["ssps%d" % i], ["rstd%d" % i])
        P.op("dve", lambda e, i=i: e.tensor_tensor(out=xn[i], in0=xp[i], in1=rstd[i].unsqueeze(1).to_broadcast([128, 8, 256]),
                                                  op=ALU.mult), reads=["xp%d" % i, "rstd%d" % i], writes=["xn%d" % i])
        for kc in range(8):
            P.op("act", lambda e, i=i, kc=kc, j=j, pc=pc: e.activation(
                out=hT[:, kc, pc * 256:(pc + 1) * 256], in_=xn[i][:, kc, :], func=AF.Identity,
                bias=g.mod[:, kc, j:j + 1], scale=g.modA[:, kc, j:j + 1]),
                reads=["xn%d" % i, "mod", "modA"], writes=["hT%d" % pc])

    wsrc = g.w_big[l].rearrange("(kc p) n -> p kc n", p=128)
    wf = [A.alloc("wf%d" % i, [8, 512]) for i in range(2)]
    wb = [A.alloc("wb%d" % i, [8, 512], BF16) for i in range(2)]
    ost = [A.alloc("ost%d" % i, [512], BF16) for i in range(8)]
    ws_f = A.alloc("ws_f", [8, 16])
    ws_b = A.alloc("ws_b", [8, 16], BF16)
    osm = [A.alloc("osm%d" % i, [16]) for i in range(4)]
    NG = 16
    hreads = ["hT%d" % pc for pc in range(NPC)]

    def loadw(gi):
        i = gi % 2
        P.dma("sp", lambda e: e.dma_start(out=wf[i], in_=wsrc[:, :, gi * 512:(gi + 1) * 512]), writes=["wf%d" % i])

    loadw(0)
    P.dma("sp", lambda e: e.dma_start(out=ws_f, in_=g.w_small[l].rearrange("(kc p) n -> p kc n", p=128)), writes=["ws_f"])
    P.op("pool", lambda e: e.tensor_copy(out=ws_b, in_=ws_f), reads=["ws_f"], writes=["ws_b"])
    ev = 0
    for gi in range(NG):
        i = gi % 2
        if gi + 1 < NG:
            loadw(gi + 1)
        P.op("pool", lambda e, i=i: e.tensor_copy(out=wb[i], in_=wf[i]), reads=["wf%d" % i], writes=["wb%d" % i])
        if gi == 8:
            for tt in range(NT // 128):
                bank = ev % 8
                ps = g.ps[bank]
                for kc in range(8):
                    P.op("pe", lambda e, kc=kc, tt=tt, ps=ps, i=i: e.matmul(
                        ps, lhsT=hT[:, kc, tt * 128:(tt + 1) * 128], rhs=wb[i][:, kc, :], start=(kc == 0), stop=(kc == 7)),
                        reads=["wb%d" % i, "hT%d" % (tt // 2)], writes=["pps%d" % bank])
                o = ost[bank]
                eng = "act" if ev % 2 == 0 else "dve"
                if eng == "act":
                    P.op("act", lambda e, o=o, ps=ps: e.copy(out=o, in_=ps), reads=["pps%d" % bank], writes=["ost%d" % bank])
                else:
                    P.op("dve", lambda e, o=o, ps=ps: e.tensor_copy(out=o, in_=ps), reads=["pps%d" % bank], writes=["ost%d" % bank])
                P.dma("sp", lambda e, o=o, tt=tt: e.dma_start(out=g.VC[tt * 128:(tt + 1) * 128, :], in_=o),
                      reads=["ost%d" % bank], writes=["VC"])
                ev += 1
            continue
        for T in range(NT // 512):
            for ci in range(4):
                ct = gi * 4 + ci
                bank = ev % 8
                ps = g.ps[bank]
                for kc in range(8):
                    P.op("pe", lambda e, kc=kc, T=T, ps=ps, i=i, ci=ci: e.matmul(
                        ps, lhsT=wb[i][:, kc, ci * 128:(ci + 1) * 128], rhs=hT[:, kc, T * 512:(T + 1) * 512],
                        start=(kc == 0), stop=(kc == 7)),
                        reads=["wb%d" % i, "hT%d" % (2 * T), "hT%d" % (2 * T + 1)], writes=["pps%d" % bank])
                o = ost[bank]
                if ev % 2 == 0:
                    P.op("act", lambda e, o=o, ps=ps: e.copy(out=o, in_=ps), reads=["pps%d" % bank], writes=["ost%d" % bank])
                else:
                    P.op("dve", lambda e, o=o, ps=ps: e.tensor_copy(out=o, in_=ps), reads=["pps%d" % bank], writes=["ost%d" % bank])
                P.dma("sp", lambda e, o=o, ct=ct, T=T: e.dma_start(out=g.PT[ct, :, T * 512:(T + 1) * 512], in_=o),
                      reads=["ost%d" % bank], writes=["PT"])
                ev += 1
    for tt in range(NT // 128):
        bank = ev % 8
        ps = g.ps[bank][:, 0:16]
        for kc in range(8):
            P.op("pe", lambda e, kc=kc, tt=tt, ps=ps: e.matmul(
                ps, lhsT=hT[:, kc, tt * 128:(tt + 1) * 128], rhs=ws_b[:, kc, :], start=(kc == 0), stop=(kc == 7)),
                reads=["ws_b", "hT%d" % (tt // 2)], writes=["pps%d" % bank])
        o = osm[tt % 4]
        P.op("dve", lambda e, o=o, ps=ps: e.tensor_copy(out=o, in_=ps), reads=["pps%d" % bank], writes=["osm%d" % (tt % 4)])
        P.dma("sp", lambda e, o=o, tt=tt: e.dma_start(out=g.BA[tt * 128:(tt + 1) * 128, :], in_=o),
              reads=["osm%d" % (tt % 4)], writes=["BA"])
        ev += 1


def evac(P, k, out, in_, reads, writes):
    if k % 2 == 0:
        P.op("act", lambda e: e.copy(out=out, in_=in_), reads=reads, writes=writes)
    else:
        P.op("dve", lambda e: e.tensor_copy(out=out, in_=in_), reads=reads, writes=writes)


PADW = 2312
PC0, PL0 = 2, 261


def load_padded(P, buf, src, name):
    P.dma("sp", lambda e: e.dma_start(out=buf[:, PC0:PC0 + CTX], in_=src[:, 0:CTX]), writes=[name])
    P.dma("sp", lambda e: e.dma_start(out=buf[:, PL0:PL0 + LAT], in_=src[:, CTX:S]), writes=[name])


def store_padded(P, dst, buf, name, wname):
    P.dma("sp", lambda e: e.dma_start(out=dst[:, 0:CTX], in_=buf[:, PC0:PC0 + CTX]), reads=[name], writes=[wname])
    P.dma("sp", lambda e: e.dma_start(out=dst[:, CTX:S], in_=buf[:, PL0:PL0 + LAT]), reads=[name], writes=[wname])


def conv4(P, eng, out, buf, w, bias, reads, wname, tmp=None):
    n = PADW - 5
    o = out[:, 2:2 + n]
    if bias is not None:
        P.op(eng, lambda e: e.tensor_scalar(out=o, in0=buf[:, 0:n], scalar1=w[:, 0:1], scalar2=bias, op0=ALU.mult, op1=ALU.add),
             reads=reads, writes=[wname])
    else:
        P.op(eng, lambda e: e.tensor_scalar_mul(out=o, in0=buf[:, 0:n], scalar1=w[:, 0:1]), reads=reads, writes=[wname])
    for j in range(1, 4):
        if eng == "dve":
            P.op(eng, lambda e, j=j: e.scalar_tensor_tensor(out=o, in0=buf[:, j:j + n], scalar=w[:, j:j + 1], in1=o,
                                                             op0=ALU.mult, op1=ALU.add), reads=reads + [wname], writes=[wname])
        else:
            t = tmp[:, 2:2 + n]
            P.op(eng, lambda e, j=j, t=t: e.tensor_scalar_mul(out=t, in0=buf[:, j:j + n], scalar1=w[:, j:j + 1]), reads=reads, writes=[wname + "_t"])
            P.op(eng, lambda e, t=t: e.tensor_tensor(out=o, in0=o, in1=t, op=ALU.add), reads=[wname, wname + "_t"], writes=[wname])


def phase_lru(g, l):
    nc, P, A = g.nc, g.P, g.A
    P.barrier()
    A.reset()
    wst = A.alloc("lruw_f", [16, 128])
    wbd = A.alloc("lruw_b", [16, 128], BF16)
    P.dma("sp", lambda e: e.dma_start(out=wst, in_=g.lruw[l].rearrange("p (a b) -> p a b", a=16)), writes=["lruw_f"])
    P.op("pool", lambda e: e.tensor_copy(out=wbd, in_=wst), reads=["lruw_f"], writes=["lruw_b"])
    sp = A.alloc("sp", [8]); sc8 = A.alloc("sc8", [8]); sc16 = A.alloc("sc16", [8])
    P.op("act", lambda e: e.activation(out=sp, in_=prm(g, "lru_lam"), func=AF.Exp, scale=-1.0), reads=[], writes=["sp"])
    P.op("act", lambda e: e.activation(out=sp, in_=sp, func=AF.Ln, bias=1.0, scale=1.0), reads=["sp"], writes=["sp"])
    P.op("dve", lambda e: e.tensor_scalar_mul(out=sc8, in0=sp, scalar1=-8.0), reads=["sp"], writes=["sc8"])
    P.op("dve", lambda e: e.tensor_scalar_mul(out=sc16, in0=sp, scalar1=-16.0), reads=["sp"], writes=["sc16"])
    nbuf = 2
    xraw = [A.alloc("xraw%d" % i, [PADW], BF16) for i in range(nbuf)]
    graw = [A.alloc("graw%d" % i, [PADW], BF16) for i in range(nbuf)]
    for i in range(nbuf):
        P.op("pool", lambda e, i=i: e.memset(xraw[i], 0.0), writes=["xraw%d" % i])
        P.op("pool", lambda e, i=i: e.memset(graw[i], 0.0), writes=["graw%d" % i])
    xc = A.alloc("xc", [PADW]); xcb = A.alloc("xcb", [PADW], BF16)
    r = A.alloc("r", [PADW]); ig = A.alloc("ig", [PADW]); av = A.alloc("av", [PADW]); sv = A.alloc("sv", [PADW])
    hh = [A.alloc("hh%d" % i, [PADW]) for i in range(2)]
    sg = A.alloc("sg", [PADW]); yb = [A.alloc("yb%d" % i, [PADW], BF16) for i in range(2)]
    for v, nm in ((xc, "xc"), (r, "r"), (ig, "ig"), (av, "av"), (sv, "sv"), (hh[0], "hh0"), (hh[1], "hh1")):
        P.op("pool", lambda e, v=v: e.memset(v, 0.0), writes=[nm])
    cw = prm(g, "conv_b").rearrange("p (a b) -> p a b", a=4)
    cb = prm(g, "conv_b_bias")
    ba = prm(g, "lru_ba").rearrange("p (a b) -> p a b", a=2)
    bx = prm(g, "lru_bx").rearrange("p (a b) -> p a b", a=2)
    sc8v = sc8.rearrange("p (a b) -> p a b", a=2); sc16v = sc16.rearrange("p (a b) -> p a b", a=2)
    items = [(b, ct) for b in range(NB) for ct in range(4)]

    def load(k):
        b, ct = items[k]
        i = k % nbuf
        load_padded(P, xraw[i], g.PT[16 + ct][:, b * S:(b + 1) * S], "xraw%d" % i)
        load_padded(P, graw[i], g.PT[20 + ct][:, b * S:(b + 1) * S], "graw%d" % i)

    load(0)
    N0, N1 = 2, PADW - 3
    chunks = [(c0, min(c0 + 512, N1)) for c0 in range(N0, N1, 512)]
    ev = 0
    for k, (b, ct) in enumerate(items):
        i = k % nbuf
        if k + 1 < len(items):
            load(k + 1)
        conv4(P, "dve", xc, xraw[i], cw[:, ct, :], cb[:, ct:ct + 1], ["xraw%d" % i], "xc")
        P.op("act", lambda e: e.copy(out=xcb[:, N0:N1], in_=xc[:, N0:N1]), reads=["xc"], writes=["xcb"])
        P.op("act", lambda e, i=i: e.activation(out=sg[:, N0:N1], in_=graw[i][:, N0:N1], func=AF.Silu), reads=["graw%d" % i], writes=["sg"])
        for d in range(2):
            for gi, (dst, bias, nm) in enumerate(((r, ba, "r"), (ig, bx, "ig"))):
                for (c0, c1) in chunks:
                    bank = ev % 8; ev += 1
                    ps = g.ps[bank][:, 0:c1 - c0]
                    P.op("pe", lambda e, ps=ps, gi=gi, d=d, ct=ct, c0=c0, c1=c1: e.matmul(
                        ps, lhsT=wbd[:, gi * 8 + d * 4 + ct, :], rhs=xcb[:, c0:c1], start=True, stop=True),
                        reads=["lruw_b", "xcb"], writes=["lps%d" % bank])
                    P.op("act", lambda e, ps=ps, dst=dst, bias=bias, d=d, ct=ct, c0=c0, c1=c1: e.activation(
                        out=dst[:, c0:c1], in_=ps, func=AF.Sigmoid, bias=bias[:, d, ct:ct + 1], scale=1.0),
                        reads=["lps%d" % bank], writes=[nm])
            P.op("act", lambda e, d=d, ct=ct: e.activation(out=av[:, N0:N1], in_=r[:, N0:N1], func=AF.Exp, scale=sc8v[:, d, ct:ct + 1]),
                 reads=["r", "sc8"], writes=["av"])
            P.op("act", lambda e, d=d, ct=ct: e.activation(out=sv[:, N0:N1], in_=r[:, N0:N1], func=AF.Exp, scale=sc16v[:, d, ct:ct + 1]),
                 reads=["r", "sc16"], writes=["sv"])
            P.op("act", lambda e: e.activation(out=sv[:, N0:N1], in_=sv[:, N0:N1], func=AF.Sqrt, bias=1.0, scale=-1.0),
                 reads=["sv"], writes=["sv"])
            P.op("dve", lambda e: e.tensor_tensor(out=ig[:, N0:N1], in0=ig[:, N0:N1], in1=xc[:, N0:N1], op=ALU.mult), reads=["ig", "xc"], writes=["ig"])
            P.op("dve", lambda e: e.tensor_tensor(out=sv[:, N0:N1], in0=sv[:, N0:N1], in1=ig[:, N0:N1], op=ALU.mult), reads=["ig", "sv"], writes=["sv"])
            h = hh[d]
            if d == 0:
                P.op("dve", lambda e, h=h: e.tensor_tensor_scan(out=h[:, PC0:PC0 + CTX], data0=av[:, PC0:PC0 + CTX], data1=sv[:, PC0:PC0 + CTX],
                                                                initial=0.0, op0=ALU.mult, op1=ALU.add), reads=["av", "sv"], writes=["hh0"])
                P.op("dve", lambda e, h=h: e.tensor_tensor_scan(out=h[:, PL0:PL0 + LAT], data0=av[:, PL0:PL0 + LAT], data1=sv[:, PL0:PL0 + LAT],
                                                                initial=h[:, PC0 + CTX - 1:PC0 + CTX], op0=ALU.mult, op1=ALU.add),
                     reads=["av", "sv", "hh0"], writes=["hh0"])
            else:
                def rv(t, a, n):
                    return t[:, a + n - 1:a - 1:-1] if a > 0 else t[:, a + n - 1::-1]
                P.op("dve", lambda e, h=h: e.tensor_tensor_scan(out=rv(h, PC0, CTX), data0=rv(av, PC0, CTX), data1=rv(sv, PC0, CTX),
                                                                initial=0.0, op0=ALU.mult, op1=ALU.add), reads=["av", "sv"], writes=["hh1"])
                P.op("dve", lambda e, h=h: e.tensor_tensor_scan(out=rv(h, PL0, LAT), data0=rv(av, PL0, LAT), data1=rv(sv, PL0, LAT),
                                                                initial=h[:, PC0:PC0 + 1], op0=ALU.mult, op1=ALU.add),
                     reads=["av", "sv", "hh1"], writes=["hh1"])
        P.op("pool", lambda e: e.tensor_tensor(out=hh[0][:, N0:N1], in0=hh[0][:, N0:N1], in1=hh[1][:, N0:N1], op=ALU.add),
             reads=["hh0", "hh1"], writes=["hh0"])
        P.op("pool", lambda e, i=i: e.tensor_tensor(out=yb[i][:, N0:N1], in0=hh[0][:, N0:N1], in1=sg[:, N0:N1], op=ALU.mult),
             reads=["hh0", "sg"], writes=["yb%d" % i])
        store_padded(P, g.Y3[1, ct][:, b * S:(b + 1) * S], yb[i], "yb%d" % i, "Y3")


def cst(g, name):
    o, n = CST_LAY[name]
    return g.cst_sb[:, o:o + n]


def phase_gdn(g, l, last):
    nc, P, A = g.nc, g.P, g.A
    P.barrier()
    A.reset()
    import os
    if os.environ.get("GDN_LIMIT"):
        Prog.limit = int(os.environ["GDN_LIMIT"]); Prog.nrec = 0
    NTL = NT // 128
    ba = A.alloc("ba", [NTL, 16])
    bav = g.BA.rearrange("(t p) c -> p t c", p=128)
    for t0 in range(0, NTL, 6):
        P.dma("sp", lambda e, t0=t0: e.dma_start(out=ba[:, t0:t0 + 6, :], in_=bav[:, t0:t0 + 6, :]), writes=["ba"])

    def sc4(name):
        return A.alloc(name, [2, NTL, 4])
    beta = sc4("beta"); nbeta = sc4("nbeta"); gl = sc4("gl"); gc = sc4("gc"); egc = sc4("egc"); egl = sc4("egl"); egt = sc4("egt")
    nea = A.alloc("nea", [8])

    def asdth(v):
        return v.rearrange("p t (d h) -> p d t h", d=2)
    P.op("act", lambda e: e.activation(out=beta, in_=asdth(ba[:, :, 0:8]), func=AF.Sigmoid), reads=["ba"], writes=["beta"])
    P.op("dve", lambda e: e.tensor_scalar_mul(out=nbeta, in0=beta, scalar1=-1.0), reads=["beta"], writes=["nbeta"])
    dtb = prm(g, "dt_bias").rearrange("p (d h) -> p d h", d=2).unsqueeze(2).to_broadcast([128, 2, NTL, 4])
    P.op("dve", lambda e: e.tensor_tensor(out=gl, in0=asdth(ba[:, :, 8:16]), in1=dtb, op=ALU.add), reads=["ba"], writes=["gl"])
    P.op("act", lambda e: e.activation(out=gl, in_=gl, func=AF.Exp), reads=["gl"], writes=["gl"])
    P.op("act", lambda e: e.activation(out=gl, in_=gl, func=AF.Ln, bias=1.0, scale=1.0), reads=["gl"], writes=["gl"])
    P.op("act", lambda e: e.activation(out=nea, in_=prm(g, "a_log"), func=AF.Exp), reads=[], writes=["nea"])
    P.op("dve", lambda e: e.tensor_scalar_mul(out=nea, in0=nea, scalar1=-1.0), reads=["nea"], writes=["nea"])
    neab = nea.rearrange("p (d h) -> p d h", d=2).unsqueeze(2).to_broadcast([128, 2, NTL, 4])
    P.op("dve", lambda e: e.tensor_tensor(out=gl, in0=gl, in1=neab, op=ALU.mult), reads=["gl", "nea"], writes=["gl"])
    for d in range(2):
        ps = g.ps[d][:, 0:NTL * 4]
        P.op("pe", lambda e, d=d, ps=ps: e.matmul(ps, lhsT=cst(g, "tri%d" % d), rhs=gl[:, d].rearrange("p t h -> p (t h)"), start=True, stop=True),
             reads=["gl"], writes=["c0ps%d" % d])
        P.op("dve", lambda e, d=d, ps=ps: e.tensor_copy(out=gc[:, d].rearrange("p t h -> p (t h)"), in_=ps), reads=["c0ps%d" % d], writes=["gc"])
    ps = g.ps[2][:, 0:2 * NTL * 4]
    P.op("pe", lambda e: e.matmul(ps, lhsT=g.ones, rhs=gl.rearrange("p d t h -> p (d t h)"), start=True, stop=True), reads=["gl"], writes=["c0ps2"])
    flat = lambda v: v.rearrange("p d t h -> p (d t h)")
    P.op("act", lambda e: e.activation(out=flat(egt), in_=ps, func=AF.Exp), reads=["c0ps2"], writes=["egt"])
    P.op("dve", lambda e: e.tensor_tensor(out=flat(egl), in0=ps, in1=flat(gc), op=ALU.subtract), reads=["c0ps2", "gc"], writes=["egl"])
    P.op("act", lambda e: e.activation(out=egl, in_=egl, func=AF.Exp), reads=["egl"], writes=["egl"])
    P.op("act", lambda e: e.activation(out=egc, in_=gc, func=AF.Exp), reads=["gc"], writes=["egc"])

    P.barrier()
    if "gstop0" in g.dbg:
        return
    rope = A.alloc("rope", [2, LAT])
    P.dma("sp", lambda e: e.dma_start(out=rope, in_=g.rope.rearrange("p (a b) -> p a b", a=2)), writes=["rope"])
    raw = A.alloc("craw", [PADW], BF16)
    P.op("pool", lambda e: e.memset(raw, 0.0), writes=["craw"])
    acc = A.alloc("cacc", [PADW])
    P.op("pool", lambda e: e.memset(acc, 0.0), writes=["cacc"])
    ctmp = A.alloc("ctmp", [PADW])
    sqb = A.alloc("csqb", [PADW], BF16)
    rinv = [A.alloc("crinv%d" % i, [512]) for i in range(2)]
    t1 = [A.alloc("ct1_%d" % i, [512]) for i in range(2)]
    QT = [A.alloc("cQT%d" % i, [S], BF16) for i in range(2)]
    KT = A.alloc("cKT", [S], BF16)
    VT = A.alloc("cVT", [S], BF16)
    Ktok = A.alloc("cKtok", [18, 128], BF16)
    Vtok = A.alloc("cVtok", [18, 128], BF16)
    Od = [A.alloc("cOd%d" % i, [18, 128]) for i in range(2)]
    graw = A.alloc("cgraw", [S], BF16)
    sg = A.alloc("csg", [S])
    onb = A.alloc("conb", [18, 128], BF16)
    ssq = A.alloc("cssq", [18])
    yaT = A.alloc("cyaT", [S], BF16)
    Sst = [A.alloc("cS%d" % i, [128]) for i in range(2)]
    Sbf = [A.alloc("cSbf%d" % i, [128], BF16) for i in range(2)]
    Ubf = [A.alloc("cU%d" % i, [128], BF16) for i in range(2)]
    O2s = [A.alloc("cO2_%d" % i, [128]) for i in range(2)]
    WkT = [A.alloc("cWkT%d" % i, [18, 128], BF16) for i in range(2)]
    Wvb = [A.alloc("cWvb%d" % i, [18, 128]) for i in range(2)]
    QKm = [A.alloc("cQKm%d" % i, [18, 128], BF16) for i in range(2)]
    Ktl = [A.alloc("cKtl%d" % i, [18, 128], BF16) for i in range(2)]
    def two(name, shape, dt=F32):
        return [A.alloc("%s%d" % (name, i), shape, dt) for i in range(2)]
    Gb = two("cGb", [128]); Em = two("cE", [128]); Dms = two("cDms", [128]); Dmi = two("cDmi", [128])
    ATp = two("cATp", [128], BF16); LT = two("cLT", [7, 128], BF16); Tm = two("cT", [128], BF16); TT = two("cTT", [128], BF16)
    Yb = two("cYb", [128], BF16); ZTs = two("cZTs", [128], BF16); Xk = two("cXk", [128], BF16)
    cw = prm(g, "conv_a").rearrange("p (a b) -> p a b", a=12)
    N0, N1 = 2, PADW - 3
    chunks = [(c0, min(c0 + 512, N1)) for c0 in range(N0, N1, 512)]
    state = {"ev": 0, "ch": 0}

    def c1(b, h, qi):
        for which in range(3):
            ct = which * 4 + h
            load_padded(P, raw, g.PT[ct][:, b * S:(b + 1) * S], "craw")
            conv4(P, "dve" if which < 2 else "pool", acc, raw, cw[:, ct, :], None, ["craw"], "cacc", tmp=ctmp)
            P.op("act", lambda e: e.activation(out=acc[:, N0:N1], in_=acc[:, N0:N1], func=AF.Silu), reads=["cacc"], writes=["cacc"])
            if which == 2:
                P.op("act", lambda e: e.copy(out=VT[:, 0:CTX], in_=acc[:, PC0:PC0 + CTX]), reads=["cacc"], writes=["cVT"])
                P.op("act", lambda e: e.copy(out=VT[:, CTX:S], in_=acc[:, PL0:PL0 + LAT]), reads=["cacc"], writes=["cVT"])
                continue
            P.op("pool", lambda e: e.tensor_tensor(out=sqb[:, N0:N1], in0=acc[:, N0:N1], in1=acc[:, N0:N1], op=ALU.mult), reads=["cacc"], writes=["csqb"])
            for ci, (c0, c1_) in enumerate(chunks):
                bank = 4 + state["ev"] % 2; state["ev"] += 1
                ps = g.ps[bank][:, 0:c1_ - c0]
                ri = rinv[ci % 2][:, 0:c1_ - c0]
                P.op("pe", lambda e, ps=ps, c0=c0, c1_=c1_: e.matmul(ps, lhsT=g.ones_bf, rhs=sqb[:, c0:c1_], start=True, stop=True),
                     reads=["csqb"], writes=["cps%d" % bank])
                rsqrt_act(P, ri, ps, EPS, ["cps%d" % bank], ["crinv%d" % (ci % 2)])
                if which == 0:
                    P.op("dve", lambda e, ri=ri, c0=c0, c1_=c1_: e.scalar_tensor_tensor(out=acc[:, c0:c1_], in0=acc[:, c0:c1_], scalar=float(128.0 ** -0.5),
                                                                                      in1=ri, op0=ALU.mult, op1=ALU.mult),
                         reads=["cacc", "crinv%d" % (ci % 2)], writes=["cacc"])
                else:
                    P.op("dve", lambda e, ri=ri, c0=c0, c1_=c1_: e.tensor_tensor(out=acc[:, c0:c1_], in0=acc[:, c0:c1_], in1=ri, op=ALU.mult),
                         reads=["cacc", "crinv%d" % (ci % 2)], writes=["cacc"])
            for ci in range(4):
                bank = 4 + state["ev"] % 2; state["ev"] += 1
                ps = g.ps[bank]
                a0 = PL0 + ci * 512
                P.op("pe", lambda e, ps=ps, a0=a0: e.matmul(ps, lhsT=cst(g, "prot"), rhs=acc[:, a0:a0 + 512], start=True, stop=True),
                     reads=["cacc"], writes=["cps%d" % bank])
                tt = t1[ci % 2]
                P.op("dve", lambda e, ps=ps, tt=tt, ci=ci: e.tensor_tensor(out=tt, in0=ps, in1=rope[:, 1, ci * 512:(ci + 1) * 512], op=ALU.mult),
                     reads=["cps%d" % bank, "rope"], writes=["ct1_%d" % (ci % 2)])
                P.op("pool", lambda e, a0=a0, ci=ci: e.tensor_tensor(out=acc[:, a0:a0 + 512], in0=acc[:, a0:a0 + 512], in1=rope[:, 0, ci * 512:(ci + 1) * 512], op=ALU.mult),
                     reads=["cacc", "rope", "cps%d" % bank], writes=["cacc"])
                P.op("pool", lambda e, a0=a0, tt=tt: e.tensor_tensor(out=acc[:, a0:a0 + 512], in0=acc[:, a0:a0 + 512], in1=tt, op=ALU.add),
                     reads=["cacc", "ct1_%d" % (ci % 2)], writes=["cacc"])
            dst = QT[qi] if which == 0 else KT
            dn = ("cQT%d" % qi) if which == 0 else "cKT"
            P.op("act", lambda e, dst=dst: e.copy(out=dst[:, 0:CTX], in_=acc[:, PC0:PC0 + CTX]), reads=["cacc"], writes=[dn])
            P.op("act", lambda e, dst=dst: e.copy(out=dst[:, CTX:S], in_=acc[:, PL0:PL0 + LAT]), reads=["cacc"], writes=[dn])
        for src, dstt, sn, dn in ((KT, Ktok, "cKT", "cKtok"), (VT, Vtok, "cVT", "cVtok")):
            for g0 in range(0, 18, 8):
                n = min(8, 18 - g0)
                bank = 4 + state["ev"] % 2; state["ev"] += 1
                pb = g.ps[bank].bitcast(BF16)
                for t in range(n):
                    P.op("pe", lambda e, pb=pb, t=t, g0=g0, src=src: e.transpose(pb[:, t * 128:(t + 1) * 128], src[:, (g0 + t) * 128:(g0 + t + 1) * 128], g.ident_bf),
                         reads=[sn], writes=["cps%d" % bank])
                evac(P, state["ev"], dstt[:, g0:g0 + n, :].rearrange("p a b -> p (a b)"), pb[:, 0:n * 128], ["cps%d" % bank], [dn])

    def chain(b, h, d, t, ip, qi):
        c = state["ch"] % 2; state["ch"] += 1
        T = b * 18 + t
        col = lambda v: v[:, d, T, h:h + 1]
        sl = slice(t * 128, (t + 1) * 128)
        bankA = 2 * c
        psA = g.ps[bankA]
        psB = g.ps[bankA + 1]
        P.op("pool", lambda e: e.tensor_copy(out=Gb[c], in_=col(gl).to_broadcast([128, 128])), reads=["gl"], writes=["cGb%d" % c])
        P.op("pe", lambda e: e.matmul(psA[:, 0:128], lhsT=Gb[c], rhs=cst(g, "tri%d" % d), start=True, stop=True), reads=["cGb%d" % c], writes=["cpA%d_0" % c])
        P.op("pe", lambda e: e.matmul(psA[:, 128:256], lhsT=KT[:, sl], rhs=KT[:, sl], start=True, stop=True), reads=["cKT"], writes=["cpA%d_1" % c])
        P.op("pe", lambda e: e.matmul(psA[:, 256:384], lhsT=KT[:, sl], rhs=QT[qi][:, sl], start=True, stop=True), reads=["cKT", "cQT%d" % qi], writes=["cpA%d_2" % c])
        P.op("dve", lambda e: e.scalar_tensor_tensor(out=Em[c], in0=psA[:, 0:128], scalar=col(gc), in1=cst(g, "mit%d" % d), op0=ALU.subtract, op1=ALU.mult),
             reads=["cpA%d_0" % c, "gc"], writes=["cE%d" % c])
        P.op("act", lambda e: e.activation(out=Em[c], in_=Em[c], func=AF.Exp), reads=["cE%d" % c], writes=["cE%d" % c])
        P.op("pool", lambda e: e.tensor_tensor(out=Dms[c], in0=Em[c], in1=cst(g, "mst%d" % d), op=ALU.mult), reads=["cE%d" % c], writes=["cDms%d" % c])
        P.op("pool", lambda e: e.tensor_tensor(out=Dmi[c], in0=Em[c], in1=cst(g, "mit%d" % d), op=ALU.mult), reads=["cE%d" % c], writes=["cDmi%d" % c])
        P.op("dve", lambda e: e.scalar_tensor_tensor(out=ATp[c], in0=psA[:, 128:256], scalar=col(beta), in1=Dms[c], op0=ALU.mult, op1=ALU.mult),
             reads=["cpA%d_1" % c, "beta", "cDms%d" % c], writes=["cATp%d" % c])
        P.op("dve", lambda e: e.tensor_tensor(out=QKm[ip][:, t, :], in0=psA[:, 256:384], in1=Dmi[c], op=ALU.mult),
             reads=["cpA%d_2" % c, "cDmi%d" % c], writes=["cQKm%d_%d" % (ip, t)])
        for lv in range(7):
            P.op("pool", lambda e, lv=lv: e.tensor_tensor(out=LT[c][:, lv, :], in0=ATp[c], in1=cst(g, "lm%d" % lv), op=ALU.mult),
                 reads=["cATp%d" % c], writes=["cLT%d_%d" % (c, lv)])
        P.op("pool", lambda e: e.tensor_tensor(out=TT[c], in0=g.ident_bf, in1=LT[c][:, 0, :], op=ALU.subtract), reads=["cLT%d_0" % c], writes=["cTT%d" % c])
        pbB = psB.bitcast(BF16)
        P.op("pe", lambda e: e.transpose(pbB[:, 0:128], LT[c][:, 0, :], g.ident_bf), reads=["cLT%d_0" % c], writes=["cpB%d_t" % c])
        P.op("dve", lambda e: e.tensor_tensor(out=Tm[c], in0=g.ident_bf, in1=pbB[:, 0:128], op=ALU.subtract), reads=["cpB%d_t" % c], writes=["cT%d" % c])
        for lv in range(1, 7):
            lastlv = (lv == 6)
            yps = psB[:, 128:256]; zps = psB[:, 256:384]; ztps = psA[:, 384:512]
            P.op("pe", lambda e, lv=lv: e.matmul(yps, lhsT=LT[c][:, lv, :], rhs=Tm[c], start=True, stop=True),
                 reads=["cLT%d_%d" % (c, lv), "cT%d" % c], writes=["cpB%d_y" % c])
            P.op("act", lambda e: e.copy(out=Yb[c], in_=yps), reads=["cpB%d_y" % c], writes=["cYb%d" % c])
            if not lastlv:
                P.op("pe", lambda e: e.matmul(zps, lhsT=TT[c], rhs=Yb[c], start=True, stop=True), reads=["cTT%d" % c, "cYb%d" % c], writes=["cpB%d_z" % c])
            P.op("pe", lambda e: e.matmul(ztps, lhsT=Yb[c], rhs=TT[c], start=True, stop=True), reads=["cTT%d" % c, "cYb%d" % c], writes=["cpA%d_zt" % c])
            if not lastlv:
                P.op("dve", lambda e: e.tensor_tensor(out=Tm[c], in0=Tm[c], in1=zps, op=ALU.subtract), reads=["cT%d" % c, "cpB%d_z" % c], writes=["cT%d" % c])
            P.op("act", lambda e: e.copy(out=ZTs[c], in_=ztps), reads=["cpA%d_zt" % c], writes=["cZTs%d" % c])
            P.op("pool", lambda e: e.tensor_tensor(out=TT[c], in0=TT[c], in1=ZTs[c], op=ALU.subtract), reads=["cTT%d" % c, "cZTs%d" % c], writes=["cTT%d" % c])
        P.op("pool", lambda e: e.tensor_scalar_mul(out=Xk[c], in0=Ktok[:, t, :], scalar1=col(egc)), reads=["cKtok", "egc"], writes=["cXk%d" % c])
        P.op("pool", lambda e: e.tensor_scalar_mul(out=Ktl[ip][:, t, :], in0=Ktok[:, t, :], scalar1=col(egl)), reads=["cKtok", "egl"], writes=["cKtl%d_%d" % (ip, t)])
        wv = psB[:, 128:256]; wk = psB[:, 256:384]
        P.op("pe", lambda e: e.matmul(wv, lhsT=TT[c], rhs=Vtok[:, t, :], start=True, stop=True), reads=["cTT%d" % c, "cVtok"], writes=["cpB%d_y" % c])
        P.op("pe", lambda e: e.matmul(wk, lhsT=Xk[c], rhs=TT[c], start=True, stop=True), reads=["cTT%d" % c, "cXk%d" % c], writes=["cpB%d_z" % c])
        P.op("act", lambda e: e.activation(out=Wvb[ip][:, t, :], in_=wv, func=AF.Copy, scale=col(beta)), reads=["cpB%d_y" % c, "beta"], writes=["cWvb%d_%d" % (ip, t)])
        P.op("act", lambda e: e.copy(out=WkT[ip][:, t, :], in_=wk), reads=["cpB%d_z" % c], writes=["cWkT%d_%d" % (ip, t)])

    def step(b, h, d, t, ip, qi, first):
        T = b * 18 + t
        col = lambda v: v[:, d, T, h:h + 1]
        sl = slice(t * 128, (t + 1) * 128)
        si = ip
        ps6 = g.ps[6]; ps7 = g.ps[7]
        if first:
            P.op("pool", lambda e: e.memset(Sst[si], 0.0), writes=["cS%d" % si])
            P.op("pool", lambda e: e.memset(Sbf[si], 0.0), writes=["cSbf%d" % si])
        P.op("pe", lambda e: e.matmul(ps6[:, 0:128], lhsT=WkT[ip][:, t, :], rhs=Sbf[si], start=True, stop=True),
             reads=["cWkT%d_%d" % (ip, t), "cSbf%d" % si], writes=["cp6_0"])
        P.op("dve", lambda e: e.scalar_tensor_tensor(out=Ubf[si], in0=ps6[:, 0:128], scalar=col(nbeta), in1=Wvb[ip][:, t, :], op0=ALU.mult, op1=ALU.add),
             reads=["cp6_0", "nbeta", "cWvb%d_%d" % (ip, t)], writes=["cU%d" % si])
        P.op("pe", lambda e: e.matmul(ps6[:, 128:256], lhsT=QT[qi][:, sl], rhs=Sbf[si], start=True, stop=True), reads=["cQT%d" % qi, "cSbf%d" % si], writes=["cp6_1"])
        P.op("pe", lambda e: e.matmul(ps6[:, 256:384], lhsT=QKm[ip][:, t, :], rhs=Ubf[si], start=True, stop=True), reads=["cQKm%d_%d" % (ip, t), "cU%d" % si], writes=["cp6_2"])
        P.op("pe", lambda e: e.matmul(ps7[:, 0:128], lhsT=Ktl[ip][:, t, :], rhs=Ubf[si], start=True, stop=True), reads=["cKtl%d_%d" % (ip, t), "cU%d" % si], writes=["cp7_0"])
        P.op("act", lambda e: e.copy(out=O2s[si], in_=ps6[:, 256:384]), reads=["cp6_2"], writes=["cO2_%d" % si])
        P.op("dve", lambda e: e.scalar_tensor_tensor(out=Od[d][:, t, :], in0=ps6[:, 128:256], scalar=col(egc), in1=O2s[si], op0=ALU.mult, op1=ALU.add),
             reads=["cp6_1", "egc", "cO2_%d" % si], writes=["cOd%d_%d" % (d, t)])
        P.op("dve", lambda e: e.scalar_tensor_tensor(out=Sst[si], in0=Sst[si], scalar=col(egt), in1=ps7[:, 0:128], op0=ALU.mult, op1=ALU.add),
             reads=["cS%d" % si, "egt", "cp7_0"], writes=["cS%d" % si])
        P.op("act", lambda e: e.copy(out=Sbf[si], in_=Sst[si]), reads=["cS%d" % si], writes=["cSbf%d" % si])

    def finalize(b, h):
        odr = ["cOd%d_%d" % (d, t) for d in range(2) for t in range(18)]
        P.dma("sp", lambda e: e.dma_start(out=graw, in_=g.PT[12 + h][:, b * S:(b + 1) * S]), writes=["cgraw"])
        P.op("pool", lambda e: e.tensor_tensor(out=Od[0], in0=Od[0], in1=Od[1], op=ALU.add), reads=odr, writes=["cOsum"] + odr[:18])
        P.op("pool", lambda e: e.tensor_tensor(out=Od[1], in0=Od[0], in1=Od[0], op=ALU.mult), reads=["cOsum"], writes=["cOsq"] + odr[18:])
        P.op("dve", lambda e: e.reduce_sum(out=ssq, in_=Od[1], axis=AX.X), reads=["cOsq"], writes=["cssq"] + odr[18:])
        P.op("act", lambda e: e.activation(out=ssq, in_=ssq, func=AF.Ln, bias=EPS, scale=1.0 / 128), reads=["cssq"], writes=["cssq"])
        P.op("act", lambda e: e.activation(out=ssq, in_=ssq, func=AF.Exp, scale=-0.5), reads=["cssq"], writes=["cssq"])
        P.op("pool", lambda e: e.tensor_tensor(out=onb, in0=Od[0], in1=ssq.unsqueeze(2).to_broadcast([128, 18, 128]), op=ALU.mult),
             reads=["cOsum", "cssq"], writes=["conb"] + odr[:18])
        P.op("act", lambda e: e.activation(out=sg, in_=graw, func=AF.Silu), reads=["cgraw"], writes=["csg"])
        for g0 in range(0, 18, 8):
            n = min(8, 18 - g0)
            bank = state["ev"] % 4 + 4; state["ev"] += 1
            bank = 4 + (state["ev"] % 2)
            pb = g.ps[bank].bitcast(BF16)
            for t in range(n):
                P.op("pe", lambda e, pb=pb, t=t, g0=g0: e.transpose(pb[:, t * 128:(t + 1) * 128], onb[:, g0 + t, :], g.ident_bf),
                     reads=["conb"], writes=["cps%d" % bank])
            P.op("dve", lambda e, pb=pb, g0=g0, n=n: e.scalar_tensor_tensor(out=yaT[:, g0 * 128:(g0 + n) * 128], in0=pb[:, 0:n * 128], scalar=prm(g, "onorm_a"),
                                                                           in1=sg[:, g0 * 128:(g0 + n) * 128], op0=ALU.mult, op1=ALU.mult),
                 reads=["cps%d" % bank, "csg"], writes=["cyaT"])
        P.dma("sp", lambda e: e.dma_start(out=g.Y3[0, h][:, b * S:(b + 1) * S], in_=yaT), reads=["cyaT"], writes=["Y3"])

    def order(d):
        return list(range(18)) if d == 0 else [1, 0] + list(range(17, 1, -1))
    items = [(b, h, d) for b in range(NB) for h in range(4) for d in range(2)]
    for k in range(len(items) + 1):
        nxt = items[k] if k < len(items) else None
        cur = items[k - 1] if k >= 1 else None
        if nxt is not None and nxt[2] == 0:
            c1(nxt[0], nxt[1], (k // 2) % 2)
            if "gstop1" in g.dbg:
                return
        if "gstop2" in g.dbg and k == 1:
            return
        if "gstop3" in g.dbg and k == 2:
            return
        on = order(nxt[2]) if nxt else [None] * 18
        oc = order(cur[2]) if cur else [None] * 18
        for idx in range(18):
            if nxt is not None:
                chain(nxt[0], nxt[1], nxt[2], on[idx], k % 2, (k // 2) % 2)
            if cur is not None:
                step(cur[0], cur[1], cur[2], oc[idx], (k - 1) % 2, ((k - 1) // 2) % 2, idx == 0)
        if cur is not None and cur[2] == 1:
            finalize(cur[0], cur[1])


NEG = -30000.0


def natten_pat(i):
    return 0 if i == 0 else 1 if i == 1 else 3 if i == 14 else 4 if i == 15 else 2


def phase_natten(g, l, last):
    nc, P, A = g.nc, g.P, g.A
    P.barrier()
    A.reset()
    SCALE = 128.0 ** -0.5
    bias = A.alloc("nbias", [5, 640])
    qT = [A.alloc("nq%d" % i, [S], BF16) for i in range(2)]
    kT = [A.alloc("nk%d" % i, [S], BF16) for i in range(2)]
    vt = [A.alloc("nv%d" % i, [18, 128], BF16) for i in range(2)]
    gr = [A.alloc("ng%d" % i, [S], BF16) for i in range(2)]
    sgt = A.alloc("nsg", [S])
    yc = [A.alloc("nyc%d" % i, [S], BF16) for i in range(2)]
    Sb = [A.alloc("nSb%d" % i, [896]) for i in range(2)]
    Pe = [A.alloc("nPe%d" % i, [896]) for i in range(2)]
    Pn = [A.alloc("nPn%d" % i, [896], BF16) for i in range(2)]
    PTs = [A.alloc("nPT%d" % i, [7, 128], BF16) for i in range(2)]
    st = [A.alloc("nst%d" % i, [4]) for i in range(2)]
    items = [(h, b) for h in range(4) for b in range(NB)]

    def load(k):
        h, b = items[k]
        i = k % 2
        P.dma("sp", lambda e: e.dma_start(out=qT[i], in_=g.PT[24 + h][:, b * S:(b + 1) * S]), writes=["nq%d" % i])
        P.dma("sp", lambda e: e.dma_start(out=kT[i], in_=g.PT[28 + h][:, b * S:(b + 1) * S]), writes=["nk%d" % i])
        P.dma("sp", lambda e: e.dma_start(out=gr[i], in_=g.PT[36 + h][:, b * S:(b + 1) * S]), writes=["ng%d" % i])
        P.dma("sp", lambda e: e.dma_start(out=vt[i], in_=g.VC[b * S:(b + 1) * S, h * 128:(h + 1) * 128].rearrange("(t p) d -> p t d", p=128)),
              writes=["nv%d" % i])

    load(0)
    cnt = 0
    for k, (h, b) in enumerate(items):
        i = k % 2
        if b == 0:
            P.dma("sp", lambda e, h=h: e.dma_start(out=bias, in_=g.rpbias[l, h].rearrange("p (a b) -> p a b", a=5)), writes=["nbias"])
        if k + 1 < len(items):
            load(k + 1)
        P.op("act", lambda e, i=i: e.activation(out=sgt, in_=gr[i], func=AF.Silu), reads=["ng%d" % i], writes=["nsg"])
        tiles = list(range(16)) + ([] if last else [16, 17])
        for ti in tiles:
            j = cnt % 2; cnt += 1
            ctxq = ti >= 16
            if not ctxq:
                q0 = CTX + ti * 128
                sr = min(max(2 * ti - 4, 0), 22)
                k0 = CTX + sr * 64
                nk = 896
                vtiles = [2 + sr // 2 + m for m in range(5)] + [0, 1]
            else:
                q0 = (ti - 16) * 128
                nk = 256
                vtiles = [0, 1]
            bk = 2 * j
            Sps = g.psall[:, bk * 512: bk * 512 + nk]
            rd = ["nq%d" % i, "nk%d" % i]
            if not ctxq:
                P.op("pe", lambda e, i=i, q0=q0, k0=k0, bk=bk: e.matmul(g.psall[:, bk * 512:bk * 512 + 512], lhsT=qT[i][:, q0:q0 + 128],
                                                                       rhs=kT[i][:, k0:k0 + 512], start=True, stop=True), reads=rd, writes=["nS%d" % j])
                P.op("pe", lambda e, i=i, q0=q0, k0=k0, bk=bk: e.matmul(g.psall[:, bk * 512 + 512:bk * 512 + 640], lhsT=qT[i][:, q0:q0 + 128],
                                                                       rhs=kT[i][:, k0 + 512:k0 + 640], start=True, stop=True), reads=rd, writes=["nS%d" % j])
                P.op("pe", lambda e, i=i, q0=q0, bk=bk: e.matmul(g.psall[:, bk * 512 + 640:bk * 512 + 896], lhsT=qT[i][:, q0:q0 + 128],
                                                                rhs=kT[i][:, 0:CTX], start=True, stop=True), reads=rd, writes=["nS%d" % j])
                pat = natten_pat(ti)
                P.op("dve", lambda e, j=j, pat=pat, bk=bk: e.scalar_tensor_tensor(
                    out=Sb[j][:, 0:640], in0=g.psall[:, bk * 512:bk * 512 + 640], scalar=SCALE, in1=bias[:, pat, :],
                    op0=ALU.mult, op1=ALU.add), reads=["nS%d" % j, "nbias"], writes=["nSb%d" % j])
                P.op("act", lambda e, j=j, bk=bk: e.activation(out=Sb[j][:, 640:896], in_=g.psall[:, bk * 512 + 640:bk * 512 + 896],
                                                               func=AF.Copy, scale=SCALE), reads=["nS%d" % j], writes=["nSb%d" % j])
            else:
                P.op("pe", lambda e, i=i, q0=q0, bk=bk: e.matmul(g.psall[:, bk * 512:bk * 512 + 256], lhsT=qT[i][:, q0:q0 + 128],
                                                                rhs=kT[i][:, 0:CTX], start=True, stop=True), reads=rd, writes=["nS%d" % j])
                P.op("act", lambda e, j=j, bk=bk: e.activation(out=Sb[j][:, 0:256], in_=g.psall[:, bk * 512:bk * 512 + 256],
                                                               func=AF.Copy, scale=SCALE), reads=["nS%d" % j], writes=["nSb%d" % j])
            sj = st[j]
            P.op("dve", lambda e, j=j, nk=nk, sj=sj: e.reduce_max(out=sj[:, 0:1], in_=Sb[j][:, 0:nk], axis=AX.X), reads=["nSb%d" % j], writes=["nst%d" % j])
            P.op("dve", lambda e, sj=sj: e.tensor_scalar_mul(out=sj[:, 1:2], in0=sj[:, 0:1], scalar1=-1.0), reads=["nst%d" % j], writes=["nst%d" % j])
            P.op("pool", lambda e, sj=sj: e.memset(sj[:, 2:3], 0.0), writes=["nst%d" % j])
            P.op("act", lambda e, j=j, nk=nk, sj=sj: e.activation(out=Pe[j][:, 0:nk], in_=Sb[j][:, 0:nk], func=AF.Exp, bias=sj[:, 1:2], scale=1.0,
                                                                 accum_out=sj[:, 2:3]), reads=["nSb%d" % j, "nst%d" % j], writes=["nPe%d" % j, "nst%d" % j])
            P.op("dve", lambda e, sj=sj: e.reciprocal(out=sj[:, 3:4], in_=sj[:, 2:3]), reads=["nst%d" % j], writes=["nst%d" % j])
            P.op("pool", lambda e, j=j, nk=nk, sj=sj: e.tensor_scalar_mul(out=Pn[j][:, 0:nk], in0=Pe[j][:, 0:nk], scalar1=sj[:, 3:4]),
                 reads=["nPe%d" % j, "nst%d" % j], writes=["nPn%d" % j])
            nkt = nk // 128
            ptp = g.ps[4 + j].bitcast(BF16)
            for kt in range(nkt):
                P.op("pe", lambda e, j=j, kt=kt, ptp=ptp: e.transpose(ptp[:, kt * 128:(kt + 1) * 128], Pn[j][:, kt * 128:(kt + 1) * 128], g.ident_bf),
                     reads=["nPn%d" % j], writes=["nPTp%d" % j])
            evac(P, cnt, PTs[j].rearrange("p a b -> p (a b)")[:, 0:nk], ptp[:, 0:nk], ["nPTp%d" % j], ["nPT%d" % j])
            ops = g.ps[6 + j][:, 0:128]
            for kt in range(nkt):
                P.op("pe", lambda e, i=i, j=j, kt=kt, vti=vtiles[kt], ops=ops, nkt=nkt: e.matmul(
                    ops, lhsT=vt[i][:, vti, :], rhs=PTs[j][:, kt, :], start=(kt == 0), stop=(kt == nkt - 1)),
                    reads=["nv%d" % i, "nPT%d" % j], writes=["no%d" % j])
            P.op("dve", lambda e, i=i, q0=q0, ops=ops: e.tensor_tensor(out=yc[i][:, q0:q0 + 128], in0=ops, in1=sgt[:, q0:q0 + 128], op=ALU.mult),
                 reads=["no%d" % j, "nsg"], writes=["nyc%d" % i])
        if last:
            P.dma("sp", lambda e, i=i, h=h, b=b: e.dma_start(out=g.Y3[2, h][:, b * S + CTX:(b + 1) * S], in_=yc[i][:, CTX:S]), reads=["nyc%d" % i], writes=["Y3"])
        else:
            P.dma("sp", lambda e, i=i, h=h, b=b: e.dma_start(out=g.Y3[2, h][:, b * S:(b + 1) * S], in_=yc[i]), reads=["nyc%d" % i], writes=["Y3"])


def phase_merge(g, l, last):
    nc, P, A = g.nc, g.P, g.A
    P.barrier()
    A.reset()
    wbr = A.alloc("wbr", [12, D], BF16)
    wo = A.alloc("wo", [8, D], BF16)
    stg = [A.alloc("mstg%d" % i, [4, D]) for i in range(2)]
    srcs = [g.w_br[l, jb].rearrange("(kc p) n -> p kc n", p=128) for jb in range(3)] + \
           [g.w_out[l].rearrange("(kc p) n -> p kc n", p=128)[:, 0:4, :], g.w_out[l].rearrange("(kc p) n -> p kc n", p=128)[:, 4:8, :]]
    dsts = [wbr[:, 0:4, :], wbr[:, 4:8, :], wbr[:, 8:12, :], wo[:, 0:4, :], wo[:, 4:8, :]]
    for k in range(5):
        i = k % 2
        P.dma("sp", lambda e, k=k, i=i: e.dma_start(out=stg[i], in_=srcs[k]), writes=["mstg%d" % i])
        P.op("pool", lambda e, k=k, i=i: e.tensor_copy(out=dsts[k], in_=stg[i]), reads=["mstg%d" % i], writes=["mw%d" % k])
    wreads = ["mw%d" % k for k in range(5)]
    yin = [A.alloc("myin%d" % i, [12, 512], BF16) for i in range(2)]
    gl = [A.alloc("mgl%d" % i, [3, 512], BF16) for i in range(2)]
    sgm = [A.alloc("msg%d" % i, [3, 512]) for i in range(2)]
    tj = [A.alloc("mtj%d" % i, [3, 512]) for i in range(2)]
    mT = A.alloc("mT", [8, 512], BF16)
    yT = A.alloc("myT", [8, 512])
    ysq = A.alloc("mysq", [8, 512], BF16)
    xin = A.alloc("mxin", [8, 512])
    rs = A.alloc("mrs", [512])
    xsrc = (g.xT_in if l == 0 else g.xs).rearrange("(kc p) t -> p kc t", p=128)
    xdst = g.xs.rearrange("(kc p) t -> p kc t", p=128)
    odst = g.outT.rearrange("(kc p) t -> p kc t", p=128)
    NTT = NT // 512
    gsrc = g.PT[40:64].rearrange("(j t) p n -> t p j n", j=3)
    ev = 0

    def loady(T):
        i = T % 2
        P.dma("sp", lambda e: e.dma_start(out=yin[i], in_=g.Y3[:, :, :, T * 512:(T + 1) * 512].rearrange("j k p n -> p (j k) n")),
              reads=["Y3"], writes=["myin%d" % i])

    loady(0)
    gcnt = 0
    for T in range(NTT):
        i = T % 2
        if T + 1 < NTT:
            loady(T + 1)
        P.dma("sp", lambda e, T=T: e.dma_start(out=xin, in_=xsrc[:, :, T * 512:(T + 1) * 512]), reads=["xs"], writes=["mxin"])
        for dt in range(8):
            gi = gcnt % 2; gcnt += 1
            P.dma("sp", lambda e, dt=dt, T=T, gi=gi: e.dma_start(out=gl[gi], in_=gsrc[dt][:, :, T * 512:(T + 1) * 512]), writes=["mgl%d" % gi])
            P.op("act", lambda e, gi=gi: e.activation(out=sgm[gi], in_=gl[gi], func=AF.Sigmoid), reads=["mgl%d" % gi], writes=["msg%d" % gi])
            for jb in range(3):
                bank = (ev % 6); ev += 1
                ps = g.ps[bank]
                for kc in range(4):
                    P.op("pe", lambda e, ps=ps, jb=jb, kc=kc, dt=dt, i=i: e.matmul(
                        ps, lhsT=wbr[:, jb * 4 + kc, dt * 128:(dt + 1) * 128], rhs=yin[i][:, jb * 4 + kc, :], start=(kc == 0), stop=(kc == 3)),
                        reads=wreads + ["myin%d" % i], writes=["mps%d" % bank])
                P.op("dve", lambda e, ps=ps, jb=jb, gi=gi: e.tensor_tensor(out=tj[gi][:, jb, :], in0=ps, in1=sgm[gi][:, jb, :], op=ALU.mult),
                     reads=["mps%d" % bank, "msg%d" % gi], writes=["mtj%d_%d" % (gi, jb)])
            P.op("pool", lambda e, gi=gi: e.tensor_tensor(out=tj[gi][:, 0, :], in0=tj[gi][:, 0, :], in1=tj[gi][:, 1, :], op=ALU.add),
                 reads=["mtj%d_0" % gi, "mtj%d_1" % gi], writes=["mtj%d_0" % gi])
            P.op("pool", lambda e, gi=gi, dt=dt: e.tensor_tensor(out=mT[:, dt, :], in0=tj[gi][:, 0, :], in1=tj[gi][:, 2, :], op=ALU.add),
                 reads=["mtj%d_0" % gi, "mtj%d_2" % gi], writes=["mT%d" % dt])
        mreads = ["mT%d" % dt for dt in range(8)]
        for d2 in range(8):
            bank = (ev % 6); ev += 1
            ps = g.ps[bank]
            for kc in range(8):
                P.op("pe", lambda e, ps=ps, kc=kc, d2=d2: e.matmul(ps, lhsT=wo[:, kc, d2 * 128:(d2 + 1) * 128], rhs=mT[:, kc, :],
                                                                   start=(kc == 0), stop=(kc == 7)), reads=wreads + mreads, writes=["mps%d" % bank])
            P.op("act", lambda e, ps=ps, d2=d2: e.copy(out=yT[:, d2, :], in_=ps), reads=["mps%d" % bank], writes=["myT%d" % d2])
            P.op("act", lambda e, d2=d2: e.activation(out=ysq[:, d2, :], in_=yT[:, d2, :], func=AF.Square), reads=["myT%d" % d2], writes=["mysq%d" % d2])
        ssp = g.ps[6]
        for d2 in range(8):
            P.op("pe", lambda e, d2=d2: e.matmul(ssp, lhsT=g.ones_bf, rhs=ysq[:, d2, :], start=(d2 == 0), stop=(d2 == 7)),
                 reads=["mysq%d" % d2], writes=["mss"])
        rsqrt_act(P, rs, ssp, float(D * EPS), ["mss"], ["mrs"])
        for half in range(2):
            pc = 2 * T + half
            j = seg_j(pc)
            isctx = (pc % 9 == 0)
            if last and isctx:
                continue
            c0, c1 = half * 256, half * 256 + 256
            for d2 in range(8):
                P.op("dve", lambda e, d2=d2, j=j, c0=c0, c1=c1: e.scalar_tensor_tensor(
                    out=yT[:, d2, c0:c1], in0=yT[:, d2, c0:c1], scalar=g.modG[:, d2, j:j + 1], in1=rs[:, c0:c1], op0=ALU.mult, op1=ALU.mult),
                    reads=["myT%d" % d2, "mrs"], writes=["myT%d" % d2])
            P.op("pool", lambda e, c0=c0, c1=c1: e.tensor_tensor(out=xin[:, :, c0:c1], in0=xin[:, :, c0:c1], in1=yT[:, :, c0:c1], op=ALU.add),
                 reads=["mxin"] + ["myT%d" % d2 for d2 in range(8)], writes=["mxin"])
            if last:
                b = pc // 9
                q = pc % 9 - 1
                oc = b * LAT + q * 256
                P.dma("sp", lambda e, c0=c0, c1=c1, oc=oc: e.dma_start(out=odst[:, :, oc:oc + 256], in_=xin[:, :, c0:c1]), reads=["mxin"], writes=["out"])
            else:
                P.dma("sp", lambda e, c0=c0, c1=c1, pc=pc: e.dma_start(out=xdst[:, :, pc * 256:(pc + 1) * 256], in_=xin[:, :, c0:c1]),
                      reads=["mxin"], writes=["xs_w"])


def host_lruw(inp):
    out = np.zeros((DEPTH, 128, 16, 128), np.float32)
    for gi, name in enumerate(("lru_wa", "lru_wx")):
        w = inp[name]
        for d in range(2):
            for ct in range(4):
                for hb in range(2):
                    out[:, hb * 64:(hb + 1) * 64, gi * 8 + d * 4 + ct, hb * 64:(hb + 1) * 64] = w[:, d, ct * 2 + hb]
    return out.reshape(DEPTH, 128, 16 * 128)


def host_rpbias(inp):
    rpb = inp["rpb"]
    out = np.full((DEPTH, 4, 128, 5, 640), NEG, np.float32)
    cq = np.arange(64)
    kc = np.arange(64)
    win = np.clip(cq - 8, 0, 48)
    col_ok = (kc[None, :] >= win[:, None]) & (kc[None, :] < win[:, None] + 16)
    dc = np.clip(kc[None, :] - cq[:, None], -15, 15) + 15
    for pat, ti in enumerate((0, 1, 2, 14, 15)):
        sr = min(max(2 * ti - 4, 0), 22)
        for a in range(2):
            r = 2 * ti + a
            r0 = min(max(r - 4, 0), 24)
            for m in range(10):
                kr = sr + m
                if r0 <= kr < r0 + 8:
                    dr = kr - r + 7
                    vals = rpb[:, :, dr, :][:, :, dc]
                    blk = np.where(col_ok[None, None], vals, NEG)
                    out[:, :, a * 64:(a + 1) * 64, pat, m * 64:(m + 1) * 64] = blk
    return out.reshape(DEPTH, 4, 128, 5 * 640)


def prep_inputs(inp, nl=DEPTH):
    w_in = inp["w_in"]
    w_big = np.ascontiguousarray(np.concatenate([w_in[:, :, :2048], w_in[:, :, 2064:]], axis=2))
    w_small = np.ascontiguousarray(w_in[:, :, 2048:2064])
    prm = host_params(inp)
    cst = host_consts()
    lruw = host_lruw(inp)
    rpbias = host_rpbias(inp)
    rope = host_rope().reshape(128, 2 * LAT)
    maps = []
    nb_total = inp["x"].shape[0]
    for core in range(nb_total // NB):
        xs = []
        cs = []
        for b in range(core * NB, (core + 1) * NB):
            xs.append(inp["ctx"][b].T)
            xs.append(inp["x"][b].T)
            cs.append(inp["c"][b])
        cs.append(inp["c_ctx"])
        xT = np.ascontiguousarray(np.concatenate(xs, axis=1))
        cT = np.ascontiguousarray(np.stack(cs, axis=1).reshape(8, 128, 3).transpose(1, 0, 2).reshape(128, 24))
        maps.append({"xT": xT, "cT": cT, "w_mod": inp["w_mod"][:nl], "w_big": w_big[:nl], "w_small": w_small[:nl],
                     "prm": prm[:nl], "cst": cst, "rope": rope, "lruw": lruw[:nl], "rpbias": rpbias[:nl],
                     "w_br": inp["w_branch"][:nl], "w_out": inp["w_out"][:nl]})
    return maps


def kernel(**inputs):
    inp = {k: np.asarray(v) for k, v in inputs.items()}
    maps = prep_inputs(inp)
    nc = build()
    res = run_bass_kernel_spmd(nc, maps, core_ids=list(range(len(maps))))
    outs = []
    for r in res.results:
        oT = np.asarray(r["outT"])
        for b in range(NB):
            outs.append(oT[:, b * LAT:(b + 1) * LAT].T)
    return np.ascontiguousarray(np.stack(outs, axis=0)).astype(np.float32)
```

```python
import contextlib
import re
import numpy as np
import ml_dtypes
import concourse.bass as bass
import concourse.mybir as mybir
from concourse.bass_utils import run_bass_kernel_spmd

F32 = mybir.dt.float32
BF16 = mybir.dt.bfloat16
AF = mybir.ActivationFunctionType
ALU = mybir.AluOpType
AX = mybir.AxisListType

D = 1024
NB = 2
CTX = 256
LAT = 2048
S = CTX + LAT
NT = NB * S
DEPTH = 4
EPS = 1e-6
ENGS = ("pe", "act", "dve", "pool", "sp")
NDMASEM = 12


_BANK_RULES = [
    (re.compile(r"^modps$"), lambda m: [0]),
    (re.compile(r"^(?:ssps|pps|lps|mps|c0ps|cps)(\d)"), lambda m: [int(m.group(1))]),
    (re.compile(r"^nQTp(\d)"), lambda m: [4 + int(m.group(1))]),
    (re.compile(r"^nqo(\d)"), lambda m: [6 + int(m.group(1))]),
    (re.compile(r"^nS(\d)"), lambda m: [2 * int(m.group(1)), 2 * int(m.group(1)) + 1]),
    (re.compile(r"^nPTp(\d)"), lambda m: [2 * int(m.group(1)) + 1]),
    (re.compile(r"^no(\d)"), lambda m: [2 * int(m.group(1))]),
    (re.compile(r"^mss$"), lambda m: [6]),
    (re.compile(r"^cpA(\d)_"), lambda m: [2 * int(m.group(1))]),
    (re.compile(r"^cpB(\d)_"), lambda m: [2 * int(m.group(1)) + 1]),
    (re.compile(r"^cp6_"), lambda m: [6]),
    (re.compile(r"^cp7_"), lambda m: [7]),
]
_BANK_CACHE = {}


def banks_of(name):
    b = _BANK_CACHE.get(name)
    if b is None:
        b = []
        for rx, fn in _BANK_RULES:
            m = rx.match(name)
            if m:
                b = fn(m)
                break
        _BANK_CACHE[name] = b
    return b


class Res:
    __slots__ = ("w", "r")

    def __init__(self):
        self.w = None
        self.r = {}


class Prog:
    def __init__(self, nc):
        self.nc = nc
        self.ops = {e: [] for e in ENGS}
        self.cnt = {e: 0 for e in ENGS}
        self.dcnt = {e: 0 for e in ENGS}
        self.seen = {e: {} for e in ENGS}
        self.pending = {e: [] for e in ENGS}
        self.res = {}
        self.inflight = {e: {} for e in ENGS}

    def R(self, name):
        r = self.res.get(name)
        if r is None:
            r = self.res[name] = Res()
        return r

    def _need(self, eng, tok, waits):
        if tok is None:
            return
        if tok[0] == "c":
            key = ("c", tok[1]); val = tok[2]
        else:
            key = ("d", tok[1], tok[2]); val = tok[3]
        if self.seen[eng].get(key, 0) >= val:
            return
        self.seen[eng][key] = val
        waits.append((key, val))

    def _deps(self, eng, reads, writes, waits):
        for t in self.pending[eng]:
            self._need(eng, t, waits)
        self.pending[eng] = []
        banks = set()
        for r in reads:
            banks.update(banks_of(r))
        for r in writes:
            banks.update(banks_of(r))
        self._banks = banks
        for k in banks:
            r = self.R("BANK%d" % k)
            if r.w is not None and r.w[1] != eng:
                self._need(eng, r.w, waits)
        for r in reads:
            self._need(eng, self.R(r).w, waits)
        for r in writes:
            r = self.R(r)
            self._need(eng, r.w, waits)
            for t in r.r.values():
                self._need(eng, t, waits)

    def _commit(self, tok, reads, writes):
        key = tok[:2] if tok[0] == "c" else tok[:3]
        for k in self._banks:
            self.R("BANK%d" % k).w = tok
        for r in reads:
            self.R(r).r[key] = tok
        for r in writes:
            r = self.R(r)
            r.w = tok
            r.r = {}

    limit = None
    nrec = 0

    def op(self, eng, fn, reads=(), writes=()):
        if Prog.limit is not None:
            Prog.nrec += 1
            if Prog.nrec > Prog.limit:
                return None
        waits = []
        self._deps(eng, reads, writes, waits)
        self.cnt[eng] += 1
        tok = ("c", eng, self.cnt[eng])
        self.ops[eng].append((waits, fn, tok))
        self._commit(tok, reads, writes)
        return tok

    def dma(self, eng, fn, reads=(), writes=()):
        if Prog.limit is not None:
            Prog.nrec += 1
            if Prog.nrec > Prog.limit:
                return None
        waits = []
        self._deps(eng, reads, writes, waits)
        i = self.dcnt[eng]
        self.dcnt[eng] += 1
        slot = i % NDMASEM
        val = 16 * (i // NDMASEM + 1)
        if i >= NDMASEM:
            self._need(eng, ("d", eng, slot, val - 16), waits)
        tok = ("d", eng, slot, val)
        self.inflight[eng][slot] = tok
        self.ops[eng].append((waits, fn, tok))
        self._commit(tok, reads, writes)
        return tok

    def barrier(self):
        toks = []
        for e in ("pe", "act", "dve", "pool"):
            if self.cnt[e]:
                toks.append(("c", e, self.cnt[e]))
        for e in ENGS:
            toks.extend(self.inflight[e].values())
        for e in ENGS:
            self.pending[e] = list(toks)
        self.res = {}

    def emit(self, final_tokens=()):
        nc = self.nc
        with contextlib.ExitStack() as st:
            csem = {e: st.enter_context(nc.semaphore("c_" + e)) for e in ("pe", "act", "dve", "pool")}
            dsem = {}
            for e in ENGS:
                for s in range(min(NDMASEM, self.dcnt[e])):
                    dsem[(e, s)] = st.enter_context(nc.semaphore("d_%s_%d" % (e, s)))
            fw = []
            for t in self.pending["sp"]:
                self._need("sp", t, fw)
            for t in final_tokens:
                self._need("sp", t, fw)
            block = st.enter_context(nc.Block())

            def semof(key):
                return csem[key[1]] if key[0] == "c" else dsem[(key[1], key[2])]

            def replay(ename):
                def body(e):
                    for waits, fn, tok in self.ops[ename]:
                        for key, val in waits:
                            e.wait_ge(semof(key), val)
                        ins = fn(e)
                        if tok[0] == "c":
                            ins.then_inc(csem[tok[1]], 1)
                        else:
                            ins.then_inc(dsem[(tok[1], tok[2])], 16)
                    if ename == "sp":
                        for key, val in fw:
                            e.wait_ge(semof(key), val)
                return body

            block.tensor(replay("pe"))
            block.scalar(replay("act"))
            block.vector(replay("dve"))
            block.gpsimd(replay("pool"))
            block.sync(replay("sp"))


class Arena:
    def __init__(self, ap, nbytes):
        self.ap = ap
        self.nbytes = nbytes
        self.off = 0
        self.names = {}

    def reset(self):
        self.off = 0
        self.names = {}

    def alloc(self, name, shape, dt=F32):
        esz = 4 if dt == F32 else 2
        n = int(np.prod(shape))
        nb = (n * esz + 63) // 64 * 64
        assert self.off + nb <= self.nbytes, "SBUF arena overflow at %s: %d + %d > %d" % (name, self.off, nb, self.nbytes)
        v = self.ap[:, self.off // 4:(self.off + nb) // 4]
        self.off += nb
        if dt != F32:
            v = v.bitcast(dt)
        v = v[:, 0:n]
        if len(shape) == 2:
            v = v.rearrange("p (a b) -> p a b", a=shape[0])
        elif len(shape) == 3:
            v = v.rearrange("p (a b c) -> p a b c", a=shape[0], b=shape[1])
        self.names[name] = v
        return v


class Ctx:
    pass


LASTP = None


def build(nlayers=DEPTH, dbg=(), run="CDEF"):
    nc = bass.Bass("TRN2", target_bir_lowering=False)
    g = Ctx()
    g.run = run
    g.nc = nc
    g.dbg = dbg

    def din(name, shape, dt=F32):
        return nc.dram_tensor(name, list(shape), dt, kind="ExternalInput").ap()

    def dscr(name, shape, dt=F32):
        kind = "ExternalOutput" if name in dbg else "ExternalInput" if (name + "_in") in dbg else "Internal"
        return nc.dram_tensor(name, list(shape), dt, kind=kind).ap()

    g.xT_in = din("xT", [D, NT])
    g.cT = din("cT", [128, 8 * 3])
    ND = nlayers
    g.w_mod = din("w_mod", [ND, D, 3 * D])
    g.w_big = din("w_big", [ND, D, 8192])
    g.w_small = din("w_small", [ND, D, 16])
    g.prm = din("prm", [ND, 128, PRM_N])
    g.lruw = din("lruw", [ND, 128, 16 * 128])
    g.rpbias = din("rpbias", [ND, 4, 128, 5 * 640])
    g.w_br = din("w_br", [ND, 3, 512, D])
    g.w_out = din("w_out", [ND, D, D])
    g.cst = din("cst", [128, CST_N])
    g.rope = din("rope", [128, 2 * LAT])
    g.outT = nc.dram_tensor("outT", [D, NB * LAT], F32, kind="ExternalOutput").ap()

    g.xs = dscr("xs", [D, NT])
    g.PT = dscr("PT", [64, 128, NT], BF16)
    g.VC = dscr("VC", [NT, 512], BF16)
    g.BA = dscr("BA", [NT, 16])
    g.Y3 = dscr("Y3", [3, 4, 128, NT], BF16)

    with contextlib.ExitStack() as st:
        arena_t = st.enter_context(nc.sbuf_tensor("arena", [128, ARENA_BYTES // 4], F32))
        pers_t = st.enter_context(nc.sbuf_tensor("pers", [128, PERS_BYTES // 4], F32))
        g.psall = st.enter_context(nc.psum_tensor("psall", [128, 4096], F32))
        g.ps = [g.psall[:, i * 512:(i + 1) * 512] for i in range(8)]
        g.A = Arena(arena_t[:], ARENA_BYTES)
        g.PA = Arena(pers_t[:], PERS_BYTES)
        P = g.P = Prog(nc)
        global LASTP
        LASTP = P
        phase_setup(g)
        for l in range(nlayers):
            last = (l == DEPTH - 1)
            phase_mod(g, l)
            if "skipB" not in dbg:
                phase_norm_proj(g, l)
            if "stopB" in dbg:
                break
            if "C" in g.run:
                phase_gdn(g, l, last)
            if "D" in g.run:
                phase_lru(g, l)
            if "E" in g.run:
                phase_natten(g, l, last)
            if "F" in g.run:
                phase_merge(g, l, last)
        P.barrier()
        P.emit()
    return nc


def _cst_layout():
    lay = {}
    off = 0
    names = ["ident", "ones", "tri0", "tri1", "mit0", "mit1", "mst0", "mst1"] + ["lm%d" % i for i in range(7)] + ["prot"]
    for name, n in [(nm, 128) for nm in names]:
        lay[name] = (off, n)
        off += n
    return lay, off


CST_LAY, CST_N = _cst_layout()


def _prm_layout():
    lay = {}
    off = 0
    for name, n in (("bmodT", 24), ("g_pre", 8), ("g_post", 8), ("conv_b", 16), ("conv_b_bias", 4),
                    ("lru_ba", 8), ("lru_bx", 8), ("lru_lam", 8), ("conv_a", 48), ("onorm_a", 1),
                    ("dt_bias", 8), ("a_log", 8)):
        lay[name] = (off, n)
        off += n
    return lay, off


PRM_LAY, PRM_N = _prm_layout()

ARENA_BYTES = 190 * 1024
PERS_BYTES = 16 * 1024


def host_consts():
    c = np.zeros((128, CST_N), np.float32)
    o, n = CST_LAY["ident"]; c[:, o:o + n] = np.eye(128, dtype=np.float32)
    o, n = CST_LAY["ones"]; c[:, o:o + n] = 1.0
    p = np.arange(128)[:, None]; f = np.arange(128)[None, :]

    def put(name, m):
        o, n = CST_LAY[name]; c[:, o:o + n] = m.astype(np.float32)
    put("tri0", p <= f); put("tri1", p >= f)
    put("mit0", f >= p); put("mit1", f <= p)
    put("mst0", f > p); put("mst1", f < p)
    for lv in range(7):
        put("lm%d" % lv, ((p >> (lv + 1)) == (f >> (lv + 1))) & ((p >> lv) != (f >> lv)))
    pr = np.zeros((128, 128), np.float32)
    for m in range(128):
        if (m % 64) < 32:
            pr[m + 32, m] = -1.0
        else:
            pr[m - 32, m] = 1.0
    put("prot", pr)
    return c


def host_rope():
    t = np.arange(LAT)
    rows = (t // 64).astype(np.float32); cols = (t % 64).astype(np.float32)
    inv = (10000.0 ** (-np.arange(32, dtype=np.float32) / 32)).astype(np.float32)
    out = np.zeros((128, 2, LAT), np.float32)
    for p in range(128):
        pos = rows if p < 64 else cols
        ang = (pos * inv[p % 32]).astype(np.float32)
        out[p, 0] = np.cos(ang); out[p, 1] = np.sin(ang)
    return out


def host_params(inp):
    p = np.zeros((DEPTH, 128, PRM_N), np.float32)
    for l in range(DEPTH):
        def put(name, arr):
            o, n = PRM_LAY[name]
            p[l, :, o:o + n] = arr.reshape(128, n)
        put("bmodT", inp["b_mod"][l].reshape(24, 128).T)
        put("g_pre", inp["g_pre"][l].reshape(8, 128).T)
        put("g_post", inp["g_post"][l].reshape(8, 128).T)
        put("conv_b", inp["conv_b"][l].reshape(4, 4, 128).transpose(2, 1, 0))
        put("conv_b_bias", inp["conv_b_bias"][l].reshape(4, 128).T)
        put("lru_ba", inp["lru_ba"][l].reshape(2, 4, 128).transpose(2, 0, 1))
        put("lru_bx", inp["lru_bx"][l].reshape(2, 4, 128).transpose(2, 0, 1))
        put("lru_lam", inp["lru_lam"][l].reshape(2, 4, 128).transpose(2, 0, 1))
        put("conv_a", inp["conv_a"][l].reshape(4, 12, 128).transpose(2, 1, 0))
        put("onorm_a", inp["onorm_a"][l].reshape(128, 1))
        put("dt_bias", np.broadcast_to(inp["dt_bias"][l].reshape(1, 8), (128, 8)))
        put("a_log", np.broadcast_to(inp["a_log"][l].reshape(1, 8), (128, 8)))
    return p


def rsqrt_act(P, out, in_, bias, reads, writes):
    P.op("act", lambda e: e.activation(out=out, in_=in_, func=AF.Ln, bias=bias, scale=1.0), reads=reads, writes=writes)
    P.op("act", lambda e: e.activation(out=out, in_=out, func=AF.Exp, scale=-0.5), reads=writes, writes=writes)


def phase_setup(g):
    nc, P, PA = g.nc, g.P, g.PA
    g.cst_sb = PA.alloc("cst", [CST_N])
    P.dma("sp", lambda e: e.dma_start(out=g.cst_sb, in_=g.cst[:, :]), writes=["cst"])
    g.cT_sb = PA.alloc("cT", [8, 3])
    P.dma("sp", lambda e: e.dma_start(out=g.cT_sb, in_=g.cT.rearrange("p (a b) -> p a b", a=8)), writes=["cT"])
    g.sc = PA.alloc("sc", [8, 3])
    P.op("act", lambda e: e.activation(out=g.sc, in_=g.cT_sb, func=AF.Silu), reads=["cT"], writes=["sc"])
    o, n = CST_LAY["ident"]; g.ident = g.cst_sb[:, o:o + n]
    o, n = CST_LAY["ones"]; g.ones = g.cst_sb[:, o:o + n]
    g.ident_bf = PA.alloc("ident_bf", [128], BF16)
    g.ones_bf = PA.alloc("ones_bf", [128], BF16)
    P.op("dve", lambda e: e.tensor_copy(out=g.ident_bf, in_=g.ident), reads=["cst"], writes=["ident_bf"])
    P.op("dve", lambda e: e.tensor_copy(out=g.ones_bf, in_=g.ones), reads=["cst"], writes=["ones_bf"])
    g.prm_sb = PA.alloc("prm", [PRM_N])
    g.mod = PA.alloc("mod", [24, 3])
    g.modA = PA.alloc("modA", [8, 3])
    g.modG = PA.alloc("modG", [8, 3])
    g.gpre32 = PA.alloc("gpre32", [8])
    g.gpost32 = PA.alloc("gpost32", [8])
    P.barrier()


def prm(g, name):
    o, n = PRM_LAY[name]
    return g.prm_sb[:, o:o + n]


def phase_mod(g, l):
    nc, P, A = g.nc, g.P, g.A
    P.barrier()
    A.reset()
    P.dma("sp", lambda e: e.dma_start(out=g.prm_sb, in_=g.prm[l, :, :]), writes=["prm"])
    wv = g.w_mod[l].rearrange("(kc p) n -> p kc n", p=128)
    wbuf = [A.alloc("wm%d" % i, [8, 512]) for i in range(2)]
    ps = g.ps[0][:, 0:72]
    for grp in range(6):
        wb = wbuf[grp % 2]
        P.dma("sp", lambda e, wb=wb, grp=grp: e.dma_start(out=wb, in_=wv[:, :, grp * 512:(grp + 1) * 512]),
              writes=["wm%d" % (grp % 2)])
        for ci in range(4):
            ct = grp * 4 + ci
            for kc in range(8):
                P.op("pe", lambda e, wb=wb, ci=ci, kc=kc, ct=ct: e.matmul(
                    ps[:, ct * 3:(ct + 1) * 3], lhsT=wb[:, kc, ci * 128:(ci + 1) * 128], rhs=g.sc[:, kc, :],
                    start=(kc == 0), stop=(kc == 7)),
                    reads=["wm%d" % (grp % 2), "sc"], writes=["modps"])
    ps3 = ps.rearrange("p (a b) -> p a b", a=24)
    bm = prm(g, "bmodT").unsqueeze(2).to_broadcast([128, 24, 3])
    P.op("dve", lambda e: e.tensor_tensor(out=g.mod, in0=ps3, in1=bm, op=ALU.add), reads=["modps", "prm"], writes=["mod"])
    P.op("dve", lambda e: e.tensor_scalar_mul(out=g.gpre32, in0=prm(g, "g_pre"), scalar1=32.0), reads=["prm"], writes=["gpre32"])
    P.op("dve", lambda e: e.tensor_scalar_mul(out=g.gpost32, in0=prm(g, "g_post"), scalar1=32.0), reads=["prm"], writes=["gpost32"])
    P.op("dve", lambda e: e.scalar_tensor_tensor(out=g.modA, in0=g.mod[:, 8:16, :], scalar=1.0,
                                                 in1=g.gpre32.unsqueeze(2).to_broadcast([128, 8, 3]),
                                                 op0=ALU.add, op1=ALU.mult), reads=["mod", "gpre32"], writes=["modA"])
    P.op("dve", lambda e: e.tensor_tensor(out=g.modG, in0=g.mod[:, 16:24, :],
                                          in1=g.gpost32.unsqueeze(2).to_broadcast([128, 8, 3]), op=ALU.mult),
         reads=["mod", "gpost32"], writes=["modG"])


def seg_j(pc):
    return 2 if pc % 9 == 0 else pc // 9


def phase_norm_proj(g, l):
    nc, P, A = g.nc, g.P, g.A
    P.barrier()
    A.reset()
    xsrc = (g.xT_in if l == 0 else g.xs).rearrange("(kc p) t -> p kc t", p=128)
    hT = A.alloc("hT", [8, NT], BF16)
    xp = [A.alloc("xp%d" % i, [8, 256]) for i in range(2)]
    sq = [A.alloc("sq%d" % i, [8, 256], BF16) for i in range(2)]
    xn = [A.alloc("xn%d" % i, [8, 256]) for i in range(2)]
    rstd = [A.alloc("rstd%d" % i, [256]) for i in range(2)]
    NPC = NT // 256

    def load(pc):
        i = pc % 2
        P.dma("sp", lambda e: e.dma_start(out=xp[i], in_=xsrc[:, :, pc * 256:(pc + 1) * 256]), writes=["xp%d" % i])

    load(0)
    for pc in range(NPC):
        i = pc % 2
        j = seg_j(pc)
        if pc + 1 < NPC:
            load(pc + 1)
        P.op("act", lambda e, i=i: e.activation(out=sq[i], in_=xp[i], func=AF.Square), reads=["xp%d" % i], writes=["sq%d" % i])
        ps = g.ps[i][:, 0:256]
        for kc in range(8):
            P.op("pe", lambda e, i=i, kc=kc, ps=ps: e.matmul(ps, lhsT=g.ones_bf, rhs=sq[i][:, kc, :], start=(kc == 0), stop=(kc == 7)),
                 reads=["sq%d" % i, "ones_bf"], writes=["ssps%d" % i])
        rsqrt_act(P, rstd[i], ps, float(D * EPS), ["ssps%d" % i], ["rstd%d" % i])
        P.op("dve", lambda e, i=i: e.tensor_tensor(out=xn[i], in0=xp[i], in1=rstd[i].unsqueeze(1).to_broadcast([128, 8, 256]),
                                                  op=ALU.mult), reads=["xp%d" % i, "rstd%d" % i], writes=["xn%d" % i])
        for kc in range(8):
            P.op("act", lambda e, i=i, kc=kc, j=j, pc=pc: e.activation(
                out=hT[:, kc, pc * 256:(pc + 1) * 256], in_=xn[i][:, kc, :], func=AF.Identity,
                bias=g.mod[:, kc, j:j + 1], scale=g.modA[:, kc, j:j + 1]),
                reads=["xn%d" % i, "mod", "modA"], writes=["hT%d" % pc])

    wsrc = g.w_big[l].rearrange("(kc p) n -> p kc n", p=128)
    wf = [A.alloc("wf%d" % i, [8, 512]) for i in range(2)]
    wb = [A.alloc("wb%d" % i, [8, 512], BF16) for i in range(2)]
    ost = [A.alloc("ost%d" % i, [512], BF16) for i in range(8)]
    ws_f = A.alloc("ws_f", [8, 16])
    ws_b = A.alloc("ws_b", [8, 16], BF16)
    osm = [A.alloc("osm%d" % i, [16]) for i in range(4)]
    NG = 16
    hreads = ["hT%d" % pc for pc in range(NPC)]

    def loadw(gi):
        i = gi % 2
        P.dma("sp", lambda e: e.dma_start(out=wf[i], in_=wsrc[:, :, gi * 512:(gi + 1) * 512]), writes=["wf%d" % i])

    loadw(0)
    P.dma("sp", lambda e: e.dma_start(out=ws_f, in_=g.w_small[l].rearrange("(kc p) n -> p kc n", p=128)), writes=["ws_f"])
    P.op("pool", lambda e: e.tensor_copy(out=ws_b, in_=ws_f), reads=["ws_f"], writes=["ws_b"])
    ev = 0
    for gi in range(NG):
        i = gi % 2
        if gi + 1 < NG:
            loadw(gi + 1)
        P.op("pool", lambda e, i=i: e.tensor_copy(out=wb[i], in_=wf[i]), reads=["wf%d" % i], writes=["wb%d" % i])
        if gi == 8:
            for tt in range(NT // 128):
                bank = ev % 8
                ps = g.ps[bank]
                for kc in range(8):
                    P.op("pe", lambda e, kc=kc, tt=tt, ps=ps, i=i: e.matmul(
                        ps, lhsT=hT[:, kc, tt * 128:(tt + 1) * 128], rhs=wb[i][:, kc, :], start=(kc == 0), stop=(kc == 7)),
                        reads=["wb%d" % i, "hT%d" % (tt // 2)], writes=["pps%d" % bank])
                o = ost[bank]
                eng = "act" if ev % 2 == 0 else "dve"
                if eng == "act":
                    P.op("act", lambda e, o=o, ps=ps: e.copy(out=o, in_=ps), reads=["pps%d" % bank], writes=["ost%d" % bank])
                else:
                    P.op("dve", lambda e, o=o, ps=ps: e.tensor_copy(out=o, in_=ps), reads=["pps%d" % bank], writes=["ost%d" % bank])
                P.dma("sp", lambda e, o=o, tt=tt: e.dma_start(out=g.VC[tt * 128:(tt + 1) * 128, :], in_=o),
                      reads=["ost%d" % bank], writes=["VC"])
                ev += 1
            continue
        for T in range(NT // 512):
            for ci in range(4):
                ct = gi * 4 + ci
                bank = ev % 8
                ps = g.ps[bank]
                for kc in range(8):
                    P.op("pe", lambda e, kc=kc, T=T, ps=ps, i=i, ci=ci: e.matmul(
                        ps, lhsT=wb[i][:, kc, ci * 128:(ci + 1) * 128], rhs=hT[:, kc, T * 512:(T + 1) * 512],
                        start=(kc == 0), stop=(kc == 7)),
                        reads=["wb%d" % i, "hT%d" % (2 * T), "hT%d" % (2 * T + 1)], writes=["pps%d" % bank])
                o = ost[bank]
                if ev % 2 == 0:
                    P.op("act", lambda e, o=o, ps=ps: e.copy(out=o, in_=ps), reads=["pps%d" % bank], writes=["ost%d" % bank])
                else:
                    P.op("dve", lambda e, o=o, ps=ps: e.tensor_copy(out=o, in_=ps), reads=["pps%d" % bank], writes=["ost%d" % bank])
                P.dma("sp", lambda e, o=o, ct=ct, T=T: e.dma_start(out=g.PT[ct, :, T * 512:(T + 1) * 512], in_=o),
                      reads=["ost%d" % bank], writes=["PT"])
                ev += 1
    for tt in range(NT // 128):
        bank = ev % 8
        ps = g.ps[bank][:, 0:16]
        for kc in range(8):
            P.op("pe", lambda e, kc=kc, tt=tt, ps=ps: e.matmul(
                ps, lhsT=hT[:, kc, tt * 128:(tt + 1) * 128], rhs=ws_b[:, kc, :], start=(kc == 0), stop=(kc == 7)),
                reads=["ws_b", "hT%d" % (tt // 2)], writes=["pps%d" % bank])
        o = osm[tt % 4]
        P.op("dve", lambda e, o=o, ps=ps: e.tensor_copy(out=o, in_=ps), reads=["pps%d" % bank], writes=["osm%d" % (tt % 4)])
        P.dma("sp", lambda e, o=o, tt=tt: e.dma_start(out=g.BA[tt * 128:(tt + 1) * 128, :], in_=o),
              reads=["osm%d" % (tt % 4)], writes=["BA"])
        ev += 1


def evac(P, k, out, in_, reads, writes):
    if k % 2 == 0:
        P.op("act", lambda e: e.copy(out=out, in_=in_), reads=reads, writes=writes)
    else:
        P.op("dve", lambda e: e.tensor_copy(out=out, in_=in_), reads=reads, writes=writes)


PADW = 2312
PC0, PL0 = 2, 261


def load_padded(P, buf, src, name):
    P.dma("sp", lambda e: e.dma_start(out=buf[:, PC0:PC0 + CTX], in_=src[:, 0:CTX]), writes=[name])
    P.dma("sp", lambda e: e.dma_start(out=buf[:, PL0:PL0 + LAT], in_=src[:, CTX:S]), writes=[name])


def store_padded(P, dst, buf, name, wname):
    P.dma("sp", lambda e: e.dma_start(out=dst[:, 0:CTX], in_=buf[:, PC0:PC0 + CTX]), reads=[name], writes=[wname])
    P.dma("sp", lambda e: e.dma_start(out=dst[:, CTX:S], in_=buf[:, PL0:PL0 + LAT]), reads=[name], writes=[wname])


def conv4(P, eng, out, buf, w, bias, reads, wname, tmp=None):
    n = PADW - 5
    o = out[:, 2:2 + n]
    if bias is not None:
        P.op(eng, lambda e: e.tensor_scalar(out=o, in0=buf[:, 0:n], scalar1=w[:, 0:1], scalar2=bias, op0=ALU.mult, op1=ALU.add),
             reads=reads, writes=[wname])
    else:
        P.op(eng, lambda e: e.tensor_scalar_mul(out=o, in0=buf[:, 0:n], scalar1=w[:, 0:1]), reads=reads, writes=[wname])
    for j in range(1, 4):
        if eng == "dve":
            P.op(eng, lambda e, j=j: e.scalar_tensor_tensor(out=o, in0=buf[:, j:j + n], scalar=w[:, j:j + 1], in1=o,
                                                             op0=ALU.mult, op1=ALU.add), reads=reads + [wname], writes=[wname])
        else:
            t = tmp[:, 2:2 + n]
            P.op(eng, lambda e, j=j, t=t: e.tensor_scalar_mul(out=t, in0=buf[:, j:j + n], scalar1=w[:, j:j + 1]), reads=reads, writes=[wname + "_t"])
            P.op(eng, lambda e, t=t: e.tensor_tensor(out=o, in0=o, in1=t, op=ALU.add), reads=[wname, wname + "_t"], writes=[wname])


def phase_lru(g, l):
    nc, P, A = g.nc, g.P, g.A
    P.barrier()
    A.reset()
    wst = A.alloc("lruw_f", [16, 128])
    wbd = A.alloc("lruw_b", [16, 128], BF16)
    P.dma("sp", lambda e: e.dma_start(out=wst, in_=g.lruw[l].rearrange("p (a b) -> p a b", a=16)), writes=["lruw_f"])
    P.op("pool", lambda e: e.tensor_copy(out=wbd, in_=wst), reads=["lruw_f"], writes=["lruw_b"])
    sp = A.alloc("sp", [8]); sc8 = A.alloc("sc8", [8]); sc16 = A.alloc("sc16", [8])
    P.op("act", lambda e: e.activation(out=sp, in_=prm(g, "lru_lam"), func=AF.Exp, scale=-1.0), reads=[], writes=["sp"])
    P.op("act", lambda e: e.activation(out=sp, in_=sp, func=AF.Ln, bias=1.0, scale=1.0), reads=["sp"], writes=["sp"])
    P.op("dve", lambda e: e.tensor_scalar_mul(out=sc8, in0=sp, scalar1=-8.0), reads=["sp"], writes=["sc8"])
    P.op("dve", lambda e: e.tensor_scalar_mul(out=sc16, in0=sp, scalar1=-16.0), reads=["sp"], writes=["sc16"])
    nbuf = 2
    xraw = [A.alloc("xraw%d" % i, [PADW], BF16) for i in range(nbuf)]
    graw = [A.alloc("graw%d" % i, [PADW], BF16) for i in range(nbuf)]
    for i in range(nbuf):
        P.op("pool", lambda e, i=i: e.memset(xraw[i], 0.0), writes=["xraw%d" % i])
        P.op("pool", lambda e, i=i: e.memset(graw[i], 0.0), writes=["graw%d" % i])
    xc = A.alloc("xc", [PADW]); xcb = A.alloc("xcb", [PADW], BF16)
    r = A.alloc("r", [PADW]); ig = A.alloc("ig", [PADW]); av = A.alloc("av", [PADW]); sv = A.alloc("sv", [PADW])
    hh = [A.alloc("hh%d" % i, [PADW]) for i in range(2)]
    sg = A.alloc("sg", [PADW]); yb = [A.alloc("yb%d" % i, [PADW], BF16) for i in range(2)]
    for v, nm in ((xc, "xc"), (r, "r"), (ig, "ig"), (av, "av"), (sv, "sv"), (hh[0], "hh0"), (hh[1], "hh1")):
        P.op("pool", lambda e, v=v: e.memset(v, 0.0), writes=[nm])
    cw = prm(g, "conv_b").rearrange("p (a b) -> p a b", a=4)
    cb = prm(g, "conv_b_bias")
    ba = prm(g, "lru_ba").rearrange("p (a b) -> p a b", a=2)
    bx = prm(g, "lru_bx").rearrange("p (a b) -> p a b", a=2)
    sc8v = sc8.rearrange("p (a b) -> p a b", a=2); sc16v = sc16.rearrange("p (a b) -> p a b", a=2)
    items = [(b, ct) for b in range(NB) for ct in range(4)]

    def load(k):
        b, ct = items[k]
        i = k % nbuf
        load_padded(P, xraw[i], g.PT[16 + ct][:, b * S:(b + 1) * S], "xraw%d" % i)
        load_padded(P, graw[i], g.PT[20 + ct][:, b * S:(b + 1) * S], "graw%d" % i)

    load(0)
    N0, N1 = 2, PADW - 3
    chunks = [(c0, min(c0 + 512, N1)) for c0 in range(N0, N1, 512)]
    ev = 0
    for k, (b, ct) in enumerate(items):
        i = k % nbuf
        if k + 1 < len(items):
            load(k + 1)
        conv4(P, "dve", xc, xraw[i], cw[:, ct, :], cb[:, ct:ct + 1], ["xraw%d" % i], "xc")
        P.op("act", lambda e: e.copy(out=xcb[:, N0:N1], in_=xc[:, N0:N1]), reads=["xc"], writes=["xcb"])
        P.op("act", lambda e, i=i: e.activation(out=sg[:, N0:N1], in_=graw[i][:, N0:N1], func=AF.Silu), reads=["graw%d" % i], writes=["sg"])
        for d in range(2):
            for gi, (dst, bias, nm) in enumerate(((r, ba, "r"), (ig, bx, "ig"))):
                for (c0, c1) in chunks:
                    bank = ev % 8; ev += 1
                    ps = g.ps[bank][:, 0:c1 - c0]
                    P.op("pe", lambda e, ps=ps, gi=gi, d=d, ct=ct, c0=c0, c1=c1: e.matmul(
                        ps, lhsT=wbd[:, gi * 8 + d * 4 + ct, :], rhs=xcb[:, c0:c1], start=True, stop=True),
                        reads=["lruw_b", "xcb"], writes=["lps%d" % bank])
                    P.op("act", lambda e, ps=ps, dst=dst, bias=bias, d=d, ct=ct, c0=c0, c1=c1: e.activation(
                        out=dst[:, c0:c1], in_=ps, func=AF.Sigmoid, bias=bias[:, d, ct:ct + 1], scale=1.0),
                        reads=["lps%d" % bank], writes=[nm])
            P.op("act", lambda e, d=d, ct=ct: e.activation(out=av[:, N0:N1], in_=r[:, N0:N1], func=AF.Exp, scale=sc8v[:, d, ct:ct + 1]),
                 reads=["r", "sc8"], writes=["av"])
            P.op("act", lambda e, d=d, ct=ct: e.activation(out=sv[:, N0:N1], in_=r[:, N0:N1], func=AF.Exp, scale=sc16v[:, d, ct:ct + 1]),
                 reads=["r", "sc16"], writes=["sv"])
            P.op("act", lambda e: e.activation(out=sv[:, N0:N1], in_=sv[:, N0:N1], func=AF.Sqrt, bias=1.0, scale=-1.0),
                 reads=["sv"], writes=["sv"])
            P.op("dve", lambda e: e.tensor_tensor(out=ig[:, N0:N1], in0=ig[:, N0:N1], in1=xc[:, N0:N1], op=ALU.mult), reads=["ig", "xc"], writes=["ig"])
            P.op("dve", lambda e: e.tensor_tensor(out=sv[:, N0:N1], in0=sv[:, N0:N1], in1=ig[:, N0:N1], op=ALU.mult), reads=["ig", "sv"], writes=["sv"])
            h = hh[d]
            if d == 0:
                P.op("dve", lambda e, h=h: e.tensor_tensor_scan(out=h[:, PC0:PC0 + CTX], data0=av[:, PC0:PC0 + CTX], data1=sv[:, PC0:PC0 + CTX],
                                                                initial=0.0, op0=ALU.mult, op1=ALU.add), reads=["av", "sv"], writes=["hh0"])
                P.op("dve", lambda e, h=h: e.tensor_tensor_scan(out=h[:, PL0:PL0 + LAT], data0=av[:, PL0:PL0 + LAT], data1=sv[:, PL0:PL0 + LAT],
                                                                initial=h[:, PC0 + CTX - 1:PC0 + CTX], op0=ALU.mult, op1=ALU.add),
                     reads=["av", "sv", "hh0"], writes=["hh0"])
            else:
                def rv(t, a, n):
                    return t[:, a + n - 1:a - 1:-1] if a > 0 else t[:, a + n - 1::-1]
                P.op("dve", lambda e, h=h: e.tensor_tensor_scan(out=rv(h, PC0, CTX), data0=rv(av, PC0, CTX), data1=rv(sv, PC0, CTX),
                                                                initial=0.0, op0=ALU.mult, op1=ALU.add), reads=["av", "sv"], writes=["hh1"])
                P.op("dve", lambda e, h=h: e.tensor_tensor_scan(out=rv(h, PL0, LAT), data0=rv(av, PL0, LAT), data1=rv(sv, PL0, LAT),
                                                                initial=h[:, PC0:PC0 + 1], op0=ALU.mult, op1=ALU.add),
                     reads=["av", "sv", "hh1"], writes=["hh1"])
        P.op("pool", lambda e: e.tensor_tensor(out=hh[0][:, N0:N1], in0=hh[0][:, N0:N1], in1=hh[1][:, N0:N1], op=ALU.add),
             reads=["hh0", "hh1"], writes=["hh0"])
        P.op("pool", lambda e, i=i: e.tensor_tensor(out=yb[i][:, N0:N1], in0=hh[0][:, N0:N1], in1=sg[:, N0:N1], op=ALU.mult),
             reads=["hh0", "sg"], writes=["yb%d" % i])
        store_padded(P, g.Y3[1, ct][:, b * S:(b + 1) * S], yb[i], "yb%d" % i, "Y3")


def cst(g, name):
    o, n = CST_LAY[name]
    return g.cst_sb[:, o:o + n]


def phase_gdn(g, l, last):
    nc, P, A = g.nc, g.P, g.A
    P.barrier()
    A.reset()
    import os
    if os.environ.get("GDN_LIMIT"):
        Prog.limit = int(os.environ["GDN_LIMIT"]); Prog.nrec = 0
    NTL = NT // 128
    ba = A.alloc("ba", [NTL, 16])
    bav = g.BA.rearrange("(t p) c -> p t c", p=128)
    for t0 in range(0, NTL, 6):
        P.dma("sp", lambda e, t0=t0: e.dma_start(out=ba[:, t0:t0 + 6, :], in_=bav[:, t0:t0 + 6, :]), writes=["ba"])

    def sc4(name):
        return A.alloc(name, [2, NTL, 4])
    beta = sc4("beta"); nbeta = sc4("nbeta"); gl = sc4("gl"); gc = sc4("gc"); egc = sc4("egc"); egl = sc4("egl"); egt = sc4("egt")
    nea = A.alloc("nea", [8])

    def asdth(v):
        return v.rearrange("p t (d h) -> p d t h", d=2)
    P.op("act", lambda e: e.activation(out=beta, in_=asdth(ba[:, :, 0:8]), func=AF.Sigmoid), reads=["ba"], writes=["beta"])
    P.op("dve", lambda e: e.tensor_scalar_mul(out=nbeta, in0=beta, scalar1=-1.0), reads=["beta"], writes=["nbeta"])
    dtb = prm(g, "dt_bias").rearrange("p (d h) -> p d h", d=2).unsqueeze(2).to_broadcast([128, 2, NTL, 4])
    P.op("dve", lambda e: e.tensor_tensor(out=gl, in0=asdth(ba[:, :, 8:16]), in1=dtb, op=ALU.add), reads=["ba"], writes=["gl"])
    P.op("act", lambda e: e.activation(out=gl, in_=gl, func=AF.Exp), reads=["gl"], writes=["gl"])
    P.op("act", lambda e: e.activation(out=gl, in_=gl, func=AF.Ln, bias=1.0, scale=1.0), reads=["gl"], writes=["gl"])
    P.op("act", lambda e: e.activation(out=nea, in_=prm(g, "a_log"), func=AF.Exp), reads=[], writes=["nea"])
    P.op("dve", lambda e: e.tensor_scalar_mul(out=nea, in0=nea, scalar1=-1.0), reads=["nea"], writes=["nea"])
    neab = nea.rearrange("p (d h) -> p d h", d=2).unsqueeze(2).to_broadcast([128, 2, NTL, 4])
    P.op("dve", lambda e: e.tensor_tensor(out=gl, in0=gl, in1=neab, op=ALU.mult), reads=["gl", "nea"], writes=["gl"])
    for d in range(2):
        ps = g.ps[d][:, 0:NTL * 4]
        P.op("pe", lambda e, d=d, ps=ps: e.matmul(ps, lhsT=cst(g, "tri%d" % d), rhs=gl[:, d].rearrange("p t h -> p (t h)"), start=True, stop=True),
             reads=["gl"], writes=["c0ps%d" % d])
        P.op("dve", lambda e, d=d, ps=ps: e.tensor_copy(out=gc[:, d].rearrange("p t h -> p (t h)"), in_=ps), reads=["c0ps%d" % d], writes=["gc"])
    ps = g.ps[2][:, 0:2 * NTL * 4]
    P.op("pe", lambda e: e.matmul(ps, lhsT=g.ones, rhs=gl.rearrange("p d t h -> p (d t h)"), start=True, stop=True), reads=["gl"], writes=["c0ps2"])
    flat = lambda v: v.rearrange("p d t h -> p (d t h)")
    P.op("act", lambda e: e.activation(out=flat(egt), in_=ps, func=AF.Exp), reads=["c0ps2"], writes=["egt"])
    P.op("dve", lambda e: e.tensor_tensor(out=flat(egl), in0=ps, in1=flat(gc), op=ALU.subtract), reads=["c0ps2", "gc"], writes=["egl"])
    P.op("act", lambda e: e.activation(out=egl, in_=egl, func=AF.Exp), reads=["egl"], writes=["egl"])
    P.op("act", lambda e: e.activation(out=egc, in_=gc, func=AF.Exp), reads=["gc"], writes=["egc"])

    P.barrier()
    if "gstop0" in g.dbg:
        return
    rope = A.alloc("rope", [2, LAT])
    P.dma("sp", lambda e: e.dma_start(out=rope, in_=g.rope.rearrange("p (a b) -> p a b", a=2)), writes=["rope"])
    raw = A.alloc("craw", [PADW], BF16)
    P.op("pool", lambda e: e.memset(raw, 0.0), writes=["craw"])
    acc = A.alloc("cacc", [PADW])
    P.op("pool", lambda e: e.memset(acc, 0.0), writes=["cacc"])
    sqb = A.alloc("csqb", [PADW], BF16)
    rinv = [A.alloc("crinv%d" % i, [512]) for i in range(2)]
    t1 = [A.alloc("ct1_%d" % i, [512]) for i in range(2)]
    QT = [A.alloc("cQT%d" % i, [S], BF16) for i in range(2)]
    KT = A.alloc("cKT", [S], BF16)
    VT = A.alloc("cVT", [S], BF16)
    Ktok = A.alloc("cKtok", [18, 128], BF16)
    Vtok = A.alloc("cVtok", [18, 128], BF16)
    Od = [A.alloc("cOd%d" % i, [18, 128]) for i in range(2)]
    graw = A.alloc("cgraw", [S], BF16)
    sg = A.alloc("csg", [S])
    onb = A.alloc("conb", [18, 128], BF16)
    ssq = A.alloc("cssq", [18])
    yaT = A.alloc("cyaT", [S], BF16)
    Sst = [A.alloc("cS%d" % i, [128]) for i in range(2)]
    Sbf = [A.alloc("cSbf%d" % i, [128], BF16) for i in range(2)]
    Ubf = [A.alloc("cU%d" % i, [128], BF16) for i in range(2)]
    O2s = [A.alloc("cO2_%d" % i, [128]) for i in range(2)]
    WkT = [A.alloc("cWkT%d" % i, [18, 128], BF16) for i in range(2)]
    Wvb = [A.alloc("cWvb%d" % i, [18, 128]) for i in range(2)]
    QKm = [A.alloc("cQKm%d" % i, [18, 128], BF16) for i in range(2)]
    Ktl = [A.alloc("cKtl%d" % i, [18, 128], BF16) for i in range(2)]
    def two(name, shape, dt=F32):
        return [A.alloc("%s%d" % (name, i), shape, dt) for i in range(4)]
    Gb = two("cGb", [128]); Em = two("cE", [128]); Dms = two("cDms", [128]); Dmi = two("cDmi", [128])
    ATp = two("cATp", [128], BF16); L0 = two("cL0_", [128], BF16); Tm = two("cT", [128], BF16); TT = two("cTT", [128], BF16)
    Yb = two("cYb", [128], BF16); ZTs = two("cZTs", [128], BF16); Xk = two("cXk", [128], BF16)
    cw = prm(g, "conv_a").rearrange("p (a b) -> p a b", a=12)
    N0, N1 = 2, PADW - 3
    chunks = [(c0, min(c0 + 512, N1)) for c0 in range(N0, N1, 512)]
    state = {"ev": 0, "ch": 0}

    def c1(b, h, qi):
        for which in range(3):
            ct = which * 4 + h
            load_padded(P, raw, g.PT[ct][:, b * S:(b + 1) * S], "craw")
            conv4(P, "dve", acc, raw, cw[:, ct, :], None, ["craw"], "cacc")
            P.op("act", lambda e: e.activation(out=acc[:, N0:N1], in_=acc[:, N0:N1], func=AF.Silu), reads=["cacc"], writes=["cacc"])
            if which == 2:
                P.op("act", lambda e: e.copy(out=VT[:, 0:CTX], in_=acc[:, PC0:PC0 + CTX]), reads=["cacc"], writes=["cVT"])
                P.op("act", lambda e: e.copy(out=VT[:, CTX:S], in_=acc[:, PL0:PL0 + LAT]), reads=["cacc"], writes=["cVT"])
                continue
            P.op("pool", lambda e: e.tensor_tensor(out=sqb[:, N0:N1], in0=acc[:, N0:N1], in1=acc[:, N0:N1], op=ALU.mult), reads=["cacc"], writes=["csqb"])
            for ci, (c0, c1_) in enumerate(chunks):
                bank = 4 + state["ev"] % 2; state["ev"] += 1
                ps = g.ps[bank][:, 0:c1_ - c0]
                ri = rinv[ci % 2][:, 0:c1_ - c0]
                P.op("pe", lambda e, ps=ps, c0=c0, c1_=c1_: e.matmul(ps, lhsT=g.ones_bf, rhs=sqb[:, c0:c1_], start=True, stop=True),
                     reads=["csqb"], writes=["cps%d" % bank])
                rsqrt_act(P, ri, ps, EPS, ["cps%d" % bank], ["crinv%d" % (ci % 2)])
                if which == 0:
                    P.op("dve", lambda e, ri=ri, c0=c0, c1_=c1_: e.scalar_tensor_tensor(out=acc[:, c0:c1_], in0=acc[:, c0:c1_], scalar=float(128.0 ** -0.5),
                                                                                      in1=ri, op0=ALU.mult, op1=ALU.mult),
                         reads=["cacc", "crinv%d" % (ci % 2)], writes=["cacc"])
                else:
                    P.op("dve", lambda e, ri=ri, c0=c0, c1_=c1_: e.tensor_tensor(out=acc[:, c0:c1_], in0=acc[:, c0:c1_], in1=ri, op=ALU.mult),
                         reads=["cacc", "crinv%d" % (ci % 2)], writes=["cacc"])
            for ci in range(4):
                bank = 4 + state["ev"] % 2; state["ev"] += 1
                ps = g.ps[bank]
                a0 = PL0 + ci * 512
                P.op("pe", lambda e, ps=ps, a0=a0: e.matmul(ps, lhsT=cst(g, "prot"), rhs=acc[:, a0:a0 + 512], start=True, stop=True),
                     reads=["cacc"], writes=["cps%d" % bank])
                tt = t1[ci % 2]
                P.op("dve", lambda e, ps=ps, tt=tt, ci=ci: e.tensor_tensor(out=tt, in0=ps, in1=rope[:, 1, ci * 512:(ci + 1) * 512], op=ALU.mult),
                     reads=["cps%d" % bank, "rope"], writes=["ct1_%d" % (ci % 2)])
                P.op("pool", lambda e, a0=a0, ci=ci: e.tensor_tensor(out=acc[:, a0:a0 + 512], in0=acc[:, a0:a0 + 512], in1=rope[:, 0, ci * 512:(ci + 1) * 512], op=ALU.mult),
                     reads=["cacc", "rope", "cps%d" % bank], writes=["cacc"])
                P.op("pool", lambda e, a0=a0, tt=tt: e.tensor_tensor(out=acc[:, a0:a0 + 512], in0=acc[:, a0:a0 + 512], in1=tt, op=ALU.add),
                     reads=["cacc", "ct1_%d" % (ci % 2)], writes=["cacc"])
            dst = QT[qi] if which == 0 else KT
            dn = ("cQT%d" % qi) if which == 0 else "cKT"
            P.op("act", lambda e, dst=dst: e.copy(out=dst[:, 0:CTX], in_=acc[:, PC0:PC0 + CTX]), reads=["cacc"], writes=[dn])
            P.op("act", lambda e, dst=dst: e.copy(out=dst[:, CTX:S], in_=acc[:, PL0:PL0 + LAT]), reads=["cacc"], writes=[dn])
        for src, dstt, sn, dn in ((KT, Ktok, "cKT", "cKtok"), (VT, Vtok, "cVT", "cVtok")):
            for g0 in range(0, 18, 8):
                n = min(8, 18 - g0)
                bank = 4 + state["ev"] % 2; state["ev"] += 1
                pb = g.ps[bank].bitcast(BF16)
                for t in range(n):
                    P.op("pe", lambda e, pb=pb, t=t, g0=g0, src=src: e.transpose(pb[:, t * 128:(t + 1) * 128], src[:, (g0 + t) * 128:(g0 + t + 1) * 128], g.ident_bf),
                         reads=[sn], writes=["cps%d" % bank])
                evac(P, state["ev"], dstt[:, g0:g0 + n, :].rearrange("p a b -> p (a b)"), pb[:, 0:n * 128], ["cps%d" % bank], [dn])

    def chain(b, h, d, t, ip, qi, c):
        T = b * 18 + t
        col = lambda v: v[:, d, T, h:h + 1]
        sl = slice(t * 128, (t + 1) * 128)
        ps = g.ps[c]
        blk = [ps[:, k * 128:(k + 1) * 128] for k in range(4)]
        pn = ["cpA%d_%d" % (c, k) for k in range(4)]
        P.op("pool", lambda e: e.tensor_copy(out=Gb[c], in_=col(gl).to_broadcast([128, 128])), reads=["gl"], writes=["cGb%d" % c])
        yield
        P.op("pe", lambda e: e.matmul(blk[0], lhsT=Gb[c], rhs=cst(g, "tri%d" % d), start=True, stop=True), reads=["cGb%d" % c], writes=[pn[0]])
        P.op("pe", lambda e: e.matmul(blk[1], lhsT=KT[:, sl], rhs=KT[:, sl], start=True, stop=True), reads=["cKT"], writes=[pn[1]])
        P.op("pe", lambda e: e.matmul(blk[2], lhsT=KT[:, sl], rhs=QT[qi][:, sl], start=True, stop=True), reads=["cKT", "cQT%d" % qi], writes=[pn[2]])
        yield
        P.op("dve", lambda e: e.scalar_tensor_tensor(out=Em[c], in0=blk[0], scalar=col(gc), in1=cst(g, "mit%d" % d), op0=ALU.subtract, op1=ALU.mult),
             reads=[pn[0], "gc"], writes=["cE%d" % c])
        yield
        P.op("act", lambda e: e.activation(out=Em[c], in_=Em[c], func=AF.Exp), reads=["cE%d" % c], writes=["cE%d" % c])
        P.op("act", lambda e: e.activation(out=Xk[c], in_=Ktok[:, t, :], func=AF.Copy, scale=col(egc)), reads=["cKtok", "egc"], writes=["cXk%d" % c])
        P.op("act", lambda e: e.activation(out=Ktl[ip][:, t, :], in_=Ktok[:, t, :], func=AF.Copy, scale=col(egl)), reads=["cKtok", "egl"], writes=["cKtl%d_%d" % (ip, t)])
        yield
        P.op("pool", lambda e: e.tensor_tensor(out=Dms[c], in0=Em[c], in1=cst(g, "mst%d" % d), op=ALU.mult), reads=["cE%d" % c], writes=["cDms%d" % c])
        P.op("pool", lambda e: e.tensor_tensor(out=Dmi[c], in0=Em[c], in1=cst(g, "mit%d" % d), op=ALU.mult), reads=["cE%d" % c], writes=["cDmi%d" % c])
        yield
        P.op("dve", lambda e: e.scalar_tensor_tensor(out=ATp[c], in0=blk[1], scalar=col(beta), in1=Dms[c], op0=ALU.mult, op1=ALU.mult),
             reads=[pn[1], "beta", "cDms%d" % c], writes=["cATp%d" % c])
        P.op("dve", lambda e: e.tensor_tensor(out=QKm[ip][:, t, :], in0=blk[2], in1=Dmi[c], op=ALU.mult),
             reads=[pn[2], "cDmi%d" % c], writes=["cQKm%d_%d" % (ip, t)])
        yield
        P.op("pool", lambda e: e.tensor_tensor(out=L0[c], in0=ATp[c], in1=cst(g, "lm0"), op=ALU.mult), reads=["cATp%d" % c], writes=["cL0_%d" % c])
        P.op("pool", lambda e: e.tensor_tensor(out=TT[c], in0=g.ident_bf, in1=L0[c], op=ALU.subtract), reads=["cL0_%d" % c], writes=["cTT%d" % c])
        yield
        pb3 = blk[3].bitcast(BF16)
        P.op("pe", lambda e: e.transpose(pb3[:, 0:128], L0[c], g.ident_bf), reads=["cL0_%d" % c], writes=[pn[3]])
        yield
        P.op("dve", lambda e: e.tensor_tensor(out=Tm[c], in0=g.ident_bf, in1=pb3[:, 0:128], op=ALU.subtract), reads=[pn[3]], writes=["cT%d" % c])
        yield
        for lv in range(1, 7):
            lastlv = (lv == 6)
            P.op("pe", lambda e: e.matmul(blk[0], lhsT=ATp[c], rhs=Tm[c], start=True, stop=True), reads=["cATp%d" % c, "cT%d" % c], writes=[pn[0]])
            yield
            P.op("dve", lambda e, lv=lv: e.tensor_tensor(out=Yb[c], in0=blk[0], in1=cst(g, "lm%d" % lv), op=ALU.mult), reads=[pn[0]], writes=["cYb%d" % c])
            yield
            if not lastlv:
                P.op("pe", lambda e: e.matmul(blk[1], lhsT=TT[c], rhs=Yb[c], start=True, stop=True), reads=["cTT%d" % c, "cYb%d" % c], writes=[pn[1]])
            P.op("pe", lambda e: e.matmul(blk[2], lhsT=Yb[c], rhs=TT[c], start=True, stop=True), reads=["cTT%d" % c, "cYb%d" % c], writes=[pn[2]])
            yield
            P.op("act", lambda e: e.copy(out=ZTs[c], in_=blk[2]), reads=[pn[2]], writes=["cZTs%d" % c])
            if not lastlv:
                P.op("dve", lambda e: e.tensor_tensor(out=Tm[c], in0=Tm[c], in1=blk[1], op=ALU.subtract), reads=["cT%d" % c, pn[1]], writes=["cT%d" % c])
            yield
            P.op("pool", lambda e: e.tensor_tensor(out=TT[c], in0=TT[c], in1=ZTs[c], op=ALU.subtract), reads=["cTT%d" % c, "cZTs%d" % c], writes=["cTT%d" % c])
            yield
        P.op("pe", lambda e: e.matmul(blk[0], lhsT=TT[c], rhs=Vtok[:, t, :], start=True, stop=True), reads=["cTT%d" % c, "cVtok"], writes=[pn[0]])
        P.op("pe", lambda e: e.matmul(blk[1], lhsT=Xk[c], rhs=TT[c], start=True, stop=True), reads=["cTT%d" % c, "cXk%d" % c], writes=[pn[1]])
        yield
        P.op("act", lambda e: e.activation(out=Wvb[ip][:, t, :], in_=blk[0], func=AF.Copy, scale=col(beta)), reads=[pn[0], "beta"], writes=["cWvb%d_%d" % (ip, t)])
        P.op("act", lambda e: e.copy(out=WkT[ip][:, t, :], in_=blk[1]), reads=[pn[1]], writes=["cWkT%d_%d" % (ip, t)])
        yield

    def steps(b, h, d, ip, qi, tiles):
        si = ip
        ps6 = g.ps[6]; ps7 = g.ps[7]
        P.op("pool", lambda e: e.memset(Sst[si], 0.0), writes=["cS%d" % si])
        P.op("pool", lambda e: e.memset(Sbf[si], 0.0), writes=["cSbf%d" % si])
        yield
        for t in tiles:
            T = b * 18 + t
            col = lambda v, T=T: v[:, d, T, h:h + 1]
            sl = slice(t * 128, (t + 1) * 128)
            P.op("pe", lambda e, t=t: e.matmul(ps6[:, 0:128], lhsT=WkT[ip][:, t, :], rhs=Sbf[si], start=True, stop=True),
                 reads=["cWkT%d_%d" % (ip, t), "cSbf%d" % si], writes=["cp6_0"])
            P.op("pe", lambda e, sl=sl: e.matmul(ps7[:, 0:128], lhsT=QT[qi][:, sl], rhs=Sbf[si], start=True, stop=True), reads=["cQT%d" % qi, "cSbf%d" % si], writes=["cp7_1"])
            yield
            P.op("dve", lambda e, t=t, col=col: e.scalar_tensor_tensor(out=Ubf[si], in0=ps6[:, 0:128], scalar=col(nbeta), in1=Wvb[ip][:, t, :], op0=ALU.mult, op1=ALU.add),
                 reads=["cp6_0", "nbeta", "cWvb%d_%d" % (ip, t)], writes=["cU%d" % si])
            yield
            P.op("pe", lambda e, t=t: e.matmul(ps6[:, 256:384], lhsT=QKm[ip][:, t, :], rhs=Ubf[si], start=True, stop=True), reads=["cQKm%d_%d" % (ip, t), "cU%d" % si], writes=["cp6_2"])
            P.op("pe", lambda e, t=t: e.matmul(ps6[:, 384:512], lhsT=Ktl[ip][:, t, :], rhs=Ubf[si], start=True, stop=True), reads=["cKtl%d_%d" % (ip, t), "cU%d" % si], writes=["cp6_3"])
            yield
            P.op("act", lambda e: e.copy(out=O2s[si], in_=ps6[:, 256:384]), reads=["cp6_2"], writes=["cO2_%d" % si])
            P.op("dve", lambda e, col=col: e.scalar_tensor_tensor(out=Sst[si], in0=Sst[si], scalar=col(egt), in1=ps6[:, 384:512], op0=ALU.mult, op1=ALU.add),
                 reads=["cS%d" % si, "egt", "cp6_3"], writes=["cS%d" % si])
            yield
            P.op("act", lambda e: e.copy(out=Sbf[si], in_=Sst[si]), reads=["cS%d" % si], writes=["cSbf%d" % si])
            P.op("dve", lambda e, t=t, col=col: e.scalar_tensor_tensor(out=Od[d][:, t, :], in0=ps7[:, 0:128], scalar=col(egc), in1=O2s[si], op0=ALU.mult, op1=ALU.add),
                 reads=["cp7_1", "egc", "cO2_%d" % si], writes=["cOd%d_%d" % (d, t)])
            yield

    def finalize(b, h):
        odr = ["cOd%d_%d" % (d, t) for d in range(2) for t in range(18)]
        P.dma("sp", lambda e: e.dma_start(out=graw, in_=g.PT[12 + h][:, b * S:(b + 1) * S]), writes=["cgraw"])
        P.op("pool", lambda e: e.tensor_tensor(out=Od[0], in0=Od[0], in1=Od[1], op=ALU.add), reads=odr, writes=["cOsum"] + odr[:18])
        P.op("pool", lambda e: e.tensor_tensor(out=Od[1], in0=Od[0], in1=Od[0], op=ALU.mult), reads=["cOsum"], writes=["cOsq"] + odr[18:])
        P.op("dve", lambda e: e.reduce_sum(out=ssq, in_=Od[1], axis=AX.X), reads=["cOsq"], writes=["cssq"] + odr[18:])
        P.op("act", lambda e: e.activation(out=ssq, in_=ssq, func=AF.Ln, bias=EPS, scale=1.0 / 128), reads=["cssq"], writes=["cssq"])
        P.op("act", lambda e: e.activation(out=ssq, in_=ssq, func=AF.Exp, scale=-0.5), reads=["cssq"], writes=["cssq"])
        P.op("pool", lambda e: e.tensor_tensor(out=onb, in0=Od[0], in1=ssq.unsqueeze(2).to_broadcast([128, 18, 128]), op=ALU.mult),
             reads=["cOsum", "cssq"], writes=["conb"] + odr[:18])
        P.op("act", lambda e: e.activation(out=sg, in_=graw, func=AF.Silu), reads=["cgraw"], writes=["csg"])
        for g0 in range(0, 18, 8):
            n = min(8, 18 - g0)
            bank = state["ev"] % 4 + 4; state["ev"] += 1
            bank = 4 + (state["ev"] % 2)
            pb = g.ps[bank].bitcast(BF16)
            for t in range(n):
                P.op("pe", lambda e, pb=pb, t=t, g0=g0: e.transpose(pb[:, t * 128:(t + 1) * 128], onb[:, g0 + t, :], g.ident_bf),
                     reads=["conb"], writes=["cps%d" % bank])
            P.op("dve", lambda e, pb=pb, g0=g0, n=n: e.scalar_tensor_tensor(out=yaT[:, g0 * 128:(g0 + n) * 128], in0=pb[:, 0:n * 128], scalar=prm(g, "onorm_a"),
                                                                           in1=sg[:, g0 * 128:(g0 + n) * 128], op0=ALU.mult, op1=ALU.mult),
                 reads=["cps%d" % bank, "csg"], writes=["cyaT"])
        P.dma("sp", lambda e: e.dma_start(out=g.Y3[0, h][:, b * S:(b + 1) * S], in_=yaT), reads=["cyaT"], writes=["Y3"])

    def order(d):
        return list(range(18)) if d == 0 else [1, 0] + list(range(17, 1, -1))
    items = [(b, h, d) for b in range(NB) for h in range(4) for d in range(2)]
    NSLOT = 4
    chain_ctr = [0]
    for k in range(len(items) + 1):
        nxt = items[k] if k < len(items) else None
        cur = items[k - 1] if k >= 1 else None
        if nxt is not None and nxt[2] == 0:
            c1(nxt[0], nxt[1], (k // 2) % 2)
        pending = []
        if nxt is not None:
            for t in order(nxt[2]):
                pending.append((nxt[0], nxt[1], nxt[2], t, k % 2, (k // 2) % 2))
        active = []
        stepgen = steps(cur[0], cur[1], cur[2], (k - 1) % 2, ((k - 1) // 2) % 2, order(cur[2])) if cur is not None else None
        while pending or active or stepgen is not None:
            while pending and len(active) < NSLOT:
                used = [a_[1] for a_ in active]
                slot = [c_ for c_ in range(NSLOT) if c_ not in used][0]
                args = pending.pop(0)
                active.append((chain(*args, slot), slot))
            for a_ in list(active):
                try:
                    next(a_[0])
                except StopIteration:
                    active.remove(a_)
            if stepgen is not None:
                try:
                    next(stepgen)
                except StopIteration:
                    stepgen = None
        if cur is not None and cur[2] == 1:
            finalize(cur[0], cur[1])


NEG = -30000.0


def natten_pat(i):
    return 0 if i == 0 else 1 if i == 1 else 3 if i == 14 else 4 if i == 15 else 2


def phase_natten_rr(g, l, last):
    nc, P, A = g.nc, g.P, g.A
    P.barrier()
    A.reset()
    SCALE = 128.0 ** -0.5
    NS = 4
    bias = A.alloc("nbias", [5, 640])
    qT = [A.alloc("nq%d" % i, [S], BF16) for i in range(2)]
    kT = [A.alloc("nk%d" % i, [S], BF16) for i in range(2)]
    vt = [A.alloc("nv%d" % i, [18, 128], BF16) for i in range(2)]
    gr = [A.alloc("ng%d" % i, [S], BF16) for i in range(2)]
    sgt = [A.alloc("nsg%d" % i, [S]) for i in range(2)]
    yc = [A.alloc("nyc%d" % i, [S], BF16) for i in range(2)]
    Sb = [A.alloc("nSb%d" % i, [896]) for i in range(NS)]
    Pe = [A.alloc("nPe%d" % i, [896]) for i in range(NS)]
    Pn = [A.alloc("nPn%d" % i, [896], BF16) for i in range(NS)]
    PTs = [A.alloc("nPT%d" % i, [7, 128], BF16) for i in range(NS)]
    st = [A.alloc("nst%d" % i, [4]) for i in range(NS)]
    items = [(h, b) for h in range(4) for b in range(NB)]
    tiles = list(range(16)) + ([] if last else [16, 17])

    def load(k):
        h, b = items[k]
        i = k % 2
        P.dma("sp", lambda e: e.dma_start(out=qT[i], in_=g.PT[24 + h][:, b * S:(b + 1) * S]), writes=["nq%d" % i])
        P.dma("sp", lambda e: e.dma_start(out=kT[i], in_=g.PT[28 + h][:, b * S:(b + 1) * S]), writes=["nk%d" % i])
        P.dma("sp", lambda e: e.dma_start(out=gr[i], in_=g.PT[36 + h][:, b * S:(b + 1) * S]), writes=["ng%d" % i])
        P.dma("sp", lambda e: e.dma_start(out=vt[i], in_=g.VC[b * S:(b + 1) * S, h * 128:(h + 1) * 128].rearrange("(t p) d -> p t d", p=128)),
              writes=["nv%d" % i])
        P.op("act", lambda e, i=i: e.activation(out=sgt[i], in_=gr[i], func=AF.Silu), reads=["ng%d" % i], writes=["nsg%d" % i])

    def store(k):
        h, b = items[k]
        i = k % 2
        if last:
            P.dma("sp", lambda e: e.dma_start(out=g.Y3[2, h][:, b * S + CTX:(b + 1) * S], in_=yc[i][:, CTX:S]), reads=["nyc%d_%d" % (i, t_) for t_ in tiles], writes=["Y3"])
        else:
            P.dma("sp", lambda e: e.dma_start(out=g.Y3[2, h][:, b * S:(b + 1) * S], in_=yc[i]), reads=["nyc%d_%d" % (i, t_) for t_ in tiles], writes=["Y3"])

    def tile(k, ti, j):
        h, b = items[k]
        i = k % 2
        ctxq = ti >= 16
        if not ctxq:
            q0 = CTX + ti * 128
            sr = min(max(2 * ti - 4, 0), 22)
            k0 = CTX + sr * 64
            nk = 896
            vtiles = [2 + sr // 2 + m for m in range(5)] + [0, 1]
        else:
            q0 = (ti - 16) * 128
            nk = 256
            vtiles = [0, 1]
        bk = 2 * j
        base = bk * 512
        rd = ["nq%d" % i, "nk%d" % i]
        sname = "nS%d" % j
        swr = [sname, "nPTp%d" % j, "no%d" % j]
        if not ctxq:
            P.op("pe", lambda e: e.matmul(g.psall[:, base:base + 512], lhsT=qT[i][:, q0:q0 + 128], rhs=kT[i][:, k0:k0 + 512], start=True, stop=True), reads=rd, writes=swr)
            P.op("pe", lambda e: e.matmul(g.psall[:, base + 512:base + 640], lhsT=qT[i][:, q0:q0 + 128], rhs=kT[i][:, k0 + 512:k0 + 640], start=True, stop=True), reads=rd, writes=swr)
            P.op("pe", lambda e: e.matmul(g.psall[:, base + 640:base + 896], lhsT=qT[i][:, q0:q0 + 128], rhs=kT[i][:, 0:CTX], start=True, stop=True), reads=rd, writes=swr)
            yield
            pat = natten_pat(ti)
            P.op("dve", lambda e: e.scalar_tensor_tensor(out=Sb[j][:, 0:640], in0=g.psall[:, base:base + 640], scalar=SCALE, in1=bias[:, pat, :],
                                                         op0=ALU.mult, op1=ALU.add), reads=[sname, "nbias"], writes=["nSb%d" % j])
            P.op("dve", lambda e: e.tensor_scalar_mul(out=Sb[j][:, 640:896], in0=g.psall[:, base + 640:base + 896], scalar1=SCALE), reads=[sname], writes=["nSb%d" % j])
        else:
            P.op("pe", lambda e: e.matmul(g.psall[:, base:base + 256], lhsT=qT[i][:, q0:q0 + 128], rhs=kT[i][:, 0:CTX], start=True, stop=True), reads=rd, writes=swr)
            yield
            P.op("dve", lambda e: e.tensor_scalar_mul(out=Sb[j][:, 0:256], in0=g.psall[:, base:base + 256], scalar1=SCALE), reads=[sname], writes=["nSb%d" % j])
        sj = st[j]
        P.op("pool", lambda e: e.memset(sj[:, 2:3], 0.0), writes=["nst%d" % j])
        yield
        P.op("dve", lambda e: e.reduce_max(out=sj[:, 0:1], in_=Sb[j][:, 0:nk], axis=AX.X), reads=["nSb%d" % j], writes=["nst%d" % j])
        yield
        P.op("dve", lambda e: e.tensor_scalar_mul(out=sj[:, 1:2], in0=sj[:, 0:1], scalar1=-1.0), reads=["nst%d" % j], writes=["nst%d" % j])
        yield
        P.op("act", lambda e: e.activation(out=Pe[j][:, 0:nk], in_=Sb[j][:, 0:nk], func=AF.Exp, bias=sj[:, 1:2], scale=1.0,
                                           accum_out=sj[:, 2:3]), reads=["nSb%d" % j, "nst%d" % j], writes=["nPe%d" % j, "nst%d" % j])
        yield
        P.op("dve", lambda e: e.reciprocal(out=sj[:, 3:4], in_=sj[:, 2:3]), reads=["nst%d" % j], writes=["nst%d" % j])
        yield
        P.op("act", lambda e: e.activation(out=Pn[j][:, 0:nk], in_=Pe[j][:, 0:nk], func=AF.Copy, scale=sj[:, 3:4]),
             reads=["nPe%d" % j, "nst%d" % j], writes=["nPn%d" % j])
        yield
        nkt = nk // 128
        ptp = g.ps[bk + 1].bitcast(BF16)
        for kt in range(nkt):
            P.op("pe", lambda e, kt=kt: e.transpose(ptp[:, kt * 128:(kt + 1) * 128], Pn[j][:, kt * 128:(kt + 1) * 128], g.ident_bf),
                 reads=["nPn%d" % j], writes=["nPTp%d" % j, sname])
        yield
        P.op("dve", lambda e: e.tensor_copy(out=PTs[j].rearrange("p a b -> p (a b)")[:, 0:nk], in_=ptp[:, 0:nk]), reads=["nPTp%d" % j], writes=["nPT%d" % j])
        yield
        ops = g.ps[bk][:, 0:128]
        for kt in range(nkt):
            P.op("pe", lambda e, kt=kt: e.matmul(ops, lhsT=vt[i][:, vtiles[kt], :], rhs=PTs[j][:, kt, :], start=(kt == 0), stop=(kt == nkt - 1)),
                 reads=["nv%d" % i, "nPT%d" % j], writes=["no%d" % j, sname])
        yield
        P.op("dve", lambda e: e.tensor_tensor(out=yc[i][:, q0:q0 + 128], in0=ops, in1=sgt[i][:, q0:q0 + 128], op=ALU.mult),
             reads=["no%d" % j, "nsg%d" % i], writes=["nyc%d_%d" % (i, ti)])
        yield

    load(0)
    pending = [(k, ti) for k in range(len(items)) for ti in tiles]
    remaining = {k: len(tiles) for k in range(len(items))}
    started = set()
    active = []
    loaded = {0}
    while pending or active:
        while pending and len(active) < NS:
            k, ti = pending[0]
            if k not in loaded:
                if k >= 2 and remaining[k - 2] > 0:
                    break
                load(k); loaded.add(k)
            if k not in started:
                h, b = items[k]
                if b == 0:
                    if k >= 1 and remaining[k - 1] > 0:
                        break
                    P.dma("sp", lambda e, h=h: e.dma_start(out=bias, in_=g.rpbias[l, h].rearrange("p (a b) -> p a b", a=5)), writes=["nbias"])
                started.add(k)
            pending.pop(0)
            used = [a_[1] for a_ in active]
            slot = [c_ for c_ in range(NS) if c_ not in used][0]
            active.append((tile(k, ti, slot), slot, k))
        for a_ in list(active):
            try:
                next(a_[0])
            except StopIteration:
                active.remove(a_)
                kk = a_[2]
                remaining[kk] -= 1
                if remaining[kk] == 0:
                    store(kk)


def phase_natten(g, l, last):
    nc, P, A = g.nc, g.P, g.A
    P.barrier()
    A.reset()
    SCALE = 128.0 ** -0.5
    bias = A.alloc("nbias", [5, 640])
    qT = [A.alloc("nq%d" % i, [S], BF16) for i in range(2)]
    kT = [A.alloc("nk%d" % i, [S], BF16) for i in range(2)]
    vt = [A.alloc("nv%d" % i, [18, 128], BF16) for i in range(2)]
    gr = [A.alloc("ng%d" % i, [S], BF16) for i in range(2)]
    sgt = A.alloc("nsg", [S])
    yc = [A.alloc("nyc%d" % i, [S], BF16) for i in range(2)]
    Sb = [A.alloc("nSb%d" % i, [896]) for i in range(2)]
    Pe = [A.alloc("nPe%d" % i, [896]) for i in range(2)]
    Pn = [A.alloc("nPn%d" % i, [896], BF16) for i in range(2)]
    PTs = [A.alloc("nPT%d" % i, [7, 128], BF16) for i in range(2)]
    st = [A.alloc("nst%d" % i, [4]) for i in range(2)]
    items = [(h, b) for h in range(4) for b in range(NB)]

    def load(k):
        h, b = items[k]
        i = k % 2
        P.dma("sp", lambda e: e.dma_start(out=qT[i], in_=g.PT[24 + h][:, b * S:(b + 1) * S]), writes=["nq%d" % i])
        P.dma("sp", lambda e: e.dma_start(out=kT[i], in_=g.PT[28 + h][:, b * S:(b + 1) * S]), writes=["nk%d" % i])
        P.dma("sp", lambda e: e.dma_start(out=gr[i], in_=g.PT[36 + h][:, b * S:(b + 1) * S]), writes=["ng%d" % i])
        P.dma("sp", lambda e: e.dma_start(out=vt[i], in_=g.VC[b * S:(b + 1) * S, h * 128:(h + 1) * 128].rearrange("(t p) d -> p t d", p=128)),
              writes=["nv%d" % i])

    load(0)
    cnt = 0
    for k, (h, b) in enumerate(items):
        i = k % 2
        if b == 0:
            P.dma("sp", lambda e, h=h: e.dma_start(out=bias, in_=g.rpbias[l, h].rearrange("p (a b) -> p a b", a=5)), writes=["nbias"])
        if k + 1 < len(items):
            load(k + 1)
        P.op("act", lambda e, i=i: e.activation(out=sgt, in_=gr[i], func=AF.Silu), reads=["ng%d" % i], writes=["nsg"])
        tiles = list(range(16)) + ([] if last else [16, 17])
        for ti in tiles:
            j = cnt % 2; cnt += 1
            ctxq = ti >= 16
            if not ctxq:
                q0 = CTX + ti * 128
                sr = min(max(2 * ti - 4, 0), 22)
                k0 = CTX + sr * 64
                nk = 896
                vtiles = [2 + sr // 2 + m for m in range(5)] + [0, 1]
            else:
                q0 = (ti - 16) * 128
                nk = 256
                vtiles = [0, 1]
            bk = 2 * j
            Sps = g.psall[:, bk * 512: bk * 512 + nk]
            rd = ["nq%d" % i, "nk%d" % i]
            if not ctxq:
                P.op("pe", lambda e, i=i, q0=q0, k0=k0, bk=bk: e.matmul(g.psall[:, bk * 512:bk * 512 + 512], lhsT=qT[i][:, q0:q0 + 128],
                                                                       rhs=kT[i][:, k0:k0 + 512], start=True, stop=True), reads=rd, writes=["nS%d" % j])
                P.op("pe", lambda e, i=i, q0=q0, k0=k0, bk=bk: e.matmul(g.psall[:, bk * 512 + 512:bk * 512 + 640], lhsT=qT[i][:, q0:q0 + 128],
                                                                       rhs=kT[i][:, k0 + 512:k0 + 640], start=True, stop=True), reads=rd, writes=["nS%d" % j])
                P.op("pe", lambda e, i=i, q0=q0, bk=bk: e.matmul(g.psall[:, bk * 512 + 640:bk * 512 + 896], lhsT=qT[i][:, q0:q0 + 128],
                                                                rhs=kT[i][:, 0:CTX], start=True, stop=True), reads=rd, writes=["nS%d" % j])
                pat = natten_pat(ti)
                P.op("dve", lambda e, j=j, pat=pat, bk=bk: e.scalar_tensor_tensor(
                    out=Sb[j][:, 0:640], in0=g.psall[:, bk * 512:bk * 512 + 640], scalar=SCALE, in1=bias[:, pat, :],
                    op0=ALU.mult, op1=ALU.add), reads=["nS%d" % j, "nbias"], writes=["nSb%d" % j])
                P.op("act", lambda e, j=j, bk=bk: e.activation(out=Sb[j][:, 640:896], in_=g.psall[:, bk * 512 + 640:bk * 512 + 896],
                                                               func=AF.Copy, scale=SCALE), reads=["nS%d" % j], writes=["nSb%d" % j])
            else:
                P.op("pe", lambda e, i=i, q0=q0, bk=bk: e.matmul(g.psall[:, bk * 512:bk * 512 + 256], lhsT=qT[i][:, q0:q0 + 128],
                                                                rhs=kT[i][:, 0:CTX], start=True, stop=True), reads=rd, writes=["nS%d" % j])
                P.op("act", lambda e, j=j, bk=bk: e.activation(out=Sb[j][:, 0:256], in_=g.psall[:, bk * 512:bk * 512 + 256],
                                                               func=AF.Copy, scale=SCALE), reads=["nS%d" % j], writes=["nSb%d" % j])
            sj = st[j]
            P.op("dve", lambda e, j=j, nk=nk, sj=sj: e.reduce_max(out=sj[:, 0:1], in_=Sb[j][:, 0:nk], axis=AX.X), reads=["nSb%d" % j], writes=["nst%d" % j])
            P.op("dve", lambda e, sj=sj: e.tensor_scalar_mul(out=sj[:, 1:2], in0=sj[:, 0:1], scalar1=-1.0), reads=["nst%d" % j], writes=["nst%d" % j])
            P.op("pool", lambda e, sj=sj: e.memset(sj[:, 2:3], 0.0), writes=["nst%d" % j])
            P.op("act", lambda e, j=j, nk=nk, sj=sj: e.activation(out=Pe[j][:, 0:nk], in_=Sb[j][:, 0:nk], func=AF.Exp, bias=sj[:, 1:2], scale=1.0,
                                                                 accum_out=sj[:, 2:3]), reads=["nSb%d" % j, "nst%d" % j], writes=["nPe%d" % j, "nst%d" % j])
            P.op("dve", lambda e, sj=sj: e.reciprocal(out=sj[:, 3:4], in_=sj[:, 2:3]), reads=["nst%d" % j], writes=["nst%d" % j])
            P.op("act", lambda e, j=j, nk=nk, sj=sj: e.activation(out=Pn[j][:, 0:nk], in_=Pe[j][:, 0:nk], func=AF.Copy, scale=sj[:, 3:4]),
                 reads=["nPe%d" % j, "nst%d" % j], writes=["nPn%d" % j])
            nkt = nk // 128
            ptp = g.ps[4 + j].bitcast(BF16)
            for kt in range(nkt):
                P.op("pe", lambda e, j=j, kt=kt, ptp=ptp: e.transpose(ptp[:, kt * 128:(kt + 1) * 128], Pn[j][:, kt * 128:(kt + 1) * 128], g.ident_bf),
                     reads=["nPn%d" % j], writes=["nQTp%d" % j])
            evac(P, cnt, PTs[j].rearrange("p a b -> p (a b)")[:, 0:nk], ptp[:, 0:nk], ["nQTp%d" % j], ["nPT%d" % j])
            ops = g.ps[6 + j][:, 0:128]
            for kt in range(nkt):
                P.op("pe", lambda e, i=i, j=j, kt=kt, vti=vtiles[kt], ops=ops, nkt=nkt: e.matmul(
                    ops, lhsT=vt[i][:, vti, :], rhs=PTs[j][:, kt, :], start=(kt == 0), stop=(kt == nkt - 1)),
                    reads=["nv%d" % i, "nPT%d" % j], writes=["nqo%d" % j])
            P.op("dve", lambda e, i=i, q0=q0, ops=ops: e.tensor_tensor(out=yc[i][:, q0:q0 + 128], in0=ops, in1=sgt[:, q0:q0 + 128], op=ALU.mult),
                 reads=["nqo%d" % j, "nsg"], writes=["nyc%d" % i])
        if last:
            P.dma("sp", lambda e, i=i, h=h, b=b: e.dma_start(out=g.Y3[2, h][:, b * S + CTX:(b + 1) * S], in_=yc[i][:, CTX:S]), reads=["nyc%d" % i], writes=["Y3"])
        else:
            P.dma("sp", lambda e, i=i, h=h, b=b: e.dma_start(out=g.Y3[2, h][:, b * S:(b + 1) * S], in_=yc[i]), reads=["nyc%d" % i], writes=["Y3"])


def phase_merge(g, l, last):
    nc, P, A = g.nc, g.P, g.A
    P.barrier()
    A.reset()
    wbr = A.alloc("wbr", [12, D], BF16)
    wo = A.alloc("wo", [8, D], BF16)
    stg = [A.alloc("mstg%d" % i, [4, D]) for i in range(2)]
    srcs = [g.w_br[l, jb].rearrange("(kc p) n -> p kc n", p=128) for jb in range(3)] + \
           [g.w_out[l].rearrange("(kc p) n -> p kc n", p=128)[:, 0:4, :], g.w_out[l].rearrange("(kc p) n -> p kc n", p=128)[:, 4:8, :]]
    dsts = [wbr[:, 0:4, :], wbr[:, 4:8, :], wbr[:, 8:12, :], wo[:, 0:4, :], wo[:, 4:8, :]]
    for k in range(5):
        i = k % 2
        P.dma("sp", lambda e, k=k, i=i: e.dma_start(out=stg[i], in_=srcs[k]), writes=["mstg%d" % i])
        P.op("pool", lambda e, k=k, i=i: e.tensor_copy(out=dsts[k], in_=stg[i]), reads=["mstg%d" % i], writes=["mw%d" % k])
    wreads = ["mw%d" % k for k in range(5)]
    yin = [A.alloc("myin%d" % i, [12, 512], BF16) for i in range(2)]
    gl = [A.alloc("mgl%d" % i, [3, 512], BF16) for i in range(2)]
    sgm = [A.alloc("msg%d" % i, [3, 512]) for i in range(2)]
    tj = [A.alloc("mtj%d" % i, [3, 512]) for i in range(2)]
    mT = A.alloc("mT", [8, 512], BF16)
    yT = A.alloc("myT", [8, 512])
    ysq = A.alloc("mysq", [8, 512], BF16)
    xin = A.alloc("mxin", [8, 512])
    rs = A.alloc("mrs", [512])
    xsrc = (g.xT_in if l == 0 else g.xs).rearrange("(kc p) t -> p kc t", p=128)
    xdst = g.xs.rearrange("(kc p) t -> p kc t", p=128)
    odst = g.outT.rearrange("(kc p) t -> p kc t", p=128)
    NTT = NT // 512
    gsrc = g.PT[40:64].rearrange("(j t) p n -> t p j n", j=3)
    ev = 0

    def loady(T):
        i = T % 2
        P.dma("sp", lambda e: e.dma_start(out=yin[i], in_=g.Y3[:, :, :, T * 512:(T + 1) * 512].rearrange("j k p n -> p (j k) n")),
              reads=["Y3"], writes=["myin%d" % i])

    loady(0)
    gcnt = 0
    for T in range(NTT):
        i = T % 2
        if T + 1 < NTT:
            loady(T + 1)
        P.dma("sp", lambda e, T=T: e.dma_start(out=xin, in_=xsrc[:, :, T * 512:(T + 1) * 512]), reads=["xs"], writes=["mxin"])
        for dt in range(8):
            gi = gcnt % 2; gcnt += 1
            P.dma("sp", lambda e, dt=dt, T=T, gi=gi: e.dma_start(out=gl[gi], in_=gsrc[dt][:, :, T * 512:(T + 1) * 512]), writes=["mgl%d" % gi])
            P.op("act", lambda e, gi=gi: e.activation(out=sgm[gi], in_=gl[gi], func=AF.Sigmoid), reads=["mgl%d" % gi], writes=["msg%d" % gi])
            for jb in range(3):
                bank = (ev % 6); ev += 1
                ps = g.ps[bank]
                for kc in range(4):
                    P.op("pe", lambda e, ps=ps, jb=jb, kc=kc, dt=dt, i=i: e.matmul(
                        ps, lhsT=wbr[:, jb * 4 + kc, dt * 128:(dt + 1) * 128], rhs=yin[i][:, jb * 4 + kc, :], start=(kc == 0), stop=(kc == 3)),
                        reads=wreads + ["myin%d" % i], writes=["mps%d" % bank])
                P.op("dve", lambda e, ps=ps, jb=jb, gi=gi: e.tensor_tensor(out=tj[gi][:, jb, :], in0=ps, in1=sgm[gi][:, jb, :], op=ALU.mult),
                     reads=["mps%d" % bank, "msg%d" % gi], writes=["mtj%d_%d" % (gi, jb)])
            P.op("pool", lambda e, gi=gi: e.tensor_tensor(out=tj[gi][:, 0, :], in0=tj[gi][:, 0, :], in1=tj[gi][:, 1, :], op=ALU.add),
                 reads=["mtj%d_0" % gi, "mtj%d_1" % gi], writes=["mtj%d_0" % gi])
            P.op("pool", lambda e, gi=gi, dt=dt: e.tensor_tensor(out=mT[:, dt, :], in0=tj[gi][:, 0, :], in1=tj[gi][:, 2, :], op=ALU.add),
                 reads=["mtj%d_0" % gi, "mtj%d_2" % gi], writes=["mT%d" % dt])
        mreads = ["mT%d" % dt for dt in range(8)]
        for d2 in range(8):
            bank = (ev % 6); ev += 1
            ps = g.ps[bank]
            for kc in range(8):
                P.op("pe", lambda e, ps=ps, kc=kc, d2=d2: e.matmul(ps, lhsT=wo[:, kc, d2 * 128:(d2 + 1) * 128], rhs=mT[:, kc, :],
                                                                   start=(kc == 0), stop=(kc == 7)), reads=wreads + mreads, writes=["mps%d" % bank])
            P.op("act", lambda e, ps=ps, d2=d2: e.copy(out=yT[:, d2, :], in_=ps), reads=["mps%d" % bank], writes=["myT%d" % d2])
            P.op("act", lambda e, d2=d2: e.activation(out=ysq[:, d2, :], in_=yT[:, d2, :], func=AF.Square), reads=["myT%d" % d2], writes=["mysq%d" % d2])
        ssp = g.ps[6]
        for d2 in range(8):
            P.op("pe", lambda e, d2=d2: e.matmul(ssp, lhsT=g.ones_bf, rhs=ysq[:, d2, :], start=(d2 == 0), stop=(d2 == 7)),
                 reads=["mysq%d" % d2], writes=["mss"])
        rsqrt_act(P, rs, ssp, float(D * EPS), ["mss"], ["mrs"])
        for half in range(2):
            pc = 2 * T + half
            j = seg_j(pc)
            isctx = (pc % 9 == 0)
            if last and isctx:
                continue
            c0, c1 = half * 256, half * 256 + 256
            for d2 in range(8):
                P.op("dve", lambda e, d2=d2, j=j, c0=c0, c1=c1: e.scalar_tensor_tensor(
                    out=yT[:, d2, c0:c1], in0=yT[:, d2, c0:c1], scalar=g.modG[:, d2, j:j + 1], in1=rs[:, c0:c1], op0=ALU.mult, op1=ALU.mult),
                    reads=["myT%d" % d2, "mrs"], writes=["myT%d" % d2])
            P.op("pool", lambda e, c0=c0, c1=c1: e.tensor_tensor(out=xin[:, :, c0:c1], in0=xin[:, :, c0:c1], in1=yT[:, :, c0:c1], op=ALU.add),
                 reads=["mxin"] + ["myT%d" % d2 for d2 in range(8)], writes=["mxin"])
            if last:
                b = pc // 9
                q = pc % 9 - 1
                oc = b * LAT + q * 256
                P.dma("sp", lambda e, c0=c0, c1=c1, oc=oc: e.dma_start(out=odst[:, :, oc:oc + 256], in_=xin[:, :, c0:c1]), reads=["mxin"], writes=["out"])
            else:
                P.dma("sp", lambda e, c0=c0, c1=c1, pc=pc: e.dma_start(out=xdst[:, :, pc * 256:(pc + 1) * 256], in_=xin[:, :, c0:c1]),
                      reads=["mxin"], writes=["xs_w"])


def host_lruw(inp):
    out = np.zeros((DEPTH, 128, 16, 128), np.float32)
    for gi, name in enumerate(("lru_wa", "lru_wx")):
        w = inp[name]
        for d in range(2):
            for ct in range(4):
                for hb in range(2):
                    out[:, hb * 64:(hb + 1) * 64, gi * 8 + d * 4 + ct, hb * 64:(hb + 1) * 64] = w[:, d, ct * 2 + hb]
    return out.reshape(DEPTH, 128, 16 * 128)


def host_rpbias(inp):
    rpb = inp["rpb"]
    out = np.full((DEPTH, 4, 128, 5, 640), NEG, np.float32)
    cq = np.arange(64)
    kc = np.arange(64)
    win = np.clip(cq - 8, 0, 48)
    col_ok = (kc[None, :] >= win[:, None]) & (kc[None, :] < win[:, None] + 16)
    dc = np.clip(kc[None, :] - cq[:, None], -15, 15) + 15
    for pat, ti in enumerate((0, 1, 2, 14, 15)):
        sr = min(max(2 * ti - 4, 0), 22)
        for a in range(2):
            r = 2 * ti + a
            r0 = min(max(r - 4, 0), 24)
            for m in range(10):
                kr = sr + m
                if r0 <= kr < r0 + 8:
                    dr = kr - r + 7
                    vals = rpb[:, :, dr, :][:, :, dc]
                    blk = np.where(col_ok[None, None], vals, NEG)
                    out[:, :, a * 64:(a + 1) * 64, pat, m * 64:(m + 1) * 64] = blk
    return out.reshape(DEPTH, 4, 128, 5 * 640)


def prep_inputs(inp, nl=DEPTH):
    w_in = inp["w_in"]
    w_big = np.ascontiguousarray(np.concatenate([w_in[:, :, :2048], w_in[:, :, 2064:]], axis=2))
    w_small = np.ascontiguousarray(w_in[:, :, 2048:2064])
    prm = host_params(inp)
    cst = host_consts()
    lruw = host_lruw(inp)
    rpbias = host_rpbias(inp)
    rope = host_rope().reshape(128, 2 * LAT)
    maps = []
    nb_total = inp["x"].shape[0]
    for core in range(nb_total // NB):
        xs = []
        cs = []
        for b in range(core * NB, (core + 1) * NB):
            xs.append(inp["ctx"][b].T)
            xs.append(inp["x"][b].T)
            cs.append(inp["c"][b])
        cs.append(inp["c_ctx"])
        xT = np.ascontiguousarray(np.concatenate(xs, axis=1))
        cT = np.ascontiguousarray(np.stack(cs, axis=1).reshape(8, 128, 3).transpose(1, 0, 2).reshape(128, 24))
        maps.append({"xT": xT, "cT": cT, "w_mod": inp["w_mod"][:nl], "w_big": w_big[:nl], "w_small": w_small[:nl],
                     "prm": prm[:nl], "cst": cst, "rope": rope, "lruw": lruw[:nl], "rpbias": rpbias[:nl],
                     "w_br": inp["w_branch"][:nl], "w_out": inp["w_out"][:nl]})
    return maps


def kernel(**inputs):
    inp = {k: np.asarray(v) for k, v in inputs.items()}
    maps = prep_inputs(inp)
    nc = build()
    res = run_bass_kernel_spmd(nc, maps, core_ids=list(range(len(maps))))
    outs = []
    for r in res.results:
        oT = np.asarray(r["outT"])
        for b in range(NB):
            outs.append(oT[:, b * LAT:(b + 1) * LAT].T)
    return np.ascontiguousarray(np.stack(outs, axis=0)).astype(np.float32)
```

```python
import contextlib
import re
import numpy as np
import ml_dtypes
import concourse.bass as bass
import concourse.mybir as mybir
from concourse.bass_utils import run_bass_kernel_spmd

F32 = mybir.dt.float32
BF16 = mybir.dt.bfloat16
AF = mybir.ActivationFunctionType
ALU = mybir.AluOpType
AX = mybir.AxisListType

D = 1024
NB = 2
CTX = 256
LAT = 2048
S = CTX + LAT
NT = NB * S
DEPTH = 4
EPS = 1e-6
ENGS = ("pe", "act", "dve", "pool", "sp")
NDMASEM = 12


_BANK_RULES = [
    (re.compile(r"^modps$"), lambda m: [0]),
    (re.compile(r"^(?:ssps|pps|lps|mps|c0ps|cps)(\d)"), lambda m: [int(m.group(1))]),
    (re.compile(r"^nQTp(\d)"), lambda m: [4 + int(m.group(1))]),
    (re.compile(r"^nqo(\d)"), lambda m: [6 + int(m.group(1))]),
    (re.compile(r"^nS(\d)"), lambda m: [2 * int(m.group(1)), 2 * int(m.group(1)) + 1]),
    (re.compile(r"^nPTp(\d)"), lambda m: [2 * int(m.group(1)) + 1]),
    (re.compile(r"^no(\d)"), lambda m: [2 * int(m.group(1))]),
    (re.compile(r"^mss$"), lambda m: [6]),
    (re.compile(r"^cpA(\d)_"), lambda m: [int(m.group(1))]),
    (re.compile(r"^cpB(\d)_"), lambda m: [2 * int(m.group(1)) + 1]),
    (re.compile(r"^cp6_"), lambda m: [6]),
    (re.compile(r"^cp7_"), lambda m: [7]),
]
_BANK_CACHE = {}


def banks_of(name):
    b = _BANK_CACHE.get(name)
    if b is None:
        b = []
        for rx, fn in _BANK_RULES:
            m = rx.match(name)
            if m:
                b = fn(m)
                break
        _BANK_CACHE[name] = b
    return b


class Res:
    __slots__ = ("w", "r")

    def __init__(self):
        self.w = None
        self.r = {}


class Prog:
    def __init__(self, nc):
        self.nc = nc
        self.ops = {e: [] for e in ENGS}
        self.cnt = {e: 0 for e in ENGS}
        self.dcnt = {e: 0 for e in ENGS}
        self.seen = {e: {} for e in ENGS}
        self.pending = {e: [] for e in ENGS}
        self.res = {}
        self.inflight = {e: {} for e in ENGS}

    def R(self, name):
        r = self.res.get(name)
        if r is None:
            r = self.res[name] = Res()
        return r

    def _need(self, eng, tok, waits):
        if tok is None:
            return
        if tok[0] == "c":
            if tok[1] == "pe" and eng == "pe":
                return
            key = ("c", tok[1]); val = tok[2]
        else:
            key = ("d", tok[1], tok[2]); val = tok[3]
        if self.seen[eng].get(key, 0) >= val:
            return
        self.seen[eng][key] = val
        waits.append((key, val))

    def _deps(self, eng, reads, writes, waits):
        for t in self.pending[eng]:
            self._need(eng, t, waits)
        self.pending[eng] = []
        banks = set()
        for r in reads:
            banks.update(banks_of(r))
        for r in writes:
            banks.update(banks_of(r))
        self._banks = banks
        for k in banks:
            r = self.R("BANK%d" % k)
            if r.w is not None and r.w[1] != eng:
                self._need(eng, r.w, waits)
        for r in reads:
            self._need(eng, self.R(r).w, waits)
        for r in writes:
            r = self.R(r)
            self._need(eng, r.w, waits)
            for t in r.r.values():
                self._need(eng, t, waits)

    def _commit(self, tok, reads, writes):
        key = tok[:2] if tok[0] == "c" else tok[:3]
        for k in self._banks:
            self.R("BANK%d" % k).w = tok
        for r in reads:
            self.R(r).r[key] = tok
        for r in writes:
            r = self.R(r)
            r.w = tok
            r.r = {}

    limit = None
    nrec = 0

    def op(self, eng, fn, reads=(), writes=()):
        if Prog.limit is not None:
            Prog.nrec += 1
            if Prog.nrec > Prog.limit:
                return None
        waits = []
        self._deps(eng, reads, writes, waits)
        self.cnt[eng] += 1
        tok = ("c", eng, self.cnt[eng])
        self.ops[eng].append((waits, fn, tok))
        self._commit(tok, reads, writes)
        return tok

    def dma(self, eng, fn, reads=(), writes=()):
        if Prog.limit is not None:
            Prog.nrec += 1
            if Prog.nrec > Prog.limit:
                return None
        waits = []
        self._deps(eng, reads, writes, waits)
        i = self.dcnt[eng]
        self.dcnt[eng] += 1
        slot = i % NDMASEM
        val = 16 * (i // NDMASEM + 1)
        if i >= NDMASEM:
            self._need(eng, ("d", eng, slot, val - 16), waits)
        tok = ("d", eng, slot, val)
        self.inflight[eng][slot] = tok
        self.ops[eng].append((waits, fn, tok))
        self._commit(tok, reads, writes)
        return tok

    def barrier(self):
        toks = []
        for e in ("pe", "act", "dve", "pool"):
            if self.cnt[e]:
                toks.append(("c", e, self.cnt[e]))
        for e in ENGS:
            toks.extend(self.inflight[e].values())
        for e in ENGS:
            self.pending[e] = list(toks)
        self.res = {}

    def emit(self, final_tokens=()):
        nc = self.nc
        with contextlib.ExitStack() as st:
            csem = {e: st.enter_context(nc.semaphore("c_" + e)) for e in ("pe", "act", "dve", "pool")}
            dsem = {}
            for e in ENGS:
                for s in range(min(NDMASEM, self.dcnt[e])):
                    dsem[(e, s)] = st.enter_context(nc.semaphore("d_%s_%d" % (e, s)))
            fw = []
            for t in self.pending["sp"]:
                self._need("sp", t, fw)
            for t in final_tokens:
                self._need("sp", t, fw)
            block = st.enter_context(nc.Block())

            def semof(key):
                return csem[key[1]] if key[0] == "c" else dsem[(key[1], key[2])]

            def replay(ename):
                def body(e):
                    for waits, fn, tok in self.ops[ename]:
                        for key, val in waits:
                            e.wait_ge(semof(key), val)
                        ins = fn(e)
                        if tok[0] == "c":
                            ins.then_inc(csem[tok[1]], 1)
                        else:
                            ins.then_inc(dsem[(tok[1], tok[2])], 16)
                    if ename == "sp":
                        for key, val in fw:
                            e.wait_ge(semof(key), val)
                return body

            block.tensor(replay("pe"))
            block.scalar(replay("act"))
            block.vector(replay("dve"))
            block.gpsimd(replay("pool"))
            block.sync(replay("sp"))


class Arena:
    def __init__(self, ap, nbytes):
        self.ap = ap
        self.nbytes = nbytes
        self.off = 0
        self.names = {}

    def reset(self):
        self.off = 0
        self.names = {}

    def alloc(self, name, shape, dt=F32):
        esz = 4 if dt == F32 else 2
        n = int(np.prod(shape))
        nb = (n * esz + 63) // 64 * 64
        assert self.off + nb <= self.nbytes, "SBUF arena overflow at %s: %d + %d > %d" % (name, self.off, nb, self.nbytes)
        v = self.ap[:, self.off // 4:(self.off + nb) // 4]
        self.off += nb
        if dt != F32:
            v = v.bitcast(dt)
        v = v[:, 0:n]
        if len(shape) == 2:
            v = v.rearrange("p (a b) -> p a b", a=shape[0])
        elif len(shape) == 3:
            v = v.rearrange("p (a b c) -> p a b c", a=shape[0], b=shape[1])
        self.names[name] = v
        return v


class Ctx:
    pass


LASTP = None


def build(nlayers=DEPTH, dbg=(), run="CDEF"):
    nc = bass.Bass("TRN2", target_bir_lowering=False)
    g = Ctx()
    g.run = run
    g.nc = nc
    g.dbg = dbg

    def din(name, shape, dt=F32):
        return nc.dram_tensor(name, list(shape), dt, kind="ExternalInput").ap()

    def dscr(name, shape, dt=F32):
        kind = "ExternalOutput" if name in dbg else "ExternalInput" if (name + "_in") in dbg else "Internal"
        return nc.dram_tensor(name, list(shape), dt, kind=kind).ap()

    g.xT_in = din("xT", [D, NT])
    g.cT = din("cT", [128, 8 * 3])
    ND = nlayers
    g.w_mod = din("w_mod", [ND, D, 3 * D])
    g.w_big = din("w_big", [ND, D, 8192])
    g.w_small = din("w_small", [ND, D, 16])
    g.prm = din("prm", [ND, 128, PRM_N])
    g.lruw = din("lruw", [ND, 128, 16 * 128])
    g.rpbias = din("rpbias", [ND, 4, 128, 5 * 640])
    g.w_br = din("w_br", [ND, 3, 512, D])
    g.w_out = din("w_out", [ND, D, D])
    g.cst = din("cst", [128, CST_N])
    g.rope = din("rope", [128, 2 * LAT])
    g.outT = nc.dram_tensor("outT", [D, NB * LAT], F32, kind="ExternalOutput").ap()

    g.xs = dscr("xs", [D, NT])
    g.PT = dscr("PT", [64, 128, NT], BF16)
    g.VC = dscr("VC", [NT, 512], BF16)
    g.BA = dscr("BA", [NT, 16])
    g.Y3 = dscr("Y3", [3, 4, 128, NT], BF16)

    with contextlib.ExitStack() as st:
        arena_t = st.enter_context(nc.sbuf_tensor("arena", [128, ARENA_BYTES // 4], F32))
        pers_t = st.enter_context(nc.sbuf_tensor("pers", [128, PERS_BYTES // 4], F32))
        g.psall = st.enter_context(nc.psum_tensor("psall", [128, 4096], F32))
        g.ps = [g.psall[:, i * 512:(i + 1) * 512] for i in range(8)]
        g.A = Arena(arena_t[:], ARENA_BYTES)
        g.PA = Arena(pers_t[:], PERS_BYTES)
        P = g.P = Prog(nc)
        global LASTP
        LASTP = P
        phase_setup(g)
        for l in range(nlayers):
            last = (l == DEPTH - 1)
            phase_mod(g, l)
            if "skipB" not in dbg:
                phase_norm_proj(g, l)
            if "stopB" in dbg:
                break
            if "C" in g.run:
                phase_gdn(g, l, last)
            if "D" in g.run:
                phase_lru(g, l)
            if "E" in g.run:
                phase_natten(g, l, last)
            if "F" in g.run:
                phase_merge(g, l, last)
        P.barrier()
        P.emit()
    return nc


def _cst_layout():
    lay = {}
    off = 0
    names = ["ident", "ones", "tri0", "tri1", "mit0", "mit1", "mst0", "mst1"] + ["lm%d" % i for i in range(7)] + ["prot"]
    for name, n in [(nm, 128) for nm in names]:
        lay[name] = (off, n)
        off += n
    return lay, off


CST_LAY, CST_N = _cst_layout()


def _prm_layout():
    lay = {}
    off = 0
    for name, n in (("bmodT", 24), ("g_pre", 8), ("g_post", 8), ("conv_b", 16), ("conv_b_bias", 4),
                    ("lru_ba", 8), ("lru_bx", 8), ("lru_lam", 8), ("conv_a", 48), ("onorm_a", 1),
                    ("dt_bias", 8), ("a_log", 8)):
        lay[name] = (off, n)
        off += n
    return lay, off


PRM_LAY, PRM_N = _prm_layout()

ARENA_BYTES = 190 * 1024
PERS_BYTES = 16 * 1024


def host_consts():
    c = np.zeros((128, CST_N), np.float32)
    o, n = CST_LAY["ident"]; c[:, o:o + n] = np.eye(128, dtype=np.float32)
    o, n = CST_LAY["ones"]; c[:, o:o + n] = 1.0
    p = np.arange(128)[:, None]; f = np.arange(128)[None, :]

    def put(name, m):
        o, n = CST_LAY[name]; c[:, o:o + n] = m.astype(np.float32)
    put("tri0", p <= f); put("tri1", p >= f)
    put("mit0", f >= p); put("mit1", f <= p)
    put("mst0", f > p); put("mst1", f < p)
    for lv in range(7):
        put("lm%d" % lv, ((p >> (lv + 1)) == (f >> (lv + 1))) & ((p >> lv) != (f >> lv)))
    pr = np.zeros((128, 128), np.float32)
    for m in range(128):
        if (m % 64) < 32:
            pr[m + 32, m] = -1.0
        else:
            pr[m - 32, m] = 1.0
    put("prot", pr)
    return c


def host_rope():
    t = np.arange(LAT)
    rows = (t // 64).astype(np.float32); cols = (t % 64).astype(np.float32)
    inv = (10000.0 ** (-np.arange(32, dtype=np.float32) / 32)).astype(np.float32)
    out = np.zeros((128, 2, LAT), np.float32)
    for p in range(128):
        pos = rows if p < 64 else cols
        ang = (pos * inv[p % 32]).astype(np.float32)
        out[p, 0] = np.cos(ang); out[p, 1] = np.sin(ang)
    return out


def host_params(inp):
    p = np.zeros((DEPTH, 128, PRM_N), np.float32)
    for l in range(DEPTH):
        def put(name, arr):
            o, n = PRM_LAY[name]
            p[l, :, o:o + n] = arr.reshape(128, n)
        put("bmodT", inp["b_mod"][l].reshape(24, 128).T)
        put("g_pre", inp["g_pre"][l].reshape(8, 128).T)
        put("g_post", inp["g_post"][l].reshape(8, 128).T)
        put("conv_b", inp["conv_b"][l].reshape(4, 4, 128).transpose(2, 1, 0))
        put("conv_b_bias", inp["conv_b_bias"][l].reshape(4, 128).T)
        put("lru_ba", inp["lru_ba"][l].reshape(2, 4, 128).transpose(2, 0, 1))
        put("lru_bx", inp["lru_bx"][l].reshape(2, 4, 128).transpose(2, 0, 1))
        put("lru_lam", inp["lru_lam"][l].reshape(2, 4, 128).transpose(2, 0, 1))
        put("conv_a", inp["conv_a"][l].reshape(4, 12, 128).transpose(2, 1, 0))
        put("onorm_a", inp["onorm_a"][l].reshape(128, 1))
        put("dt_bias", np.broadcast_to(inp["dt_bias"][l].reshape(1, 8), (128, 8)))
        put("a_log", np.broadcast_to(inp["a_log"][l].reshape(1, 8), (128, 8)))
    return p


def rsqrt_act(P, out, in_, bias, reads, writes):
    P.op("act", lambda e: e.activation(out=out, in_=in_, func=AF.Ln, bias=bias, scale=1.0), reads=reads, writes=writes)
    P.op("act", lambda e: e.activation(out=out, in_=out, func=AF.Exp, scale=-0.5), reads=writes, writes=writes)


def phase_setup(g):
    nc, P, PA = g.nc, g.P, g.PA
    g.cst_sb = PA.alloc("cst", [CST_N])
    P.dma("sp", lambda e: e.dma_start(out=g.cst_sb, in_=g.cst[:, :]), writes=["cst"])
    g.cT_sb = PA.alloc("cT", [8, 3])
    P.dma("sp", lambda e: e.dma_start(out=g.cT_sb, in_=g.cT.rearrange("p (a b) -> p a b", a=8)), writes=["cT"])
    g.sc = PA.alloc("sc", [8, 3])
    P.op("act", lambda e: e.activation(out=g.sc, in_=g.cT_sb, func=AF.Silu), reads=["cT"], writes=["sc"])
    o, n = CST_LAY["ident"]; g.ident = g.cst_sb[:, o:o + n]
    o, n = CST_LAY["ones"]; g.ones = g.cst_sb[:, o:o + n]
    g.ident_bf = PA.alloc("ident_bf", [128], BF16)
    g.ones_bf = PA.alloc("ones_bf", [128], BF16)
    P.op("dve", lambda e: e.tensor_copy(out=g.ident_bf, in_=g.ident), reads=["cst"], writes=["ident_bf"])
    P.op("dve", lambda e: e.tensor_copy(out=g.ones_bf, in_=g.ones), reads=["cst"], writes=["ones_bf"])
    g.prm_sb = PA.alloc("prm", [PRM_N])
    g.mod = PA.alloc("mod", [24, 3])
    g.modA = PA.alloc("modA", [8, 3])
    g.modG = PA.alloc("modG", [8, 3])
    g.gpre32 = PA.alloc("gpre32", [8])
    g.gpost32 = PA.alloc("gpost32", [8])
    P.barrier()


def prm(g, name):
    o, n = PRM_LAY[name]
    return g.prm_sb[:, o:o + n]


def phase_mod(g, l):
    nc, P, A = g.nc, g.P, g.A
    P.barrier()
    A.reset()
    P.dma("sp", lambda e: e.dma_start(out=g.prm_sb, in_=g.prm[l, :, :]), writes=["prm"])
    wv = g.w_mod[l].rearrange("(kc p) n -> p kc n", p=128)
    wbuf = [A.alloc("wm%d" % i, [8, 512]) for i in range(2)]
    ps = g.ps[0][:, 0:72]
    for grp in range(6):
        wb = wbuf[grp % 2]
        P.dma("sp", lambda e, wb=wb, grp=grp: e.dma_start(out=wb, in_=wv[:, :, grp * 512:(grp + 1) * 512]),
              writes=["wm%d" % (grp % 2)])
        for ci in range(4):
            ct = grp * 4 + ci
            for kc in range(8):
                P.op("pe", lambda e, wb=wb, ci=ci, kc=kc, ct=ct: e.matmul(
                    ps[:, ct * 3:(ct + 1) * 3], lhsT=wb[:, kc, ci * 128:(ci + 1) * 128], rhs=g.sc[:, kc, :],
                    start=(kc == 0), stop=(kc == 7)),
                    reads=["wm%d" % (grp % 2), "sc"], writes=["modps"])
    ps3 = ps.rearrange("p (a b) -> p a b", a=24)
    bm = prm(g, "bmodT").unsqueeze(2).to_broadcast([128, 24, 3])
    P.op("dve", lambda e: e.tensor_tensor(out=g.mod, in0=ps3, in1=bm, op=ALU.add), reads=["modps", "prm"], writes=["mod"])
    P.op("dve", lambda e: e.tensor_scalar_mul(out=g.gpre32, in0=prm(g, "g_pre"), scalar1=32.0), reads=["prm"], writes=["gpre32"])
    P.op("dve", lambda e: e.tensor_scalar_mul(out=g.gpost32, in0=prm(g, "g_post"), scalar1=32.0), reads=["prm"], writes=["gpost32"])
    P.op("dve", lambda e: e.scalar_tensor_tensor(out=g.modA, in0=g.mod[:, 8:16, :], scalar=1.0,
                                                 in1=g.gpre32.unsqueeze(2).to_broadcast([128, 8, 3]),
                                                 op0=ALU.add, op1=ALU.mult), reads=["mod", "gpre32"], writes=["modA"])
    P.op("dve", lambda e: e.tensor_tensor(out=g.modG, in0=g.mod[:, 16:24, :],
                                          in1=g.gpost32.unsqueeze(2).to_broadcast([128, 8, 3]), op=ALU.mult),
         reads=["mod", "gpost32"], writes=["modG"])


def seg_j(pc):
    return 2 if pc % 9 == 0 else pc // 9


def phase_norm_proj(g, l):
    nc, P, A = g.nc, g.P, g.A
    P.barrier()
    A.reset()
    xsrc = (g.xT_in if l == 0 else g.xs).rearrange("(kc p) t -> p kc t", p=128)
    hT = A.alloc("hT", [8, NT], BF16)
    xp = [A.alloc("xp%d" % i, [8, 256]) for i in range(2)]
    sq = [A.alloc("sq%d" % i, [8, 256], BF16) for i in range(2)]
    xn = [A.alloc("xn%d" % i, [8, 256]) for i in range(2)]
    rstd = [A.alloc("rstd%d" % i, [256]) for i in range(2)]
    NPC = NT // 256

    def load(pc):
        i = pc % 2
        P.dma("sp", lambda e: e.dma_start(out=xp[i], in_=xsrc[:, :, pc * 256:(pc + 1) * 256]), writes=["xp%d" % i])

    load(0)
    for pc in range(NPC):
        i = pc % 2
        j = seg_j(pc)
        if pc + 1 < NPC:
            load(pc + 1)
        P.op("act", lambda e, i=i: e.activation(out=sq[i], in_=xp[i], func=AF.Square), reads=["xp%d" % i], writes=["sq%d" % i])
        ps = g.ps[i][:, 0:256]
        for kc in range(8):
            P.op("pe", lambda e, i=i, kc=kc, ps=ps: e.matmul(ps, lhsT=g.ones_bf, rhs=sq[i][:, kc, :], start=(kc == 0), stop=(kc == 7)),
                 reads=["sq%d" % i, "ones_bf"], writes=["ssps%d" % i])
        rsqrt_act(P, rstd[i], ps, float(D * EPS), ["ssps%d" % i], ["rstd%d" % i])
        P.op("dve", lambda e, i=i: e.tensor_tensor(out=xn[i], in0=xp[i], in1=rstd[i].unsqueeze(1).to_broadcast([128, 8, 256]),
                                                  op=ALU.mult), reads=["xp%d" % i, "rstd%d" % i], writes=["xn%d" % i])
        for kc in range(8):
            P.op("act", lambda e, i=i, kc=kc, j=j, pc=pc: e.activation(
                out=hT[:, kc, pc * 256:(pc + 1) * 256], in_=xn[i][:, kc, :], func=AF.Identity,
                bias=g.mod[:, kc, j:j + 1], scale=g.modA[:, kc, j:j + 1]),
                reads=["xn%d" % i, "mod", "modA"], writes=["hT%d" % pc])

    wsrc = g.w_big[l].rearrange("(kc p) n -> p kc n", p=128)
    wf = [A.alloc("wf%d" % i, [8, 512]) for i in range(2)]
    wb = [A.alloc("wb%d" % i, [8, 512], BF16) for i in range(2)]
    ost = [A.alloc("ost%d" % i, [512], BF16) for i in range(8)]
    ws_f = A.alloc("ws_f", [8, 16])
    ws_b = A.alloc("ws_b", [8, 16], BF16)
    osm = [A.alloc("osm%d" % i, [16]) for i in range(4)]
    NG = 16
    hreads = ["hT%d" % pc for pc in range(NPC)]

    def loadw(gi):
        i = gi % 2
        P.dma("sp", lambda e: e.dma_start(out=wf[i], in_=wsrc[:, :, gi * 512:(gi + 1) * 512]), writes=["wf%d" % i])

    loadw(0)
    P.dma("sp", lambda e: e.dma_start(out=ws_f, in_=g.w_small[l].rearrange("(kc p) n -> p kc n", p=128)), writes=["ws_f"])
    P.op("pool", lambda e: e.tensor_copy(out=ws_b, in_=ws_f), reads=["ws_f"], writes=["ws_b"])
    ev = 0
    for gi in range(NG):
        i = gi % 2
        if gi + 1 < NG:
            loadw(gi + 1)
        P.op("pool", lambda e, i=i: e.tensor_copy(out=wb[i], in_=wf[i]), reads=["wf%d" % i], writes=["wb%d" % i])
        if gi == 8:
            for tt in range(NT // 128):
                bank = ev % 8
                ps = g.ps[bank]
                for kc in range(8):
                    P.op("pe", lambda e, kc=kc, tt=tt, ps=ps, i=i: e.matmul(
                        ps, lhsT=hT[:, kc, tt * 128:(tt + 1) * 128], rhs=wb[i][:, kc, :], start=(kc == 0), stop=(kc == 7)),
                        reads=["wb%d" % i, "hT%d" % (tt // 2)], writes=["pps%d" % bank])
                o = ost[bank]
                eng = "act" if ev % 2 == 0 else "dve"
                if eng == "act":
                    P.op("act", lambda e, o=o, ps=ps: e.copy(out=o, in_=ps), reads=["pps%d" % bank], writes=["ost%d" % bank])
                else:
                    P.op("dve", lambda e, o=o, ps=ps: e.tensor_copy(out=o, in_=ps), reads=["pps%d" % bank], writes=["ost%d" % bank])
                P.dma("sp", lambda e, o=o, tt=tt: e.dma_start(out=g.VC[tt * 128:(tt + 1) * 128, :], in_=o),
                      reads=["ost%d" % bank], writes=["VC"])
                ev += 1
            continue
        for T in range(NT // 512):
            for ci in range(4):
                ct = gi * 4 + ci
                bank = ev % 8
                ps = g.ps[bank]
                for kc in range(8):
                    P.op("pe", lambda e, kc=kc, T=T, ps=ps, i=i, ci=ci: e.matmul(
                        ps, lhsT=wb[i][:, kc, ci * 128:(ci + 1) * 128], rhs=hT[:, kc, T * 512:(T + 1) * 512],
                        start=(kc == 0), stop=(kc == 7)),
                        reads=["wb%d" % i, "hT%d" % (2 * T), "hT%d" % (2 * T + 1)], writes=["pps%d" % bank])
                o = ost[bank]
                if ev % 2 == 0:
                    P.op("act", lambda e, o=o, ps=ps: e.copy(out=o, in_=ps), reads=["pps%d" % bank], writes=["ost%d" % bank])
                else:
                    P.op("dve", lambda e, o=o, ps=ps: e.tensor_copy(out=o, in_=ps), reads=["pps%d" % bank], writes=["ost%d" % bank])
                P.dma("sp", lambda e, o=o, ct=ct, T=T: e.dma_start(out=g.PT[ct, :, T * 512:(T + 1) * 512], in_=o),
                      reads=["ost%d" % bank], writes=["PT"])
                ev += 1
    for tt in range(NT // 128):
        bank = ev % 8
        ps = g.ps[bank][:, 0:16]
        for kc in range(8):
            P.op("pe", lambda e, kc=kc, tt=tt, ps=ps: e.matmul(
                ps, lhsT=hT[:, kc, tt * 128:(tt + 1) * 128], rhs=ws_b[:, kc, :], start=(kc == 0), stop=(kc == 7)),
                reads=["ws_b", "hT%d" % (tt // 2)], writes=["pps%d" % bank])
        o = osm[tt % 4]
        P.op("dve", lambda e, o=o, ps=ps: e.tensor_copy(out=o, in_=ps), reads=["pps%d" % bank], writes=["osm%d" % (tt % 4)])
        P.dma("sp", lambda e, o=o, tt=tt: e.dma_start(out=g.BA[tt * 128:(tt + 1) * 128, :], in_=o),
              reads=["osm%d" % (tt % 4)], writes=["BA"])
        ev += 1


def evac(P, k, out, in_, reads, writes):
    if k % 2 == 0:
        P.op("act", lambda e: e.copy(out=out, in_=in_), reads=reads, writes=writes)
    else:
        P.op("dve", lambda e: e.tensor_copy(out=out, in_=in_), reads=reads, writes=writes)


PADW = 2312
PC0, PL0 = 2, 261


def load_padded(P, buf, src, name):
    P.dma("sp", lambda e: e.dma_start(out=buf[:, PC0:PC0 + CTX], in_=src[:, 0:CTX]), writes=[name])
    P.dma("sp", lambda e: e.dma_start(out=buf[:, PL0:PL0 + LAT], in_=src[:, CTX:S]), writes=[name])


def store_padded(P, dst, buf, name, wname):
    P.dma("sp", lambda e: e.dma_start(out=dst[:, 0:CTX], in_=buf[:, PC0:PC0 + CTX]), reads=[name], writes=[wname])
    P.dma("sp", lambda e: e.dma_start(out=dst[:, CTX:S], in_=buf[:, PL0:PL0 + LAT]), reads=[name], writes=[wname])


def conv4(P, eng, out, buf, w, bias, reads, wname, tmp=None):
    n = PADW - 5
    o = out[:, 2:2 + n]
    if bias is not None:
        P.op(eng, lambda e: e.tensor_scalar(out=o, in0=buf[:, 0:n], scalar1=w[:, 0:1], scalar2=bias, op0=ALU.mult, op1=ALU.add),
             reads=reads, writes=[wname])
    else:
        P.op(eng, lambda e: e.tensor_scalar_mul(out=o, in0=buf[:, 0:n], scalar1=w[:, 0:1]), reads=reads, writes=[wname])
    for j in range(1, 4):
        if eng == "dve":
            P.op(eng, lambda e, j=j: e.scalar_tensor_tensor(out=o, in0=buf[:, j:j + n], scalar=w[:, j:j + 1], in1=o,
                                                             op0=ALU.mult, op1=ALU.add), reads=reads + [wname], writes=[wname])
        else:
            t = tmp[:, 2:2 + n]
            P.op(eng, lambda e, j=j, t=t: e.tensor_scalar_mul(out=t, in0=buf[:, j:j + n], scalar1=w[:, j:j + 1]), reads=reads, writes=[wname + "_t"])
            P.op(eng, lambda e, t=t: e.tensor_tensor(out=o, in0=o, in1=t, op=ALU.add), reads=[wname, wname + "_t"], writes=[wname])


def phase_lru(g, l):
    nc, P, A = g.nc, g.P, g.A
    P.barrier()
    A.reset()
    wst = A.alloc("lruw_f", [16, 128])
    wbd = A.alloc("lruw_b", [16, 128], BF16)
    P.dma("sp", lambda e: e.dma_start(out=wst, in_=g.lruw[l].rearrange("p (a b) -> p a b", a=16)), writes=["lruw_f"])
    P.op("pool", lambda e: e.tensor_copy(out=wbd, in_=wst), reads=["lruw_f"], writes=["lruw_b"])
    sp = A.alloc("sp", [8]); sc8 = A.alloc("sc8", [8]); sc16 = A.alloc("sc16", [8])
    P.op("act", lambda e: e.activation(out=sp, in_=prm(g, "lru_lam"), func=AF.Exp, scale=-1.0), reads=[], writes=["sp"])
    P.op("act", lambda e: e.activation(out=sp, in_=sp, func=AF.Ln, bias=1.0, scale=1.0), reads=["sp"], writes=["sp"])
    P.op("dve", lambda e: e.tensor_scalar_mul(out=sc8, in0=sp, scalar1=-8.0), reads=["sp"], writes=["sc8"])
    P.op("dve", lambda e: e.tensor_scalar_mul(out=sc16, in0=sp, scalar1=-16.0), reads=["sp"], writes=["sc16"])
    nbuf = 2
    xraw = [A.alloc("xraw%d" % i, [PADW], BF16) for i in range(nbuf)]
    graw = [A.alloc("graw%d" % i, [PADW], BF16) for i in range(nbuf)]
    for i in range(nbuf):
        P.op("pool", lambda e, i=i: e.memset(xraw[i], 0.0), writes=["xraw%d" % i])
        P.op("pool", lambda e, i=i: e.memset(graw[i], 0.0), writes=["graw%d" % i])
    xc = A.alloc("xc", [PADW]); xcb = A.alloc("xcb", [PADW], BF16)
    r = A.alloc("r", [PADW]); ig = A.alloc("ig", [PADW]); av = A.alloc("av", [PADW]); sv = A.alloc("sv", [PADW])
    hh = [A.alloc("hh%d" % i, [PADW]) for i in range(2)]
    sg = A.alloc("sg", [PADW]); yb = [A.alloc("yb%d" % i, [PADW], BF16) for i in range(2)]
    for v, nm in ((xc, "xc"), (r, "r"), (ig, "ig"), (av, "av"), (sv, "sv"), (hh[0], "hh0"), (hh[1], "hh1")):
        P.op("pool", lambda e, v=v: e.memset(v, 0.0), writes=[nm])
    cw = prm(g, "conv_b").rearrange("p (a b) -> p a b", a=4)
    cb = prm(g, "conv_b_bias")
    ba = prm(g, "lru_ba").rearrange("p (a b) -> p a b", a=2)
    bx = prm(g, "lru_bx").rearrange("p (a b) -> p a b", a=2)
    sc8v = sc8.rearrange("p (a b) -> p a b", a=2); sc16v = sc16.rearrange("p (a b) -> p a b", a=2)
    items = [(b, ct) for b in range(NB) for ct in range(4)]

    def load(k):
        b, ct = items[k]
        i = k % nbuf
        load_padded(P, xraw[i], g.PT[16 + ct][:, b * S:(b + 1) * S], "xraw%d" % i)
        load_padded(P, graw[i], g.PT[20 + ct][:, b * S:(b + 1) * S], "graw%d" % i)

    load(0)
    N0, N1 = 2, PADW - 3
    chunks = [(c0, min(c0 + 512, N1)) for c0 in range(N0, N1, 512)]
    ev = 0
    for k, (b, ct) in enumerate(items):
        i = k % nbuf
        if k + 1 < len(items):
            load(k + 1)
        conv4(P, "dve", xc, xraw[i], cw[:, ct, :], cb[:, ct:ct + 1], ["xraw%d" % i], "xc")
        P.op("act", lambda e: e.copy(out=xcb[:, N0:N1], in_=xc[:, N0:N1]), reads=["xc"], writes=["xcb"])
        P.op("act", lambda e, i=i: e.activation(out=sg[:, N0:N1], in_=graw[i][:, N0:N1], func=AF.Silu), reads=["graw%d" % i], writes=["sg"])
        for d in range(2):
            for gi, (dst, bias, nm) in enumerate(((r, ba, "r"), (ig, bx, "ig"))):
                for (c0, c1) in chunks:
                    bank = ev % 8; ev += 1
                    ps = g.ps[bank][:, 0:c1 - c0]
                    P.op("pe", lambda e, ps=ps, gi=gi, d=d, ct=ct, c0=c0, c1=c1: e.matmul(
                        ps, lhsT=wbd[:, gi * 8 + d * 4 + ct, :], rhs=xcb[:, c0:c1], start=True, stop=True),
                        reads=["lruw_b", "xcb"], writes=["lps%d" % bank])
                    P.op("act", lambda e, ps=ps, dst=dst, bias=bias, d=d, ct=ct, c0=c0, c1=c1: e.activation(
                        out=dst[:, c0:c1], in_=ps, func=AF.Sigmoid, bias=bias[:, d, ct:ct + 1], scale=1.0),
                        reads=["lps%d" % bank], writes=[nm])
            P.op("act", lambda e, d=d, ct=ct: e.activation(out=av[:, N0:N1], in_=r[:, N0:N1], func=AF.Exp, scale=sc8v[:, d, ct:ct + 1]),
                 reads=["r", "sc8"], writes=["av"])
            P.op("act", lambda e, d=d, ct=ct: e.activation(out=sv[:, N0:N1], in_=r[:, N0:N1], func=AF.Exp, scale=sc16v[:, d, ct:ct + 1]),
                 reads=["r", "sc16"], writes=["sv"])
            P.op("act", lambda e: e.activation(out=sv[:, N0:N1], in_=sv[:, N0:N1], func=AF.Sqrt, bias=1.0, scale=-1.0),
                 reads=["sv"], writes=["sv"])
            P.op("dve", lambda e: e.tensor_tensor(out=ig[:, N0:N1], in0=ig[:, N0:N1], in1=xc[:, N0:N1], op=ALU.mult), reads=["ig", "xc"], writes=["ig"])
            P.op("dve", lambda e: e.tensor_tensor(out=sv[:, N0:N1], in0=sv[:, N0:N1], in1=ig[:, N0:N1], op=ALU.mult), reads=["ig", "sv"], writes=["sv"])
            h = hh[d]
            if d == 0:
                P.op("dve", lambda e, h=h: e.tensor_tensor_scan(out=h[:, PC0:PC0 + CTX], data0=av[:, PC0:PC0 + CTX], data1=sv[:, PC0:PC0 + CTX],
                                                                initial=0.0, op0=ALU.mult, op1=ALU.add), reads=["av", "sv"], writes=["hh0"])
                P.op("dve", lambda e, h=h: e.tensor_tensor_scan(out=h[:, PL0:PL0 + LAT], data0=av[:, PL0:PL0 + LAT], data1=sv[:, PL0:PL0 + LAT],
                                                                initial=h[:, PC0 + CTX - 1:PC0 + CTX], op0=ALU.mult, op1=ALU.add),
                     reads=["av", "sv", "hh0"], writes=["hh0"])
            else:
                def rv(t, a, n):
                    return t[:, a + n - 1:a - 1:-1] if a > 0 else t[:, a + n - 1::-1]
                P.op("dve", lambda e, h=h: e.tensor_tensor_scan(out=rv(h, PC0, CTX), data0=rv(av, PC0, CTX), data1=rv(sv, PC0, CTX),
                                                                initial=0.0, op0=ALU.mult, op1=ALU.add), reads=["av", "sv"], writes=["hh1"])
                P.op("dve", lambda e, h=h: e.tensor_tensor_scan(out=rv(h, PL0, LAT), data0=rv(av, PL0, LAT), data1=rv(sv, PL0, LAT),
                                                                initial=h[:, PC0:PC0 + 1], op0=ALU.mult, op1=ALU.add),
                     reads=["av", "sv", "hh1"], writes=["hh1"])
        P.op("pool", lambda e: e.tensor_tensor(out=hh[0][:, N0:N1], in0=hh[0][:, N0:N1], in1=hh[1][:, N0:N1], op=ALU.add),
             reads=["hh0", "hh1"], writes=["hh0"])
        P.op("pool", lambda e, i=i: e.tensor_tensor(out=yb[i][:, N0:N1], in0=hh[0][:, N0:N1], in1=sg[:, N0:N1], op=ALU.mult),
             reads=["hh0", "sg"], writes=["yb%d" % i])
        store_padded(P, g.Y3[1, ct][:, b * S:(b + 1) * S], yb[i], "yb%d" % i, "Y3")


def cst(g, name):
    o, n = CST_LAY[name]
    return g.cst_sb[:, o:o + n]


def phase_gdn(g, l, last):
    nc, P, A = g.nc, g.P, g.A
    P.barrier()
    A.reset()
    import os
    if os.environ.get("GDN_LIMIT"):
        Prog.limit = int(os.environ["GDN_LIMIT"]); Prog.nrec = 0
    NTL = NT // 128
    ba = A.alloc("ba", [NTL, 16])
    bav = g.BA.rearrange("(t p) c -> p t c", p=128)
    for t0 in range(0, NTL, 6):
        P.dma("sp", lambda e, t0=t0: e.dma_start(out=ba[:, t0:t0 + 6, :], in_=bav[:, t0:t0 + 6, :]), writes=["ba"])

    def sc4(name):
        return A.alloc(name, [2, NTL, 4])
    beta = sc4("beta"); nbeta = sc4("nbeta"); gl = sc4("gl"); gc = sc4("gc"); egc = sc4("egc"); egl = sc4("egl"); egt = sc4("egt")
    nea = A.alloc("nea", [8])

    def asdth(v):
        return v.rearrange("p t (d h) -> p d t h", d=2)
    P.op("act", lambda e: e.activation(out=beta, in_=asdth(ba[:, :, 0:8]), func=AF.Sigmoid), reads=["ba"], writes=["beta"])
    P.op("dve", lambda e: e.tensor_scalar_mul(out=nbeta, in0=beta, scalar1=-1.0), reads=["beta"], writes=["nbeta"])
    dtb = prm(g, "dt_bias").rearrange("p (d h) -> p d h", d=2).unsqueeze(2).to_broadcast([128, 2, NTL, 4])
    P.op("dve", lambda e: e.tensor_tensor(out=gl, in0=asdth(ba[:, :, 8:16]), in1=dtb, op=ALU.add), reads=["ba"], writes=["gl"])
    P.op("act", lambda e: e.activation(out=gl, in_=gl, func=AF.Exp), reads=["gl"], writes=["gl"])
    P.op("act", lambda e: e.activation(out=gl, in_=gl, func=AF.Ln, bias=1.0, scale=1.0), reads=["gl"], writes=["gl"])
    P.op("act", lambda e: e.activation(out=nea, in_=prm(g, "a_log"), func=AF.Exp), reads=[], writes=["nea"])
    P.op("dve", lambda e: e.tensor_scalar_mul(out=nea, in0=nea, scalar1=-1.0), reads=["nea"], writes=["nea"])
    neab = nea.rearrange("p (d h) -> p d h", d=2).unsqueeze(2).to_broadcast([128, 2, NTL, 4])
    P.op("dve", lambda e: e.tensor_tensor(out=gl, in0=gl, in1=neab, op=ALU.mult), reads=["gl", "nea"], writes=["gl"])
    for d in range(2):
        ps = g.ps[d][:, 0:NTL * 4]
        P.op("pe", lambda e, d=d, ps=ps: e.matmul(ps, lhsT=cst(g, "tri%d" % d), rhs=gl[:, d].rearrange("p t h -> p (t h)"), start=True, stop=True),
             reads=["gl"], writes=["c0ps%d" % d])
        P.op("dve", lambda e, d=d, ps=ps: e.tensor_copy(out=gc[:, d].rearrange("p t h -> p (t h)"), in_=ps), reads=["c0ps%d" % d], writes=["gc"])
    ps = g.ps[2][:, 0:2 * NTL * 4]
    P.op("pe", lambda e: e.matmul(ps, lhsT=g.ones, rhs=gl.rearrange("p d t h -> p (d t h)"), start=True, stop=True), reads=["gl"], writes=["c0ps2"])
    flat = lambda v: v.rearrange("p d t h -> p (d t h)")
    P.op("act", lambda e: e.activation(out=flat(egt), in_=ps, func=AF.Exp), reads=["c0ps2"], writes=["egt"])
    P.op("dve", lambda e: e.tensor_tensor(out=flat(egl), in0=ps, in1=flat(gc), op=ALU.subtract), reads=["c0ps2", "gc"], writes=["egl"])
    P.op("act", lambda e: e.activation(out=egl, in_=egl, func=AF.Exp), reads=["egl"], writes=["egl"])
    P.op("act", lambda e: e.activation(out=egc, in_=gc, func=AF.Exp), reads=["gc"], writes=["egc"])

    P.barrier()
    if "gstop0" in g.dbg:
        return
    rope = A.alloc("rope", [2, LAT])
    P.dma("sp", lambda e: e.dma_start(out=rope, in_=g.rope.rearrange("p (a b) -> p a b", a=2)), writes=["rope"])
    raw = A.alloc("craw", [PADW], BF16)
    P.op("pool", lambda e: e.memset(raw, 0.0), writes=["craw"])
    acc = A.alloc("cacc", [PADW])
    P.op("pool", lambda e: e.memset(acc, 0.0), writes=["cacc"])
    sqb = A.alloc("csqb", [PADW], BF16)
    rinv = [A.alloc("crinv%d" % i, [512]) for i in range(2)]
    t1 = [A.alloc("ct1_%d" % i, [512]) for i in range(2)]
    QT = [A.alloc("cQT%d" % i, [S], BF16) for i in range(2)]
    KT = A.alloc("cKT", [S], BF16)
    VT = A.alloc("cVT", [S], BF16)
    Ktok = A.alloc("cKtok", [18, 128], BF16)
    Vtok = A.alloc("cVtok", [18, 128], BF16)
    Od = [A.alloc("cOd%d" % i, [18, 128]) for i in range(2)]
    graw = A.alloc("cgraw", [S], BF16)
    sg = A.alloc("csg", [S])
    onb = A.alloc("conb", [18, 128], BF16)
    ssq = A.alloc("cssq", [18])
    yaT = A.alloc("cyaT", [S], BF16)
    Sst = [A.alloc("cS%d" % i, [128]) for i in range(2)]
    Sbf = [A.alloc("cSbf%d" % i, [128], BF16) for i in range(2)]
    Ubf = [A.alloc("cU%d" % i, [128], BF16) for i in range(2)]
    O2s = [A.alloc("cO2_%d" % i, [128]) for i in range(2)]
    WkT = [A.alloc("cWkT%d" % i, [18, 128], BF16) for i in range(2)]
    Wvb = [A.alloc("cWvb%d" % i, [18, 128]) for i in range(2)]
    QKm = [A.alloc("cQKm%d" % i, [18, 128], BF16) for i in range(2)]
    Ktl = [A.alloc("cKtl%d" % i, [18, 128], BF16) for i in range(2)]
    def two(name, shape, dt=F32):
        return [A.alloc("%s%d" % (name, i), shape, dt) for i in range(4)]
    Gb = two("cGb", [128]); Em = two("cE", [128]); Dms = two("cDms", [128]); Dmi = two("cDmi", [128])
    ATp = two("cATp", [128], BF16); L0 = two("cL0_", [128], BF16); Tm = two("cT", [128], BF16); TT = two("cTT", [128], BF16)
    Yb = two("cYb", [128], BF16); ZTs = two("cZTs", [128], BF16); Xk = two("cXk", [128], BF16)
    cw = prm(g, "conv_a").rearrange("p (a b) -> p a b", a=12)
    N0, N1 = 2, PADW - 3
    chunks = [(c0, min(c0 + 512, N1)) for c0 in range(N0, N1, 512)]
    state = {"ev": 0, "ch": 0}

    def c1(b, h, qi):
        for which in range(3):
            ct = which * 4 + h
            load_padded(P, raw, g.PT[ct][:, b * S:(b + 1) * S], "craw")
            conv4(P, "dve", acc, raw, cw[:, ct, :], None, ["craw"], "cacc")
            P.op("act", lambda e: e.activation(out=acc[:, N0:N1], in_=acc[:, N0:N1], func=AF.Silu), reads=["cacc"], writes=["cacc"])
            if which == 2:
                P.op("act", lambda e: e.copy(out=VT[:, 0:CTX], in_=acc[:, PC0:PC0 + CTX]), reads=["cacc"], writes=["cVT"])
                P.op("act", lambda e: e.copy(out=VT[:, CTX:S], in_=acc[:, PL0:PL0 + LAT]), reads=["cacc"], writes=["cVT"])
                continue
            P.op("pool", lambda e: e.tensor_tensor(out=sqb[:, N0:N1], in0=acc[:, N0:N1], in1=acc[:, N0:N1], op=ALU.mult), reads=["cacc"], writes=["csqb"])
            for ci, (c0, c1_) in enumerate(chunks):
                bank = 4 + state["ev"] % 2; state["ev"] += 1
                ps = g.ps[bank][:, 0:c1_ - c0]
                ri = rinv[ci % 2][:, 0:c1_ - c0]
                P.op("pe", lambda e, ps=ps, c0=c0, c1_=c1_: e.matmul(ps, lhsT=g.ones_bf, rhs=sqb[:, c0:c1_], start=True, stop=True),
                     reads=["csqb"], writes=["cps%d" % bank])
                rsqrt_act(P, ri, ps, EPS, ["cps%d" % bank], ["crinv%d" % (ci % 2)])
                if which == 0:
                    P.op("dve", lambda e, ri=ri, c0=c0, c1_=c1_: e.scalar_tensor_tensor(out=acc[:, c0:c1_], in0=acc[:, c0:c1_], scalar=float(128.0 ** -0.5),
                                                                                      in1=ri, op0=ALU.mult, op1=ALU.mult),
                         reads=["cacc", "crinv%d" % (ci % 2)], writes=["cacc"])
                else:
                    P.op("dve", lambda e, ri=ri, c0=c0, c1_=c1_: e.tensor_tensor(out=acc[:, c0:c1_], in0=acc[:, c0:c1_], in1=ri, op=ALU.mult),
                         reads=["cacc", "crinv%d" % (ci % 2)], writes=["cacc"])
            for ci in range(4):
                bank = 4 + state["ev"] % 2; state["ev"] += 1
                ps = g.ps[bank]
                a0 = PL0 + ci * 512
                P.op("pe", lambda e, ps=ps, a0=a0: e.matmul(ps, lhsT=cst(g, "prot"), rhs=acc[:, a0:a0 + 512], start=True, stop=True),
                     reads=["cacc"], writes=["cps%d" % bank])
                tt = t1[ci % 2]
                P.op("dve", lambda e, ps=ps, tt=tt, ci=ci: e.tensor_tensor(out=tt, in0=ps, in1=rope[:, 1, ci * 512:(ci + 1) * 512], op=ALU.mult),
                     reads=["cps%d" % bank, "rope"], writes=["ct1_%d" % (ci % 2)])
                P.op("pool", lambda e, a0=a0, ci=ci: e.tensor_tensor(out=acc[:, a0:a0 + 512], in0=acc[:, a0:a0 + 512], in1=rope[:, 0, ci * 512:(ci + 1) * 512], op=ALU.mult),
                     reads=["cacc", "rope", "cps%d" % bank], writes=["cacc"])
                P.op("pool", lambda e, a0=a0, tt=tt: e.tensor_tensor(out=acc[:, a0:a0 + 512], in0=acc[:, a0:a0 + 512], in1=tt, op=ALU.add),
                     reads=["cacc", "ct1_%d" % (ci % 2)], writes=["cacc"])
            dst = QT[qi] if which == 0 else KT
            dn = ("cQT%d" % qi) if which == 0 else "cKT"
            P.op("act", lambda e, dst=dst: e.copy(out=dst[:, 0:CTX], in_=acc[:, PC0:PC0 + CTX]), reads=["cacc"], writes=[dn])
            P.op("act", lambda e, dst=dst: e.copy(out=dst[:, CTX:S], in_=acc[:, PL0:PL0 + LAT]), reads=["cacc"], writes=[dn])
        for src, dstt, sn, dn in ((KT, Ktok, "cKT", "cKtok"), (VT, Vtok, "cVT", "cVtok")):
            for g0 in range(0, 18, 8):
                n = min(8, 18 - g0)
                bank = 4 + state["ev"] % 2; state["ev"] += 1
                pb = g.ps[bank].bitcast(BF16)
                for t in range(n):
                    P.op("pe", lambda e, pb=pb, t=t, g0=g0, src=src: e.transpose(pb[:, t * 128:(t + 1) * 128], src[:, (g0 + t) * 128:(g0 + t + 1) * 128], g.ident_bf),
                         reads=[sn], writes=["cps%d" % bank])
                evac(P, state["ev"], dstt[:, g0:g0 + n, :].rearrange("p a b -> p (a b)"), pb[:, 0:n * 128], ["cps%d" % bank], [dn])

    def chain(b, h, d, t, ip, qi, c):
        T = b * 18 + t
        col = lambda v: v[:, d, T, h:h + 1]
        sl = slice(t * 128, (t + 1) * 128)
        ps = g.ps[c]
        blk = [ps[:, k * 128:(k + 1) * 128] for k in range(4)]
        pn = ["cpA%d_%d" % (c, k) for k in range(4)]
        P.op("pool", lambda e: e.tensor_copy(out=Gb[c], in_=col(gl).to_broadcast([128, 128])), reads=["gl"], writes=["cGb%d" % c])
        yield
        P.op("pe", lambda e: e.matmul(blk[0], lhsT=Gb[c], rhs=cst(g, "tri%d" % d), start=True, stop=True), reads=["cGb%d" % c], writes=[pn[0]])
        P.op("pe", lambda e: e.matmul(blk[1], lhsT=KT[:, sl], rhs=KT[:, sl], start=True, stop=True), reads=["cKT"], writes=[pn[1]])
        P.op("pe", lambda e: e.matmul(blk[2], lhsT=KT[:, sl], rhs=QT[qi][:, sl], start=True, stop=True), reads=["cKT", "cQT%d" % qi], writes=[pn[2]])
        yield
        P.op("dve", lambda e: e.scalar_tensor_tensor(out=Em[c], in0=blk[0], scalar=col(gc), in1=cst(g, "mit%d" % d), op0=ALU.subtract, op1=ALU.mult),
             reads=[pn[0], "gc"], writes=["cE%d" % c])
        yield
        P.op("act", lambda e: e.activation(out=Em[c], in_=Em[c], func=AF.Exp), reads=["cE%d" % c], writes=["cE%d" % c])
        P.op("act", lambda e: e.activation(out=Xk[c], in_=Ktok[:, t, :], func=AF.Copy, scale=col(egc)), reads=["cKtok", "egc"], writes=["cXk%d" % c])
        P.op("act", lambda e: e.activation(out=Ktl[ip][:, t, :], in_=Ktok[:, t, :], func=AF.Copy, scale=col(egl)), reads=["cKtok", "egl"], writes=["cKtl%d_%d" % (ip, t)])
        yield
        P.op("pool", lambda e: e.tensor_tensor(out=Dms[c], in0=Em[c], in1=cst(g, "mst%d" % d), op=ALU.mult), reads=["cE%d" % c], writes=["cDms%d" % c])
        P.op("pool", lambda e: e.tensor_tensor(out=Dmi[c], in0=Em[c], in1=cst(g, "mit%d" % d), op=ALU.mult), reads=["cE%d" % c], writes=["cDmi%d" % c])
        yield
        P.op("dve", lambda e: e.scalar_tensor_tensor(out=ATp[c], in0=blk[1], scalar=col(beta), in1=Dms[c], op0=ALU.mult, op1=ALU.mult),
             reads=[pn[1], "beta", "cDms%d" % c], writes=["cATp%d" % c])
        P.op("dve", lambda e: e.tensor_tensor(out=QKm[ip][:, t, :], in0=blk[2], in1=Dmi[c], op=ALU.mult),
             reads=[pn[2], "cDmi%d" % c], writes=["cQKm%d_%d" % (ip, t)])
        yield
        P.op("pool", lambda e: e.tensor_tensor(out=L0[c], in0=ATp[c], in1=cst(g, "lm0"), op=ALU.mult), reads=["cATp%d" % c], writes=["cL0_%d" % c])
        P.op("pool", lambda e: e.tensor_tensor(out=TT[c], in0=g.ident_bf, in1=L0[c], op=ALU.subtract), reads=["cL0_%d" % c], writes=["cTT%d" % c])
        yield
        pb3 = blk[3].bitcast(BF16)
        P.op("pe", lambda e: e.transpose(pb3[:, 0:128], L0[c], g.ident_bf), reads=["cL0_%d" % c], writes=[pn[3]])
        yield
        P.op("dve", lambda e: e.tensor_tensor(out=Tm[c], in0=g.ident_bf, in1=pb3[:, 0:128], op=ALU.subtract), reads=[pn[3]], writes=["cT%d" % c])
        yield
        for lv in range(1, 7):
            lastlv = (lv == 6)
            P.op("pe", lambda e: e.matmul(blk[0], lhsT=ATp[c], rhs=Tm[c], start=True, stop=True), reads=["cATp%d" % c, "cT%d" % c], writes=[pn[0]])
            yield
            P.op("dve", lambda e, lv=lv: e.tensor_tensor(out=Yb[c], in0=blk[0], in1=cst(g, "lm%d" % lv), op=ALU.mult), reads=[pn[0]], writes=["cYb%d" % c])
            yield
            if not lastlv:
                P.op("pe", lambda e: e.matmul(blk[1], lhsT=TT[c], rhs=Yb[c], start=True, stop=True), reads=["cTT%d" % c, "cYb%d" % c], writes=[pn[1]])
            P.op("pe", lambda e: e.matmul(blk[2], lhsT=Yb[c], rhs=TT[c], start=True, stop=True), reads=["cTT%d" % c, "cYb%d" % c], writes=[pn[2]])
            yield
            P.op("act", lambda e: e.copy(out=ZTs[c], in_=blk[2]), reads=[pn[2]], writes=["cZTs%d" % c])
            if not lastlv:
                P.op("dve", lambda e: e.tensor_tensor(out=Tm[c], in0=Tm[c], in1=blk[1], op=ALU.subtract), reads=["cT%d" % c, pn[1]], writes=["cT%d" % c])
            yield
            P.op("pool", lambda e: e.tensor_tensor(out=TT[c], in0=TT[c], in1=ZTs[c], op=ALU.subtract), reads=["cTT%d" % c, "cZTs%d" % c], writes=["cTT%d" % c])
            yield
        P.op("pe", lambda e: e.matmul(blk[0], lhsT=TT[c], rhs=Vtok[:, t, :], start=True, stop=True), reads=["cTT%d" % c, "cVtok"], writes=[pn[0]])
        P.op("pe", lambda e: e.matmul(blk[1], lhsT=Xk[c], rhs=TT[c], start=True, stop=True), reads=["cTT%d" % c, "cXk%d" % c], writes=[pn[1]])
        yield
        P.op("act", lambda e: e.activation(out=Wvb[ip][:, t, :], in_=blk[0], func=AF.Copy, scale=col(beta)), reads=[pn[0], "beta"], writes=["cWvb%d_%d" % (ip, t)])
        P.op("act", lambda e: e.copy(out=WkT[ip][:, t, :], in_=blk[1]), reads=[pn[1]], writes=["cWkT%d_%d" % (ip, t)])
        yield

    def steps(b, h, d, ip, qi, tiles):
        si = ip
        ps6 = g.ps[6]; ps7 = g.ps[7]
        P.op("pool", lambda e: e.memset(Sst[si], 0.0), writes=["cS%d" % si])
        P.op("pool", lambda e: e.memset(Sbf[si], 0.0), writes=["cSbf%d" % si])
        yield
        for t in tiles:
            T = b * 18 + t
            col = lambda v, T=T: v[:, d, T, h:h + 1]
            sl = slice(t * 128, (t + 1) * 128)
            P.op("pe", lambda e, t=t: e.matmul(ps6[:, 0:128], lhsT=WkT[ip][:, t, :], rhs=Sbf[si], start=True, stop=True),
                 reads=["cWkT%d_%d" % (ip, t), "cSbf%d" % si], writes=["cp6_0"])
            P.op("pe", lambda e, sl=sl: e.matmul(ps7[:, 0:128], lhsT=QT[qi][:, sl], rhs=Sbf[si], start=True, stop=True), reads=["cQT%d" % qi, "cSbf%d" % si], writes=["cp7_1"])
            yield
            P.op("dve", lambda e, t=t, col=col: e.scalar_tensor_tensor(out=Ubf[si], in0=ps6[:, 0:128], scalar=col(nbeta), in1=Wvb[ip][:, t, :], op0=ALU.mult, op1=ALU.add),
                 reads=["cp6_0", "nbeta", "cWvb%d_%d" % (ip, t)], writes=["cU%d" % si])
            yield
            P.op("pe", lambda e, t=t: e.matmul(ps6[:, 256:384], lhsT=QKm[ip][:, t, :], rhs=Ubf[si], start=True, stop=True), reads=["cQKm%d_%d" % (ip, t), "cU%d" % si], writes=["cp6_2"])
            P.op("pe", lambda e, t=t: e.matmul(ps6[:, 384:512], lhsT=Ktl[ip][:, t, :], rhs=Ubf[si], start=True, stop=True), reads=["cKtl%d_%d" % (ip, t), "cU%d" % si], writes=["cp6_3"])
            yield
            P.op("act", lambda e: e.copy(out=O2s[si], in_=ps6[:, 256:384]), reads=["cp6_2"], writes=["cO2_%d" % si])
            P.op("dve", lambda e, col=col: e.scalar_tensor_tensor(out=Sst[si], in0=Sst[si], scalar=col(egt), in1=ps6[:, 384:512], op0=ALU.mult, op1=ALU.add),
                 reads=["cS%d" % si, "egt", "cp6_3"], writes=["cS%d" % si])
            yield
            P.op("act", lambda e: e.copy(out=Sbf[si], in_=Sst[si]), reads=["cS%d" % si], writes=["cSbf%d" % si])
            P.op("dve", lambda e, t=t, col=col: e.scalar_tensor_tensor(out=Od[d][:, t, :], in0=ps7[:, 0:128], scalar=col(egc), in1=O2s[si], op0=ALU.mult, op1=ALU.add),
                 reads=["cp7_1", "egc", "cO2_%d" % si], writes=["cOd%d_%d" % (d, t)])
            yield

    def finalize(b, h):
        odr = ["cOd%d_%d" % (d, t) for d in range(2) for t in range(18)]
        P.dma("sp", lambda e: e.dma_start(out=graw, in_=g.PT[12 + h][:, b * S:(b + 1) * S]), writes=["cgraw"])
        P.op("pool", lambda e: e.tensor_tensor(out=Od[0], in0=Od[0], in1=Od[1], op=ALU.add), reads=odr, writes=["cOsum"] + odr[:18])
        P.op("pool", lambda e: e.tensor_tensor(out=Od[1], in0=Od[0], in1=Od[0], op=ALU.mult), reads=["cOsum"], writes=["cOsq"] + odr[18:])
        P.op("dve", lambda e: e.reduce_sum(out=ssq, in_=Od[1], axis=AX.X), reads=["cOsq"], writes=["cssq"] + odr[18:])
        P.op("act", lambda e: e.activation(out=ssq, in_=ssq, func=AF.Ln, bias=EPS, scale=1.0 / 128), reads=["cssq"], writes=["cssq"])
        P.op("act", lambda e: e.activation(out=ssq, in_=ssq, func=AF.Exp, scale=-0.5), reads=["cssq"], writes=["cssq"])
        P.op("pool", lambda e: e.tensor_tensor(out=onb, in0=Od[0], in1=ssq.unsqueeze(2).to_broadcast([128, 18, 128]), op=ALU.mult),
             reads=["cOsum", "cssq"], writes=["conb"] + odr[:18])
        P.op("act", lambda e: e.activation(out=sg, in_=graw, func=AF.Silu), reads=["cgraw"], writes=["csg"])
        for g0 in range(0, 18, 8):
            n = min(8, 18 - g0)
            bank = state["ev"] % 4 + 4; state["ev"] += 1
            bank = 4 + (state["ev"] % 2)
            pb = g.ps[bank].bitcast(BF16)
            for t in range(n):
                P.op("pe", lambda e, pb=pb, t=t, g0=g0: e.transpose(pb[:, t * 128:(t + 1) * 128], onb[:, g0 + t, :], g.ident_bf),
                     reads=["conb"], writes=["cps%d" % bank])
            P.op("dve", lambda e, pb=pb, g0=g0, n=n: e.scalar_tensor_tensor(out=yaT[:, g0 * 128:(g0 + n) * 128], in0=pb[:, 0:n * 128], scalar=prm(g, "onorm_a"),
                                                                           in1=sg[:, g0 * 128:(g0 + n) * 128], op0=ALU.mult, op1=ALU.mult),
                 reads=["cps%d" % bank, "csg"], writes=["cyaT"])
        P.dma("sp", lambda e: e.dma_start(out=g.Y3[0, h][:, b * S:(b + 1) * S], in_=yaT), reads=["cyaT"], writes=["Y3"])

    def order(d):
        return list(range(18)) if d == 0 else [1, 0] + list(range(17, 1, -1))
    items = [(b, h, d) for b in range(NB) for h in range(4) for d in range(2)]
    NSLOT = 4
    chain_ctr = [0]
    for k in range(len(items) + 1):
        nxt = items[k] if k < len(items) else None
        cur = items[k - 1] if k >= 1 else None
        if nxt is not None and nxt[2] == 0:
            c1(nxt[0], nxt[1], (k // 2) % 2)
        pending = []
        if nxt is not None:
            for t in order(nxt[2]):
                pending.append((nxt[0], nxt[1], nxt[2], t, k % 2, (k // 2) % 2))
        active = []
        stepgen = steps(cur[0], cur[1], cur[2], (k - 1) % 2, ((k - 1) // 2) % 2, order(cur[2])) if cur is not None else None
        while pending or active or stepgen is not None:
            while pending and len(active) < NSLOT:
                used = [a_[1] for a_ in active]
                slot = [c_ for c_ in range(NSLOT) if c_ not in used][0]
                args = pending.pop(0)
                active.append((chain(*args, slot), slot))
            for a_ in list(active):
                try:
                    next(a_[0])
                except StopIteration:
                    active.remove(a_)
            if stepgen is not None:
                try:
                    next(stepgen)
                except StopIteration:
                    stepgen = None
        if cur is not None and cur[2] == 1:
            finalize(cur[0], cur[1])


NEG = -30000.0


def natten_pat(i):
    return 0 if i == 0 else 1 if i == 1 else 3 if i == 14 else 4 if i == 15 else 2


def phase_natten_rr(g, l, last):
    nc, P, A = g.nc, g.P, g.A
    P.barrier()
    A.reset()
    SCALE = 128.0 ** -0.5
    NS = 4
    bias = A.alloc("nbias", [5, 640])
    qT = [A.alloc("nq%d" % i, [S], BF16) for i in range(2)]
    kT = [A.alloc("nk%d" % i, [S], BF16) for i in range(2)]
    vt = [A.alloc("nv%d" % i, [18, 128], BF16) for i in range(2)]
    gr = [A.alloc("ng%d" % i, [S], BF16) for i in range(2)]
    sgt = [A.alloc("nsg%d" % i, [S]) for i in range(2)]
    yc = [A.alloc("nyc%d" % i, [S], BF16) for i in range(2)]
    Sb = [A.alloc("nSb%d" % i, [896]) for i in range(NS)]
    Pe = [A.alloc("nPe%d" % i, [896]) for i in range(NS)]
    Pn = [A.alloc("nPn%d" % i, [896], BF16) for i in range(NS)]
    PTs = [A.alloc("nPT%d" % i, [7, 128], BF16) for i in range(NS)]
    st = [A.alloc("nst%d" % i, [4]) for i in range(NS)]
    items = [(h, b) for h in range(4) for b in range(NB)]
    tiles = list(range(16)) + ([] if last else [16, 17])

    def load(k):
        h, b = items[k]
        i = k % 2
        P.dma("sp", lambda e: e.dma_start(out=qT[i], in_=g.PT[24 + h][:, b * S:(b + 1) * S]), writes=["nq%d" % i])
        P.dma("sp", lambda e: e.dma_start(out=kT[i], in_=g.PT[28 + h][:, b * S:(b + 1) * S]), writes=["nk%d" % i])
        P.dma("sp", lambda e: e.dma_start(out=gr[i], in_=g.PT[36 + h][:, b * S:(b + 1) * S]), writes=["ng%d" % i])
        P.dma("sp", lambda e: e.dma_start(out=vt[i], in_=g.VC[b * S:(b + 1) * S, h * 128:(h + 1) * 128].rearrange("(t p) d -> p t d", p=128)),
              writes=["nv%d" % i])
        P.op("act", lambda e, i=i: e.activation(out=sgt[i], in_=gr[i], func=AF.Silu), reads=["ng%d" % i], writes=["nsg%d" % i])

    def store(k):
        h, b = items[k]
        i = k % 2
        if last:
            P.dma("sp", lambda e: e.dma_start(out=g.Y3[2, h][:, b * S + CTX:(b + 1) * S], in_=yc[i][:, CTX:S]), reads=["nyc%d_%d" % (i, t_) for t_ in tiles], writes=["Y3"])
        else:
            P.dma("sp", lambda e: e.dma_start(out=g.Y3[2, h][:, b * S:(b + 1) * S], in_=yc[i]), reads=["nyc%d_%d" % (i, t_) for t_ in tiles], writes=["Y3"])

    def tile(k, ti, j):
        h, b = items[k]
        i = k % 2
        ctxq = ti >= 16
        if not ctxq:
            q0 = CTX + ti * 128
            sr = min(max(2 * ti - 4, 0), 22)
            k0 = CTX + sr * 64
            nk = 896
            vtiles = [2 + sr // 2 + m for m in range(5)] + [0, 1]
        else:
            q0 = (ti - 16) * 128
            nk = 256
            vtiles = [0, 1]
        bk = 2 * j
        base = bk * 512
        rd = ["nq%d" % i, "nk%d" % i]
        sname = "nS%d" % j
        swr = [sname, "nPTp%d" % j, "no%d" % j]
        if not ctxq:
            P.op("pe", lambda e: e.matmul(g.psall[:, base:base + 512], lhsT=qT[i][:, q0:q0 + 128], rhs=kT[i][:, k0:k0 + 512], start=True, stop=True), reads=rd, writes=swr)
            P.op("pe", lambda e: e.matmul(g.psall[:, base + 512:base + 640], lhsT=qT[i][:, q0:q0 + 128], rhs=kT[i][:, k0 + 512:k0 + 640], start=True, stop=True), reads=rd, writes=swr)
            P.op("pe", lambda e: e.matmul(g.psall[:, base + 640:base + 896], lhsT=qT[i][:, q0:q0 + 128], rhs=kT[i][:, 0:CTX], start=True, stop=True), reads=rd, writes=swr)
            yield
            pat = natten_pat(ti)
            P.op("dve", lambda e: e.scalar_tensor_tensor(out=Sb[j][:, 0:640], in0=g.psall[:, base:base + 640], scalar=SCALE, in1=bias[:, pat, :],
                                                         op0=ALU.mult, op1=ALU.add), reads=[sname, "nbias"], writes=["nSb%d" % j])
            P.op("dve", lambda e: e.tensor_scalar_mul(out=Sb[j][:, 640:896], in0=g.psall[:, base + 640:base + 896], scalar1=SCALE), reads=[sname], writes=["nSb%d" % j])
        else:
            P.op("pe", lambda e: e.matmul(g.psall[:, base:base + 256], lhsT=qT[i][:, q0:q0 + 128], rhs=kT[i][:, 0:CTX], start=True, stop=True), reads=rd, writes=swr)
            yield
            P.op("dve", lambda e: e.tensor_scalar_mul(out=Sb[j][:, 0:256], in0=g.psall[:, base:base + 256], scalar1=SCALE), reads=[sname], writes=["nSb%d" % j])
        sj = st[j]
        P.op("pool", lambda e: e.memset(sj[:, 2:3], 0.0), writes=["nst%d" % j])
        yield
        P.op("dve", lambda e: e.reduce_max(out=sj[:, 0:1], in_=Sb[j][:, 0:nk], axis=AX.X), reads=["nSb%d" % j], writes=["nst%d" % j])
        yield
        P.op("dve", lambda e: e.tensor_scalar_mul(out=sj[:, 1:2], in0=sj[:, 0:1], scalar1=-1.0), reads=["nst%d" % j], writes=["nst%d" % j])
        yield
        P.op("act", lambda e: e.activation(out=Pe[j][:, 0:nk], in_=Sb[j][:, 0:nk], func=AF.Exp, bias=sj[:, 1:2], scale=1.0,
                                           accum_out=sj[:, 2:3]), reads=["nSb%d" % j, "nst%d" % j], writes=["nPe%d" % j, "nst%d" % j])
        yield
        P.op("dve", lambda e: e.reciprocal(out=sj[:, 3:4], in_=sj[:, 2:3]), reads=["nst%d" % j], writes=["nst%d" % j])
        yield
        P.op("act", lambda e: e.activation(out=Pn[j][:, 0:nk], in_=Pe[j][:, 0:nk], func=AF.Copy, scale=sj[:, 3:4]),
             reads=["nPe%d" % j, "nst%d" % j], writes=["nPn%d" % j])
        yield
        nkt = nk // 128
        ptp = g.ps[bk + 1].bitcast(BF16)
        for kt in range(nkt):
            P.op("pe", lambda e, kt=kt: e.transpose(ptp[:, kt * 128:(kt + 1) * 128], Pn[j][:, kt * 128:(kt + 1) * 128], g.ident_bf),
                 reads=["nPn%d" % j], writes=["nPTp%d" % j, sname])
        yield
        P.op("dve", lambda e: e.tensor_copy(out=PTs[j].rearrange("p a b -> p (a b)")[:, 0:nk], in_=ptp[:, 0:nk]), reads=["nPTp%d" % j], writes=["nPT%d" % j])
        yield
        ops = g.ps[bk][:, 0:128]
        for kt in range(nkt):
            P.op("pe", lambda e, kt=kt: e.matmul(ops, lhsT=vt[i][:, vtiles[kt], :], rhs=PTs[j][:, kt, :], start=(kt == 0), stop=(kt == nkt - 1)),
                 reads=["nv%d" % i, "nPT%d" % j], writes=["no%d" % j, sname])
        yield
        P.op("dve", lambda e: e.tensor_tensor(out=yc[i][:, q0:q0 + 128], in0=ops, in1=sgt[i][:, q0:q0 + 128], op=ALU.mult),
             reads=["no%d" % j, "nsg%d" % i], writes=["nyc%d_%d" % (i, ti)])
        yield

    load(0)
    pending = [(k, ti) for k in range(len(items)) for ti in tiles]
    remaining = {k: len(tiles) for k in range(len(items))}
    started = set()
    active = []
    loaded = {0}
    while pending or active:
        while pending and len(active) < NS:
            k, ti = pending[0]
            if k not in loaded:
                if k >= 2 and remaining[k - 2] > 0:
                    break
                load(k); loaded.add(k)
            if k not in started:
                h, b = items[k]
                if b == 0:
                    if k >= 1 and remaining[k - 1] > 0:
                        break
                    P.dma("sp", lambda e, h=h: e.dma_start(out=bias, in_=g.rpbias[l, h].rearrange("p (a b) -> p a b", a=5)), writes=["nbias"])
                started.add(k)
            pending.pop(0)
            used = [a_[1] for a_ in active]
            slot = [c_ for c_ in range(NS) if c_ not in used][0]
            active.append((tile(k, ti, slot), slot, k))
        for a_ in list(active):
            try:
                next(a_[0])
            except StopIteration:
                active.remove(a_)
                kk = a_[2]
                remaining[kk] -= 1
                if remaining[kk] == 0:
                    store(kk)


def phase_natten(g, l, last):
    nc, P, A = g.nc, g.P, g.A
    P.barrier()
    A.reset()
    SCALE = 128.0 ** -0.5
    bias = A.alloc("nbias", [5, 640])
    qT = [A.alloc("nq%d" % i, [S], BF16) for i in range(2)]
    kT = [A.alloc("nk%d" % i, [S], BF16) for i in range(2)]
    vt = [A.alloc("nv%d" % i, [18, 128], BF16) for i in range(2)]
    gr = [A.alloc("ng%d" % i, [S], BF16) for i in range(2)]
    sgt = A.alloc("nsg", [S])
    yc = [A.alloc("nyc%d" % i, [S], BF16) for i in range(2)]
    Sb = [A.alloc("nSb%d" % i, [896]) for i in range(2)]
    Pe = [A.alloc("nPe%d" % i, [896]) for i in range(2)]
    Pn = [A.alloc("nPn%d" % i, [896], BF16) for i in range(2)]
    PTs = [A.alloc("nPT%d" % i, [7, 128], BF16) for i in range(2)]
    st = [A.alloc("nst%d" % i, [4]) for i in range(2)]
    items = [(h, b) for h in range(4) for b in range(NB)]

    def load(k):
        h, b = items[k]
        i = k % 2
        P.dma("sp", lambda e: e.dma_start(out=qT[i], in_=g.PT[24 + h][:, b * S:(b + 1) * S]), writes=["nq%d" % i])
        P.dma("sp", lambda e: e.dma_start(out=kT[i], in_=g.PT[28 + h][:, b * S:(b + 1) * S]), writes=["nk%d" % i])
        P.dma("sp", lambda e: e.dma_start(out=gr[i], in_=g.PT[36 + h][:, b * S:(b + 1) * S]), writes=["ng%d" % i])
        P.dma("sp", lambda e: e.dma_start(out=vt[i], in_=g.VC[b * S:(b + 1) * S, h * 128:(h + 1) * 128].rearrange("(t p) d -> p t d", p=128)),
              writes=["nv%d" % i])

    load(0)
    cnt = 0
    for k, (h, b) in enumerate(items):
        i = k % 2
        if b == 0:
            P.dma("sp", lambda e, h=h: e.dma_start(out=bias, in_=g.rpbias[l, h].rearrange("p (a b) -> p a b", a=5)), writes=["nbias"])
        if k + 1 < len(items):
            load(k + 1)
        P.op("act", lambda e, i=i: e.activation(out=sgt, in_=gr[i], func=AF.Silu), reads=["ng%d" % i], writes=["nsg"])
        tiles = list(range(16)) + ([] if last else [16, 17])
        for ti in tiles:
            j = cnt % 2; cnt += 1
            ctxq = ti >= 16
            if not ctxq:
                q0 = CTX + ti * 128
                sr = min(max(2 * ti - 4, 0), 22)
                k0 = CTX + sr * 64
                nk = 896
                vtiles = [2 + sr // 2 + m for m in range(5)] + [0, 1]
            else:
                q0 = (ti - 16) * 128
                nk = 256
                vtiles = [0, 1]
            bk = 2 * j
            Sps = g.psall[:, bk * 512: bk * 512 + nk]
            rd = ["nq%d" % i, "nk%d" % i]
            if not ctxq:
                P.op("pe", lambda e, i=i, q0=q0, k0=k0, bk=bk: e.matmul(g.psall[:, bk * 512:bk * 512 + 512], lhsT=qT[i][:, q0:q0 + 128],
                                                                       rhs=kT[i][:, k0:k0 + 512], start=True, stop=True), reads=rd, writes=["nS%d" % j])
                P.op("pe", lambda e, i=i, q0=q0, k0=k0, bk=bk: e.matmul(g.psall[:, bk * 512 + 512:bk * 512 + 640], lhsT=qT[i][:, q0:q0 + 128],
                                                                       rhs=kT[i][:, k0 + 512:k0 + 640], start=True, stop=True), reads=rd, writes=["nS%d" % j])
                P.op("pe", lambda e, i=i, q0=q0, bk=bk: e.matmul(g.psall[:, bk * 512 + 640:bk * 512 + 896], lhsT=qT[i][:, q0:q0 + 128],
                                                                rhs=kT[i][:, 0:CTX], start=True, stop=True), reads=rd, writes=["nS%d" % j])
                pat = natten_pat(ti)
                P.op("dve", lambda e, j=j, pat=pat, bk=bk: e.scalar_tensor_tensor(
                    out=Sb[j][:, 0:640], in0=g.psall[:, bk * 512:bk * 512 + 640], scalar=SCALE, in1=bias[:, pat, :],
                    op0=ALU.mult, op1=ALU.add), reads=["nS%d" % j, "nbias"], writes=["nSb%d" % j])
                P.op("act", lambda e, j=j, bk=bk: e.activation(out=Sb[j][:, 640:896], in_=g.psall[:, bk * 512 + 640:bk * 512 + 896],
                                                               func=AF.Copy, scale=SCALE), reads=["nS%d" % j], writes=["nSb%d" % j])
            else:
                P.op("pe", lambda e, i=i, q0=q0, bk=bk: e.matmul(g.psall[:, bk * 512:bk * 512 + 256], lhsT=qT[i][:, q0:q0 + 128],
                                                                rhs=kT[i][:, 0:CTX], start=True, stop=True), reads=rd, writes=["nS%d" % j])
                P.op("act", lambda e, j=j, bk=bk: e.activation(out=Sb[j][:, 0:256], in_=g.psall[:, bk * 512:bk * 512 + 256],
                                                               func=AF.Copy, scale=SCALE), reads=["nS%d" % j], writes=["nSb%d" % j])
            sj = st[j]
            P.op("dve", lambda e, j=j, nk=nk, sj=sj: e.reduce_max(out=sj[:, 0:1], in_=Sb[j][:, 0:nk], axis=AX.X), reads=["nSb%d" % j], writes=["nst%d" % j])
            P.op("dve", lambda e, sj=sj: e.tensor_scalar_mul(out=sj[:, 1:2], in0=sj[:, 0:1], scalar1=-1.0), reads=["nst%d" % j], writes=["nst%d" % j])
            P.op("pool", lambda e, sj=sj: e.memset(sj[:, 2:3], 0.0), writes=["nst%d" % j])
            P.op("act", lambda e, j=j, nk=nk, sj=sj: e.activation(out=Pe[j][:, 0:nk], in_=Sb[j][:, 0:nk], func=AF.Exp, bias=sj[:, 1:2], scale=1.0,
                                                                 accum_out=sj[:, 2:3]), reads=["nSb%d" % j, "nst%d" % j], writes=["nPe%d" % j, "nst%d" % j])
            P.op("dve", lambda e, sj=sj: e.reciprocal(out=sj[:, 3:4], in_=sj[:, 2:3]), reads=["nst%d" % j], writes=["nst%d" % j])
            P.op("act", lambda e, j=j, nk=nk, sj=sj: e.activation(out=Pn[j][:, 0:nk], in_=Pe[j][:, 0:nk], func=AF.Copy, scale=sj[:, 3:4]),
                 reads=["nPe%d" % j, "nst%d" % j], writes=["nPn%d" % j])
            nkt = nk // 128
            ptp = g.ps[4 + j].bitcast(BF16)
            for kt in range(nkt):
                P.op("pe", lambda e, j=j, kt=kt, ptp=ptp: e.transpose(ptp[:, kt * 128:(kt + 1) * 128], Pn[j][:, kt * 128:(kt + 1) * 128], g.ident_bf),
                     reads=["nPn%d" % j], writes=["nQTp%d" % j])
            evac(P, cnt, PTs[j].rearrange("p a b -> p (a b)")[:, 0:nk], ptp[:, 0:nk], ["nQTp%d" % j], ["nPT%d" % j])
            ops = g.ps[6 + j][:, 0:128]
            for kt in range(nkt):
                P.op("pe", lambda e, i=i, j=j, kt=kt, vti=vtiles[kt], ops=ops, nkt=nkt: e.matmul(
                    ops, lhsT=vt[i][:, vti, :], rhs=PTs[j][:, kt, :], start=(kt == 0), stop=(kt == nkt - 1)),
                    reads=["nv%d" % i, "nPT%d" % j], writes=["nqo%d" % j])
            P.op("dve", lambda e, i=i, q0=q0, ops=ops: e.tensor_tensor(out=yc[i][:, q0:q0 + 128], in0=ops, in1=sgt[:, q0:q0 + 128], op=ALU.mult),
                 reads=["nqo%d" % j, "nsg"], writes=["nyc%d" % i])
        if last:
            P.dma("sp", lambda e, i=i, h=h, b=b: e.dma_start(out=g.Y3[2, h][:, b * S + CTX:(b + 1) * S], in_=yc[i][:, CTX:S]), reads=["nyc%d" % i], writes=["Y3"])
        else:
            P.dma("sp", lambda e, i=i, h=h, b=b: e.dma_start(out=g.Y3[2, h][:, b * S:(b + 1) * S], in_=yc[i]), reads=["nyc%d" % i], writes=["Y3"])


def phase_merge(g, l, last):
    nc, P, A = g.nc, g.P, g.A
    P.barrier()
    A.reset()
    wbr = A.alloc("wbr", [12, D], BF16)
    wo = A.alloc("wo", [8, D], BF16)
    stg = [A.alloc("mstg%d" % i, [4, D]) for i in range(2)]
    srcs = [g.w_br[l, jb].rearrange("(kc p) n -> p kc n", p=128) for jb in range(3)] + \
           [g.w_out[l].rearrange("(kc p) n -> p kc n", p=128)[:, 0:4, :], g.w_out[l].rearrange("(kc p) n -> p kc n", p=128)[:, 4:8, :]]
    dsts = [wbr[:, 0:4, :], wbr[:, 4:8, :], wbr[:, 8:12, :], wo[:, 0:4, :], wo[:, 4:8, :]]
    for k in range(5):
        i = k % 2
        P.dma("sp", lambda e, k=k, i=i: e.dma_start(out=stg[i], in_=srcs[k]), writes=["mstg%d" % i])
        P.op("pool", lambda e, k=k, i=i: e.tensor_copy(out=dsts[k], in_=stg[i]), reads=["mstg%d" % i], writes=["mw%d" % k])
    wreads = ["mw%d" % k for k in range(5)]
    yin = [A.alloc("myin%d" % i, [12, 512], BF16) for i in range(2)]
    gl = [A.alloc("mgl%d" % i, [3, 512], BF16) for i in range(2)]
    sgm = [A.alloc("msg%d" % i, [3, 512]) for i in range(2)]
    tj = [A.alloc("mtj%d" % i, [3, 512]) for i in range(2)]
    mT = A.alloc("mT", [8, 512], BF16)
    yT = A.alloc("myT", [8, 512])
    ysq = A.alloc("mysq", [8, 512], BF16)
    xin = A.alloc("mxin", [8, 512])
    rs = A.alloc("mrs", [512])
    xsrc = (g.xT_in if l == 0 else g.xs).rearrange("(kc p) t -> p kc t", p=128)
    xdst = g.xs.rearrange("(kc p) t -> p kc t", p=128)
    odst = g.outT.rearrange("(kc p) t -> p kc t", p=128)
    NTT = NT // 512
    gsrc = g.PT[40:64].rearrange("(j t) p n -> t p j n", j=3)
    ev = 0

    def loady(T):
        i = T % 2
        P.dma("sp", lambda e: e.dma_start(out=yin[i], in_=g.Y3[:, :, :, T * 512:(T + 1) * 512].rearrange("j k p n -> p (j k) n")),
              reads=["Y3"], writes=["myin%d" % i])

    loady(0)
    gcnt = 0
    for T in range(NTT):
        i = T % 2
        if T + 1 < NTT:
            loady(T + 1)
        P.dma("sp", lambda e, T=T: e.dma_start(out=xin, in_=xsrc[:, :, T * 512:(T + 1) * 512]), reads=["xs"], writes=["mxin"])
        for dt in range(8):
            gi = gcnt % 2; gcnt += 1
            P.dma("sp", lambda e, dt=dt, T=T, gi=gi: e.dma_start(out=gl[gi], in_=gsrc[dt][:, :, T * 512:(T + 1) * 512]), writes=["mgl%d" % gi])
            P.op("act", lambda e, gi=gi: e.activation(out=sgm[gi], in_=gl[gi], func=AF.Sigmoid), reads=["mgl%d" % gi], writes=["msg%d" % gi])
            for jb in range(3):
                bank = (ev % 6); ev += 1
                ps = g.ps[bank]
                for kc in range(4):
                    P.op("pe", lambda e, ps=ps, jb=jb, kc=kc, dt=dt, i=i: e.matmul(
                        ps, lhsT=wbr[:, jb * 4 + kc, dt * 128:(dt + 1) * 128], rhs=yin[i][:, jb * 4 + kc, :], start=(kc == 0), stop=(kc == 3)),
                        reads=wreads + ["myin%d" % i], writes=["mps%d" % bank])
                P.op("dve", lambda e, ps=ps, jb=jb, gi=gi: e.tensor_tensor(out=tj[gi][:, jb, :], in0=ps, in1=sgm[gi][:, jb, :], op=ALU.mult),
                     reads=["mps%d" % bank, "msg%d" % gi], writes=["mtj%d_%d" % (gi, jb)])
            P.op("pool", lambda e, gi=gi: e.tensor_tensor(out=tj[gi][:, 0, :], in0=tj[gi][:, 0, :], in1=tj[gi][:, 1, :], op=ALU.add),
                 reads=["mtj%d_0" % gi, "mtj%d_1" % gi], writes=["mtj%d_0" % gi])
            P.op("pool", lambda e, gi=gi, dt=dt: e.tensor_tensor(out=mT[:, dt, :], in0=tj[gi][:, 0, :], in1=tj[gi][:, 2, :], op=ALU.add),
                 reads=["mtj%d_0" % gi, "mtj%d_2" % gi], writes=["mT%d" % dt])
        mreads = ["mT%d" % dt for dt in range(8)]
        for d2 in range(8):
            bank = (ev % 6); ev += 1
            ps = g.ps[bank]
            for kc in range(8):
                P.op("pe", lambda e, ps=ps, kc=kc, d2=d2: e.matmul(ps, lhsT=wo[:, kc, d2 * 128:(d2 + 1) * 128], rhs=mT[:, kc, :],
                                                                   start=(kc == 0), stop=(kc == 7)), reads=wreads + mreads, writes=["mps%d" % bank])
            P.op("act", lambda e, ps=ps, d2=d2: e.copy(out=yT[:, d2, :], in_=ps), reads=["mps%d" % bank], writes=["myT%d" % d2])
            P.op("act", lambda e, d2=d2: e.activation(out=ysq[:, d2, :], in_=yT[:, d2, :], func=AF.Square), reads=["myT%d" % d2], writes=["mysq%d" % d2])
        ssp = g.ps[6]
        for d2 in range(8):
            P.op("pe", lambda e, d2=d2: e.matmul(ssp, lhsT=g.ones_bf, rhs=ysq[:, d2, :], start=(d2 == 0), stop=(d2 == 7)),
                 reads=["mysq%d" % d2], writes=["mss"])
        rsqrt_act(P, rs, ssp, float(D * EPS), ["mss"], ["mrs"])
        for half in range(2):
            pc = 2 * T + half
            j = seg_j(pc)
            isctx = (pc % 9 == 0)
            if last and isctx:
                continue
            c0, c1 = half * 256, half * 256 + 256
            for d2 in range(8):
                P.op("dve", lambda e, d2=d2, j=j, c0=c0, c1=c1: e.scalar_tensor_tensor(
                    out=yT[:, d2, c0:c1], in0=yT[:, d2, c0:c1], scalar=g.modG[:, d2, j:j + 1], in1=rs[:, c0:c1], op0=ALU.mult, op1=ALU.mult),
                    reads=["myT%d" % d2, "mrs"], writes=["myT%d" % d2])
            P.op("pool", lambda e, c0=c0, c1=c1: e.tensor_tensor(out=xin[:, :, c0:c1], in0=xin[:, :, c0:c1], in1=yT[:, :, c0:c1], op=ALU.add),
                 reads=["mxin"] + ["myT%d" % d2 for d2 in range(8)], writes=["mxin"])
            if last:
                b = pc // 9
                q = pc % 9 - 1
                oc = b * LAT + q * 256
                P.dma("sp", lambda e, c0=c0, c1=c1, oc=oc: e.dma_start(out=odst[:, :, oc:oc + 256], in_=xin[:, :, c0:c1]), reads=["mxin"], writes=["out"])
            else:
                P.dma("sp", lambda e, c0=c0, c1=c1, pc=pc: e.dma_start(out=xdst[:, :, pc * 256:(pc + 1) * 256], in_=xin[:, :, c0:c1]),
                      reads=["mxin"], writes=["xs_w"])


def host_lruw(inp):
    out = np.zeros((DEPTH, 128, 16, 128), np.float32)
    for gi, name in enumerate(("lru_wa", "lru_wx")):
        w = inp[name]
        for d in range(2):
            for ct in range(4):
                for hb in range(2):
                    out[:, hb * 64:(hb + 1) * 64, gi * 8 + d * 4 + ct, hb * 64:(hb + 1) * 64] = w[:, d, ct * 2 + hb]
    return out.reshape(DEPTH, 128, 16 * 128)


def host_rpbias(inp):
    rpb = inp["rpb"]
    out = np.full((DEPTH, 4, 128, 5, 640), NEG, np.float32)
    cq = np.arange(64)
    kc = np.arange(64)
    win = np.clip(cq - 8, 0, 48)
    col_ok = (kc[None, :] >= win[:, None]) & (kc[None, :] < win[:, None] + 16)
    dc = np.clip(kc[None, :] - cq[:, None], -15, 15) + 15
    for pat, ti in enumerate((0, 1, 2, 14, 15)):
        sr = min(max(2 * ti - 4, 0), 22)
        for a in range(2):
            r = 2 * ti + a
            r0 = min(max(r - 4, 0), 24)
            for m in range(10):
                kr = sr + m
                if r0 <= kr < r0 + 8:
                    dr = kr - r + 7
                    vals = rpb[:, :, dr, :][:, :, dc]
                    blk = np.where(col_ok[None, None], vals, NEG)
                    out[:, :, a * 64:(a + 1) * 64, pat, m * 64:(m + 1) * 64] = blk
    return out.reshape(DEPTH, 4, 128, 5 * 640)


def prep_inputs(inp, nl=DEPTH):
    w_in = inp["w_in"]
    w_big = np.ascontiguousarray(np.concatenate([w_in[:, :, :2048], w_in[:, :, 2064:]], axis=2))
    w_small = np.ascontiguousarray(w_in[:, :, 2048:2064])
    prm = host_params(inp)
    cst = host_consts()
    lruw = host_lruw(inp)
    rpbias = host_rpbias(inp)
    rope = host_rope().reshape(128, 2 * LAT)
    maps = []
    nb_total = inp["x"].shape[0]
    for core in range(nb_total // NB):
        xs = []
        cs = []
        for b in range(core * NB, (core + 1) * NB):
            xs.append(inp["ctx"][b].T)
            xs.append(inp["x"][b].T)
            cs.append(inp["c"][b])
        cs.append(inp["c_ctx"])
        xT = np.ascontiguousarray(np.concatenate(xs, axis=1))
        cT = np.ascontiguousarray(np.stack(cs, axis=1).reshape(8, 128, 3).transpose(1, 0, 2).reshape(128, 24))
        maps.append({"xT": xT, "cT": cT, "w_mod": inp["w_mod"][:nl], "w_big": w_big[:nl], "w_small": w_small[:nl],
                     "prm": prm[:nl], "cst": cst, "rope": rope, "lruw": lruw[:nl], "rpbias": rpbias[:nl],
                     "w_br": inp["w_branch"][:nl], "w_out": inp["w_out"][:nl]})
    return maps


def kernel(**inputs):
    inp = {k: np.asarray(v) for k, v in inputs.items()}
    maps = prep_inputs(inp)
    nc = build()
    res = run_bass_kernel_spmd(nc, maps, core_ids=list(range(len(maps))))
    outs = []
    for r in res.results:
        oT = np.asarray(r["outT"])
        for b in range(NB):
            outs.append(oT[:, b * LAT:(b + 1) * LAT].T)
    return np.ascontiguousarray(np.stack(outs, axis=0)).astype(np.float32)
```

```python
import contextlib
import re
import numpy as np
import ml_dtypes
import concourse.bass as bass
import concourse.mybir as mybir
from concourse.bass_utils import run_bass_kernel_spmd

F32 = mybir.dt.float32
BF16 = mybir.dt.bfloat16
AF = mybir.ActivationFunctionType
ALU = mybir.AluOpType
AX = mybir.AxisListType

D = 1024
NB = 2
CTX = 256
LAT = 2048
S = CTX + LAT
NT = NB * S
DEPTH = 4
EPS = 1e-6
ENGS = ("pe", "act", "dve", "pool", "sp")
NDMASEM = 12


_BANK_RULES = [
    (re.compile(r"^modps$"), lambda m: [0]),
    (re.compile(r"^(?:ssps|pps|lps|mps|c0ps|cps)(\d)"), lambda m: [int(m.group(1))]),
    (re.compile(r"^nQTp(\d)"), lambda m: [4 + int(m.group(1))]),
    (re.compile(r"^nqo(\d)"), lambda m: [6 + int(m.group(1))]),
    (re.compile(r"^nS(\d)"), lambda m: [2 * int(m.group(1)), 2 * int(m.group(1)) + 1]),
    (re.compile(r"^nPTp(\d)"), lambda m: [2 * int(m.group(1)) + 1]),
    (re.compile(r"^no(\d)"), lambda m: [2 * int(m.group(1))]),
    (re.compile(r"^mss$"), lambda m: [6]),
    (re.compile(r"^cpA(\d)_"), lambda m: [int(m.group(1))]),
    (re.compile(r"^cpB(\d)_"), lambda m: [2 * int(m.group(1)) + 1]),
    (re.compile(r"^cp6_"), lambda m: [6]),
    (re.compile(r"^cp7_"), lambda m: [7]),
]
_BANK_CACHE = {}


def banks_of(name):
    b = _BANK_CACHE.get(name)
    if b is None:
        b = []
        for rx, fn in _BANK_RULES:
            m = rx.match(name)
            if m:
                b = fn(m)
                break
        _BANK_CACHE[name] = b
    return b


class Res:
    __slots__ = ("w", "r")

    def __init__(self):
        self.w = None
        self.r = {}


class Prog:
    def __init__(self, nc):
        self.nc = nc
        self.ops = {e: [] for e in ENGS}
        self.cnt = {e: 0 for e in ENGS}
        self.dcnt = {e: 0 for e in ENGS}
        self.seen = {e: {} for e in ENGS}
        self.pending = {e: [] for e in ENGS}
        self.res = {}
        self.inflight = {e: {} for e in ENGS}

    def R(self, name):
        r = self.res.get(name)
        if r is None:
            r = self.res[name] = Res()
        return r

    def _need(self, eng, tok, waits):
        if tok is None:
            return
        if tok[0] == "c":
            if tok[1] == "pe" and eng == "pe":
                return
            key = ("c", tok[1]); val = tok[2]
        else:
            key = ("d", tok[1], tok[2]); val = tok[3]
        if self.seen[eng].get(key, 0) >= val:
            return
        self.seen[eng][key] = val
        waits.append((key, val))

    def _deps(self, eng, reads, writes, waits):
        for t in self.pending[eng]:
            self._need(eng, t, waits)
        self.pending[eng] = []
        banks = set()
        for r in reads:
            banks.update(banks_of(r))
        for r in writes:
            banks.update(banks_of(r))
        self._banks = banks
        for k in banks:
            r = self.R("BANK%d" % k)
            if r.w is not None and r.w[1] != eng:
                self._need(eng, r.w, waits)
        for r in reads:
            self._need(eng, self.R(r).w, waits)
        for r in writes:
            r = self.R(r)
            self._need(eng, r.w, waits)
            for t in r.r.values():
                self._need(eng, t, waits)

    def _commit(self, tok, reads, writes):
        key = tok[:2] if tok[0] == "c" else tok[:3]
        for k in self._banks:
            self.R("BANK%d" % k).w = tok
        for r in reads:
            self.R(r).r[key] = tok
        for r in writes:
            r = self.R(r)
            r.w = tok
            r.r = {}

    limit = None
    nrec = 0

    def op(self, eng, fn, reads=(), writes=()):
        if Prog.limit is not None:
            Prog.nrec += 1
            if Prog.nrec > Prog.limit:
                return None
        waits = []
        self._deps(eng, reads, writes, waits)
        self.cnt[eng] += 1
        tok = ("c", eng, self.cnt[eng])
        self.ops[eng].append((waits, fn, tok))
        self._commit(tok, reads, writes)
        return tok

    def dma(self, eng, fn, reads=(), writes=()):
        if Prog.limit is not None:
            Prog.nrec += 1
            if Prog.nrec > Prog.limit:
                return None
        waits = []
        self._deps(eng, reads, writes, waits)
        i = self.dcnt[eng]
        self.dcnt[eng] += 1
        slot = i % NDMASEM
        val = 16 * (i // NDMASEM + 1)
        if i >= NDMASEM:
            self._need(eng, ("d", eng, slot, val - 16), waits)
        tok = ("d", eng, slot, val)
        self.inflight[eng][slot] = tok
        self.ops[eng].append((waits, fn, tok))
        self._commit(tok, reads, writes)
        return tok

    def barrier(self):
        toks = []
        for e in ("pe", "act", "dve", "pool"):
            if self.cnt[e]:
                toks.append(("c", e, self.cnt[e]))
        for e in ENGS:
            toks.extend(self.inflight[e].values())
        for e in ENGS:
            self.pending[e] = list(toks)
        self.res = {}

    def emit(self, final_tokens=()):
        nc = self.nc
        with contextlib.ExitStack() as st:
            csem = {e: st.enter_context(nc.semaphore("c_" + e)) for e in ("pe", "act", "dve", "pool")}
            dsem = {}
            for e in ENGS:
                for s in range(min(NDMASEM, self.dcnt[e])):
                    dsem[(e, s)] = st.enter_context(nc.semaphore("d_%s_%d" % (e, s)))
            fw = []
            for t in self.pending["sp"]:
                self._need("sp", t, fw)
            for t in final_tokens:
                self._need("sp", t, fw)
            block = st.enter_context(nc.Block())

            def semof(key):
                return csem[key[1]] if key[0] == "c" else dsem[(key[1], key[2])]

            def replay(ename):
                def body(e):
                    for waits, fn, tok in self.ops[ename]:
                        for key, val in waits:
                            e.wait_ge(semof(key), val)
                        ins = fn(e)
                        if tok[0] == "c":
                            ins.then_inc(csem[tok[1]], 1)
                        else:
                            ins.then_inc(dsem[(tok[1], tok[2])], 16)
                    if ename == "sp":
                        for key, val in fw:
                            e.wait_ge(semof(key), val)
                return body

            block.tensor(replay("pe"))
            block.scalar(replay("act"))
            block.vector(replay("dve"))
            block.gpsimd(replay("pool"))
            block.sync(replay("sp"))


class Arena:
    def __init__(self, ap, nbytes):
        self.ap = ap
        self.nbytes = nbytes
        self.off = 0
        self.names = {}

    def reset(self):
        self.off = 0
        self.names = {}

    def alloc(self, name, shape, dt=F32):
        esz = 4 if dt == F32 else 2
        n = int(np.prod(shape))
        nb = (n * esz + 63) // 64 * 64
        assert self.off + nb <= self.nbytes, "SBUF arena overflow at %s: %d + %d > %d" % (name, self.off, nb, self.nbytes)
        v = self.ap[:, self.off // 4:(self.off + nb) // 4]
        self.off += nb
        if dt != F32:
            v = v.bitcast(dt)
        v = v[:, 0:n]
        if len(shape) == 2:
            v = v.rearrange("p (a b) -> p a b", a=shape[0])
        elif len(shape) == 3:
            v = v.rearrange("p (a b c) -> p a b c", a=shape[0], b=shape[1])
        self.names[name] = v
        return v


class Ctx:
    pass


LASTP = None


def build(nlayers=DEPTH, dbg=(), run="CDEF"):
    nc = bass.Bass("TRN2", target_bir_lowering=False)
    g = Ctx()
    g.run = run
    g.nc = nc
    g.dbg = dbg

    def din(name, shape, dt=F32):
        return nc.dram_tensor(name, list(shape), dt, kind="ExternalInput").ap()

    def dscr(name, shape, dt=F32):
        kind = "ExternalOutput" if name in dbg else "ExternalInput" if (name + "_in") in dbg else "Internal"
        return nc.dram_tensor(name, list(shape), dt, kind=kind).ap()

    g.xT_in = din("xT", [D, NT])
    g.cT = din("cT", [128, 8 * 3])
    ND = nlayers
    g.w_mod = din("w_mod", [ND, D, 3 * D])
    g.w_big = din("w_big", [ND, D, 8192])
    g.w_small = din("w_small", [ND, D, 16])
    g.prm = din("prm", [ND, 128, PRM_N])
    g.lruw = din("lruw", [ND, 128, 16 * 128])
    g.rpbias = din("rpbias", [ND, 4, 128, 5 * 640])
    g.w_br = din("w_br", [ND, 3, 512, D])
    g.w_out = din("w_out", [ND, D, D])
    g.cst = din("cst", [128, CST_N])
    g.rope = din("rope", [128, 2 * LAT])
    g.outT = nc.dram_tensor("outT", [D, NB * LAT], F32, kind="ExternalOutput").ap()

    g.xs = dscr("xs", [D, NT])
    g.PT = dscr("PT", [64, 128, NT], BF16)
    g.VC = dscr("VC", [NT, 512], BF16)
    g.BA = dscr("BA", [NT, 16])
    g.Y3 = dscr("Y3", [3, 4, 128, NT], BF16)

    with contextlib.ExitStack() as st:
        arena_t = st.enter_context(nc.sbuf_tensor("arena", [128, ARENA_BYTES // 4], F32))
        pers_t = st.enter_context(nc.sbuf_tensor("pers", [128, PERS_BYTES // 4], F32))
        g.psall = st.enter_context(nc.psum_tensor("psall", [128, 4096], F32))
        g.ps = [g.psall[:, i * 512:(i + 1) * 512] for i in range(8)]
        g.A = Arena(arena_t[:], ARENA_BYTES)
        g.PA = Arena(pers_t[:], PERS_BYTES)
        P = g.P = Prog(nc)
        global LASTP
        LASTP = P
        phase_setup(g)
        for l in range(nlayers):
            last = (l == DEPTH - 1)
            phase_mod(g, l)
            if "skipB" not in dbg:
                phase_norm_proj(g, l)
            if "stopB" in dbg:
                break
            if "C" in g.run:
                phase_gdn(g, l, last)
            if "D" in g.run:
                phase_lru(g, l)
            if "E" in g.run:
                phase_natten(g, l, last)
            if "F" in g.run:
                phase_merge(g, l, last)
        P.barrier()
        P.emit()
    return nc


def _cst_layout():
    lay = {}
    off = 0
    names = ["ident", "ones", "tri0", "tri1", "mit0", "mit1", "mst0", "mst1"] + ["lm%d" % i for i in range(7)] + ["prot"]
    for name, n in [(nm, 128) for nm in names]:
        lay[name] = (off, n)
        off += n
    return lay, off


CST_LAY, CST_N = _cst_layout()


def _prm_layout():
    lay = {}
    off = 0
    for name, n in (("bmodT", 24), ("g_pre", 8), ("g_post", 8), ("conv_b", 16), ("conv_b_bias", 4),
                    ("lru_ba", 8), ("lru_bx", 8), ("lru_lam", 8), ("conv_a", 48), ("onorm_a", 1),
                    ("dt_bias", 8), ("a_log", 8)):
        lay[name] = (off, n)
        off += n
    return lay, off


PRM_LAY, PRM_N = _prm_layout()

ARENA_BYTES = 196 * 1024
PERS_BYTES = 11 * 1024


def host_consts():
    c = np.zeros((128, CST_N), np.float32)
    o, n = CST_LAY["ident"]; c[:, o:o + n] = np.eye(128, dtype=np.float32)
    o, n = CST_LAY["ones"]; c[:, o:o + n] = 1.0
    p = np.arange(128)[:, None]; f = np.arange(128)[None, :]

    def put(name, m):
        o, n = CST_LAY[name]; c[:, o:o + n] = m.astype(np.float32)
    put("tri0", p <= f); put("tri1", p >= f)
    put("mit0", f >= p); put("mit1", f <= p)
    put("mst0", f > p); put("mst1", f < p)
    for lv in range(7):
        put("lm%d" % lv, ((p >> (lv + 1)) == (f >> (lv + 1))) & ((p >> lv) != (f >> lv)))
    pr = np.zeros((128, 128), np.float32)
    for m in range(128):
        if (m % 64) < 32:
            pr[m + 32, m] = -1.0
        else:
            pr[m - 32, m] = 1.0
    put("prot", pr)
    return c


def host_rope():
    t = np.arange(LAT)
    rows = (t // 64).astype(np.float32); cols = (t % 64).astype(np.float32)
    inv = (10000.0 ** (-np.arange(32, dtype=np.float32) / 32)).astype(np.float32)
    out = np.zeros((128, 2, LAT), np.float32)
    for p in range(128):
        pos = rows if p < 64 else cols
        ang = (pos * inv[p % 32]).astype(np.float32)
        out[p, 0] = np.cos(ang); out[p, 1] = np.sin(ang)
    return out


def host_params(inp):
    p = np.zeros((DEPTH, 128, PRM_N), np.float32)
    for l in range(DEPTH):
        def put(name, arr):
            o, n = PRM_LAY[name]
            p[l, :, o:o + n] = arr.reshape(128, n)
        put("bmodT", inp["b_mod"][l].reshape(24, 128).T)
        put("g_pre", inp["g_pre"][l].reshape(8, 128).T)
        put("g_post", inp["g_post"][l].reshape(8, 128).T)
        put("conv_b", inp["conv_b"][l].reshape(4, 4, 128).transpose(2, 1, 0))
        put("conv_b_bias", inp["conv_b_bias"][l].reshape(4, 128).T)
        put("lru_ba", inp["lru_ba"][l].reshape(2, 4, 128).transpose(2, 0, 1))
        put("lru_bx", inp["lru_bx"][l].reshape(2, 4, 128).transpose(2, 0, 1))
        put("lru_lam", inp["lru_lam"][l].reshape(2, 4, 128).transpose(2, 0, 1))
        put("conv_a", inp["conv_a"][l].reshape(4, 12, 128).transpose(2, 1, 0))
        put("onorm_a", inp["onorm_a"][l].reshape(128, 1))
        put("dt_bias", np.broadcast_to(inp["dt_bias"][l].reshape(1, 8), (128, 8)))
        put("a_log", np.broadcast_to(inp["a_log"][l].reshape(1, 8), (128, 8)))
    return p


def rsqrt_act(P, out, in_, bias, reads, writes):
    P.op("act", lambda e: e.activation(out=out, in_=in_, func=AF.Ln, bias=bias, scale=1.0), reads=reads, writes=writes)
    P.op("act", lambda e: e.activation(out=out, in_=out, func=AF.Exp, scale=-0.5), reads=writes, writes=writes)


def phase_setup(g):
    nc, P, PA = g.nc, g.P, g.PA
    g.cst_sb = PA.alloc("cst", [CST_N])
    P.dma("sp", lambda e: e.dma_start(out=g.cst_sb, in_=g.cst[:, :]), writes=["cst"])
    g.cT_sb = PA.alloc("cT", [8, 3])
    P.dma("sp", lambda e: e.dma_start(out=g.cT_sb, in_=g.cT.rearrange("p (a b) -> p a b", a=8)), writes=["cT"])
    g.sc = PA.alloc("sc", [8, 3])
    P.op("act", lambda e: e.activation(out=g.sc, in_=g.cT_sb, func=AF.Silu), reads=["cT"], writes=["sc"])
    o, n = CST_LAY["ident"]; g.ident = g.cst_sb[:, o:o + n]
    o, n = CST_LAY["ones"]; g.ones = g.cst_sb[:, o:o + n]
    g.ident_bf = PA.alloc("ident_bf", [128], BF16)
    g.ones_bf = PA.alloc("ones_bf", [128], BF16)
    P.op("dve", lambda e: e.tensor_copy(out=g.ident_bf, in_=g.ident), reads=["cst"], writes=["ident_bf"])
    P.op("dve", lambda e: e.tensor_copy(out=g.ones_bf, in_=g.ones), reads=["cst"], writes=["ones_bf"])
    g.prm_sb = PA.alloc("prm", [PRM_N])
    g.mod = PA.alloc("mod", [24, 3])
    g.modA = PA.alloc("modA", [8, 3])
    g.modG = PA.alloc("modG", [8, 3])
    g.gpre32 = PA.alloc("gpre32", [8])
    g.gpost32 = PA.alloc("gpost32", [8])
    P.barrier()


def prm(g, name):
    o, n = PRM_LAY[name]
    return g.prm_sb[:, o:o + n]


def phase_mod(g, l):
    nc, P, A = g.nc, g.P, g.A
    P.barrier()
    A.reset()
    P.dma("sp", lambda e: e.dma_start(out=g.prm_sb, in_=g.prm[l, :, :]), writes=["prm"])
    wv = g.w_mod[l].rearrange("(kc p) n -> p kc n", p=128)
    wbuf = [A.alloc("wm%d" % i, [8, 512]) for i in range(2)]
    ps = g.ps[0][:, 0:72]
    for grp in range(6):
        wb = wbuf[grp % 2]
        P.dma("sp", lambda e, wb=wb, grp=grp: e.dma_start(out=wb, in_=wv[:, :, grp * 512:(grp + 1) * 512]),
              writes=["wm%d" % (grp % 2)])
        for ci in range(4):
            ct = grp * 4 + ci
            for kc in range(8):
                P.op("pe", lambda e, wb=wb, ci=ci, kc=kc, ct=ct: e.matmul(
                    ps[:, ct * 3:(ct + 1) * 3], lhsT=wb[:, kc, ci * 128:(ci + 1) * 128], rhs=g.sc[:, kc, :],
                    start=(kc == 0), stop=(kc == 7)),
                    reads=["wm%d" % (grp % 2), "sc"], writes=["modps"])
    ps3 = ps.rearrange("p (a b) -> p a b", a=24)
    bm = prm(g, "bmodT").unsqueeze(2).to_broadcast([128, 24, 3])
    P.op("dve", lambda e: e.tensor_tensor(out=g.mod, in0=ps3, in1=bm, op=ALU.add), reads=["modps", "prm"], writes=["mod"])
    P.op("dve", lambda e: e.tensor_scalar_mul(out=g.gpre32, in0=prm(g, "g_pre"), scalar1=32.0), reads=["prm"], writes=["gpre32"])
    P.op("dve", lambda e: e.tensor_scalar_mul(out=g.gpost32, in0=prm(g, "g_post"), scalar1=32.0), reads=["prm"], writes=["gpost32"])
    P.op("dve", lambda e: e.scalar_tensor_tensor(out=g.modA, in0=g.mod[:, 8:16, :], scalar=1.0,
                                                 in1=g.gpre32.unsqueeze(2).to_broadcast([128, 8, 3]),
                                                 op0=ALU.add, op1=ALU.mult), reads=["mod", "gpre32"], writes=["modA"])
    P.op("dve", lambda e: e.tensor_tensor(out=g.modG, in0=g.mod[:, 16:24, :],
                                          in1=g.gpost32.unsqueeze(2).to_broadcast([128, 8, 3]), op=ALU.mult),
         reads=["mod", "gpost32"], writes=["modG"])


def seg_j(pc):
    return 2 if pc % 9 == 0 else pc // 9


def phase_norm_proj(g, l):
    nc, P, A = g.nc, g.P, g.A
    P.barrier()
    A.reset()
    xsrc = (g.xT_in if l == 0 else g.xs).rearrange("(kc p) t -> p kc t", p=128)
    hT = A.alloc("hT", [8, NT], BF16)
    xp = [A.alloc("xp%d" % i, [8, 256]) for i in range(2)]
    sq = [A.alloc("sq%d" % i, [8, 256], BF16) for i in range(2)]
    xn = [A.alloc("xn%d" % i, [8, 256]) for i in range(2)]
    rstd = [A.alloc("rstd%d" % i, [256]) for i in range(2)]
    NPC = NT // 256

    def load(pc):
        i = pc % 2
        P.dma("sp", lambda e: e.dma_start(out=xp[i], in_=xsrc[:, :, pc * 256:(pc + 1) * 256]), writes=["xp%d" % i])

    load(0)
    for pc in range(NPC):
        i = pc % 2
        j = seg_j(pc)
        if pc + 1 < NPC:
            load(pc + 1)
        P.op("act", lambda e, i=i: e.activation(out=sq[i], in_=xp[i], func=AF.Square), reads=["xp%d" % i], writes=["sq%d" % i])
        ps = g.ps[i][:, 0:256]
        for kc in range(8):
            P.op("pe", lambda e, i=i, kc=kc, ps=ps: e.matmul(ps, lhsT=g.ones_bf, rhs=sq[i][:, kc, :], start=(kc == 0), stop=(kc == 7)),
                 reads=["sq%d" % i, "ones_bf"], writes=["ssps%d" % i])
        rsqrt_act(P, rstd[i], ps, float(D * EPS), ["ssps%d" % i], ["rstd%d" % i])
        P.op("dve", lambda e, i=i: e.tensor_tensor(out=xn[i], in0=xp[i], in1=rstd[i].unsqueeze(1).to_broadcast([128, 8, 256]),
                                                  op=ALU.mult), reads=["xp%d" % i, "rstd%d" % i], writes=["xn%d" % i])
        for kc in range(8):
            P.op("act", lambda e, i=i, kc=kc, j=j, pc=pc: e.activation(
                out=hT[:, kc, pc * 256:(pc + 1) * 256], in_=xn[i][:, kc, :], func=AF.Identity,
                bias=g.mod[:, kc, j:j + 1], scale=g.modA[:, kc, j:j + 1]),
                reads=["xn%d" % i, "mod", "modA"], writes=["hT%d" % pc])

    wsrc = g.w_big[l].rearrange("(kc p) n -> p kc n", p=128)
    wf = [A.alloc("wf%d" % i, [8, 512]) for i in range(2)]
    wb = [A.alloc("wb%d" % i, [8, 512], BF16) for i in range(2)]
    ost = [A.alloc("ost%d" % i, [512], BF16) for i in range(8)]
    ws_f = A.alloc("ws_f", [8, 16])
    ws_b = A.alloc("ws_b", [8, 16], BF16)
    osm = [A.alloc("osm%d" % i, [16]) for i in range(4)]
    NG = 16
    hreads = ["hT%d" % pc for pc in range(NPC)]

    def loadw(gi):
        i = gi % 2
        P.dma("sp", lambda e: e.dma_start(out=wf[i], in_=wsrc[:, :, gi * 512:(gi + 1) * 512]), writes=["wf%d" % i])

    loadw(0)
    P.dma("sp", lambda e: e.dma_start(out=ws_f, in_=g.w_small[l].rearrange("(kc p) n -> p kc n", p=128)), writes=["ws_f"])
    P.op("pool", lambda e: e.tensor_copy(out=ws_b, in_=ws_f), reads=["ws_f"], writes=["ws_b"])
    ev = 0
    for gi in range(NG):
        i = gi % 2
        if gi + 1 < NG:
            loadw(gi + 1)
        P.op("pool", lambda e, i=i: e.tensor_copy(out=wb[i], in_=wf[i]), reads=["wf%d" % i], writes=["wb%d" % i])
        if gi == 8:
            for tt in range(NT // 128):
                bank = ev % 8
                ps = g.ps[bank]
                for kc in range(8):
                    P.op("pe", lambda e, kc=kc, tt=tt, ps=ps, i=i: e.matmul(
                        ps, lhsT=hT[:, kc, tt * 128:(tt + 1) * 128], rhs=wb[i][:, kc, :], start=(kc == 0), stop=(kc == 7)),
                        reads=["wb%d" % i, "hT%d" % (tt // 2)], writes=["pps%d" % bank])
                o = ost[bank]
                eng = "act" if ev % 2 == 0 else "dve"
                if eng == "act":
                    P.op("act", lambda e, o=o, ps=ps: e.copy(out=o, in_=ps), reads=["pps%d" % bank], writes=["ost%d" % bank])
                else:
                    P.op("dve", lambda e, o=o, ps=ps: e.tensor_copy(out=o, in_=ps), reads=["pps%d" % bank], writes=["ost%d" % bank])
                P.dma("sp", lambda e, o=o, tt=tt: e.dma_start(out=g.VC[tt * 128:(tt + 1) * 128, :], in_=o),
                      reads=["ost%d" % bank], writes=["VC"])
                ev += 1
            continue
        for T in range(NT // 512):
            for ci in range(4):
                ct = gi * 4 + ci
                bank = ev % 8
                ps = g.ps[bank]
                for kc in range(8):
                    P.op("pe", lambda e, kc=kc, T=T, ps=ps, i=i, ci=ci: e.matmul(
                        ps, lhsT=wb[i][:, kc, ci * 128:(ci + 1) * 128], rhs=hT[:, kc, T * 512:(T + 1) * 512],
                        start=(kc == 0), stop=(kc == 7)),
                        reads=["wb%d" % i, "hT%d" % (2 * T), "hT%d" % (2 * T + 1)], writes=["pps%d" % bank])
                o = ost[bank]
                if ev % 2 == 0:
                    P.op("act", lambda e, o=o, ps=ps: e.copy(out=o, in_=ps), reads=["pps%d" % bank], writes=["ost%d" % bank])
                else:
                    P.op("dve", lambda e, o=o, ps=ps: e.tensor_copy(out=o, in_=ps), reads=["pps%d" % bank], writes=["ost%d" % bank])
                P.dma("sp", lambda e, o=o, ct=ct, T=T: e.dma_start(out=g.PT[ct, :, T * 512:(T + 1) * 512], in_=o),
                      reads=["ost%d" % bank], writes=["PT"])
                ev += 1
    for tt in range(NT // 128):
        bank = ev % 8
        ps = g.ps[bank][:, 0:16]
        for kc in range(8):
            P.op("pe", lambda e, kc=kc, tt=tt, ps=ps: e.matmul(
                ps, lhsT=hT[:, kc, tt * 128:(tt + 1) * 128], rhs=ws_b[:, kc, :], start=(kc == 0), stop=(kc == 7)),
                reads=["ws_b", "hT%d" % (tt // 2)], writes=["pps%d" % bank])
        o = osm[tt % 4]
        P.op("dve", lambda e, o=o, ps=ps: e.tensor_copy(out=o, in_=ps), reads=["pps%d" % bank], writes=["osm%d" % (tt % 4)])
        P.dma("sp", lambda e, o=o, tt=tt: e.dma_start(out=g.BA[tt * 128:(tt + 1) * 128, :], in_=o),
              reads=["osm%d" % (tt % 4)], writes=["BA"])
        ev += 1


def evac(P, k, out, in_, reads, writes):
    if k % 2 == 0:
        P.op("act", lambda e: e.copy(out=out, in_=in_), reads=reads, writes=writes)
    else:
        P.op("dve", lambda e: e.tensor_copy(out=out, in_=in_), reads=reads, writes=writes)


PADW = 2312
PC0, PL0 = 2, 261


def load_padded(P, buf, src, name):
    P.dma("sp", lambda e: e.dma_start(out=buf[:, PC0:PC0 + CTX], in_=src[:, 0:CTX]), writes=[name])
    P.dma("sp", lambda e: e.dma_start(out=buf[:, PL0:PL0 + LAT], in_=src[:, CTX:S]), writes=[name])


def store_padded(P, dst, buf, name, wname):
    P.dma("sp", lambda e: e.dma_start(out=dst[:, 0:CTX], in_=buf[:, PC0:PC0 + CTX]), reads=[name], writes=[wname])
    P.dma("sp", lambda e: e.dma_start(out=dst[:, CTX:S], in_=buf[:, PL0:PL0 + LAT]), reads=[name], writes=[wname])


def conv4(P, eng, out, buf, w, bias, reads, wname, tmp=None):
    n = PADW - 5
    o = out[:, 2:2 + n]
    if bias is not None:
        P.op(eng, lambda e: e.tensor_scalar(out=o, in0=buf[:, 0:n], scalar1=w[:, 0:1], scalar2=bias, op0=ALU.mult, op1=ALU.add),
             reads=reads, writes=[wname])
    else:
        P.op(eng, lambda e: e.tensor_scalar_mul(out=o, in0=buf[:, 0:n], scalar1=w[:, 0:1]), reads=reads, writes=[wname])
    for j in range(1, 4):
        if eng == "dve":
            P.op(eng, lambda e, j=j: e.scalar_tensor_tensor(out=o, in0=buf[:, j:j + n], scalar=w[:, j:j + 1], in1=o,
                                                             op0=ALU.mult, op1=ALU.add), reads=reads + [wname], writes=[wname])
        else:
            t = tmp[:, 2:2 + n]
            P.op(eng, lambda e, j=j, t=t: e.tensor_scalar_mul(out=t, in0=buf[:, j:j + n], scalar1=w[:, j:j + 1]), reads=reads, writes=[wname + "_t"])
            P.op(eng, lambda e, t=t: e.tensor_tensor(out=o, in0=o, in1=t, op=ALU.add), reads=[wname, wname + "_t"], writes=[wname])


def phase_lru(g, l):
    nc, P, A = g.nc, g.P, g.A
    P.barrier()
    A.reset()
    wst = A.alloc("lruw_f", [16, 128])
    wbd = A.alloc("lruw_b", [16, 128], BF16)
    P.dma("sp", lambda e: e.dma_start(out=wst, in_=g.lruw[l].rearrange("p (a b) -> p a b", a=16)), writes=["lruw_f"])
    P.op("pool", lambda e: e.tensor_copy(out=wbd, in_=wst), reads=["lruw_f"], writes=["lruw_b"])
    sp = A.alloc("sp", [8]); sc8 = A.alloc("sc8", [8]); sc16 = A.alloc("sc16", [8])
    P.op("act", lambda e: e.activation(out=sp, in_=prm(g, "lru_lam"), func=AF.Exp, scale=-1.0), reads=[], writes=["sp"])
    P.op("act", lambda e: e.activation(out=sp, in_=sp, func=AF.Ln, bias=1.0, scale=1.0), reads=["sp"], writes=["sp"])
    P.op("dve", lambda e: e.tensor_scalar_mul(out=sc8, in0=sp, scalar1=-8.0), reads=["sp"], writes=["sc8"])
    P.op("dve", lambda e: e.tensor_scalar_mul(out=sc16, in0=sp, scalar1=-16.0), reads=["sp"], writes=["sc16"])
    nbuf = 2
    xraw = [A.alloc("xraw%d" % i, [PADW], BF16) for i in range(nbuf)]
    graw = [A.alloc("graw%d" % i, [PADW], BF16) for i in range(nbuf)]
    for i in range(nbuf):
        P.op("pool", lambda e, i=i: e.memset(xraw[i], 0.0), writes=["xraw%d" % i])
        P.op("pool", lambda e, i=i: e.memset(graw[i], 0.0), writes=["graw%d" % i])
    xc = A.alloc("xc", [PADW]); xcb = A.alloc("xcb", [PADW], BF16)
    r = A.alloc("r", [PADW]); ig = A.alloc("ig", [PADW]); av = A.alloc("av", [PADW]); sv = A.alloc("sv", [PADW])
    hh = [A.alloc("hh%d" % i, [PADW]) for i in range(2)]
    sg = A.alloc("sg", [PADW]); yb = [A.alloc("yb%d" % i, [PADW], BF16) for i in range(2)]
    for v, nm in ((xc, "xc"), (r, "r"), (ig, "ig"), (av, "av"), (sv, "sv"), (hh[0], "hh0"), (hh[1], "hh1")):
        P.op("pool", lambda e, v=v: e.memset(v, 0.0), writes=[nm])
    cw = prm(g, "conv_b").rearrange("p (a b) -> p a b", a=4)
    cb = prm(g, "conv_b_bias")
    ba = prm(g, "lru_ba").rearrange("p (a b) -> p a b", a=2)
    bx = prm(g, "lru_bx").rearrange("p (a b) -> p a b", a=2)
    sc8v = sc8.rearrange("p (a b) -> p a b", a=2); sc16v = sc16.rearrange("p (a b) -> p a b", a=2)
    items = [(b, ct) for b in range(NB) for ct in range(4)]

    def load(k):
        b, ct = items[k]
        i = k % nbuf
        load_padded(P, xraw[i], g.PT[16 + ct][:, b * S:(b + 1) * S], "xraw%d" % i)
        load_padded(P, graw[i], g.PT[20 + ct][:, b * S:(b + 1) * S], "graw%d" % i)

    load(0)
    N0, N1 = 2, PADW - 3
    chunks = [(c0, min(c0 + 512, N1)) for c0 in range(N0, N1, 512)]
    ev = 0
    for k, (b, ct) in enumerate(items):
        i = k % nbuf
        if k + 1 < len(items):
            load(k + 1)
        conv4(P, "dve", xc, xraw[i], cw[:, ct, :], cb[:, ct:ct + 1], ["xraw%d" % i], "xc")
        P.op("act", lambda e: e.copy(out=xcb[:, N0:N1], in_=xc[:, N0:N1]), reads=["xc"], writes=["xcb"])
        P.op("act", lambda e, i=i: e.activation(out=sg[:, N0:N1], in_=graw[i][:, N0:N1], func=AF.Silu), reads=["graw%d" % i], writes=["sg"])
        for d in range(2):
            for gi, (dst, bias, nm) in enumerate(((r, ba, "r"), (ig, bx, "ig"))):
                for (c0, c1) in chunks:
                    bank = ev % 8; ev += 1
                    ps = g.ps[bank][:, 0:c1 - c0]
                    P.op("pe", lambda e, ps=ps, gi=gi, d=d, ct=ct, c0=c0, c1=c1: e.matmul(
                        ps, lhsT=wbd[:, gi * 8 + d * 4 + ct, :], rhs=xcb[:, c0:c1], start=True, stop=True),
                        reads=["lruw_b", "xcb"], writes=["lps%d" % bank])
                    P.op("act", lambda e, ps=ps, dst=dst, bias=bias, d=d, ct=ct, c0=c0, c1=c1: e.activation(
                        out=dst[:, c0:c1], in_=ps, func=AF.Sigmoid, bias=bias[:, d, ct:ct + 1], scale=1.0),
                        reads=["lps%d" % bank], writes=[nm])
            P.op("act", lambda e, d=d, ct=ct: e.activation(out=av[:, N0:N1], in_=r[:, N0:N1], func=AF.Exp, scale=sc8v[:, d, ct:ct + 1]),
                 reads=["r", "sc8"], writes=["av"])
            P.op("act", lambda e, d=d, ct=ct: e.activation(out=sv[:, N0:N1], in_=r[:, N0:N1], func=AF.Exp, scale=sc16v[:, d, ct:ct + 1]),
                 reads=["r", "sc16"], writes=["sv"])
            P.op("act", lambda e: e.activation(out=sv[:, N0:N1], in_=sv[:, N0:N1], func=AF.Sqrt, bias=1.0, scale=-1.0),
                 reads=["sv"], writes=["sv"])
            P.op("dve", lambda e: e.tensor_tensor(out=ig[:, N0:N1], in0=ig[:, N0:N1], in1=xc[:, N0:N1], op=ALU.mult), reads=["ig", "xc"], writes=["ig"])
            P.op("dve", lambda e: e.tensor_tensor(out=sv[:, N0:N1], in0=sv[:, N0:N1], in1=ig[:, N0:N1], op=ALU.mult), reads=["ig", "sv"], writes=["sv"])
            h = hh[d]
            if d == 0:
                P.op("dve", lambda e, h=h: e.tensor_tensor_scan(out=h[:, PC0:PC0 + CTX], data0=av[:, PC0:PC0 + CTX], data1=sv[:, PC0:PC0 + CTX],
                                                                initial=0.0, op0=ALU.mult, op1=ALU.add), reads=["av", "sv"], writes=["hh0"])
                P.op("dve", lambda e, h=h: e.tensor_tensor_scan(out=h[:, PL0:PL0 + LAT], data0=av[:, PL0:PL0 + LAT], data1=sv[:, PL0:PL0 + LAT],
                                                                initial=h[:, PC0 + CTX - 1:PC0 + CTX], op0=ALU.mult, op1=ALU.add),
                     reads=["av", "sv", "hh0"], writes=["hh0"])
            else:
                def rv(t, a, n):
                    return t[:, a + n - 1:a - 1:-1] if a > 0 else t[:, a + n - 1::-1]
                P.op("dve", lambda e, h=h: e.tensor_tensor_scan(out=rv(h, PC0, CTX), data0=rv(av, PC0, CTX), data1=rv(sv, PC0, CTX),
                                                                initial=0.0, op0=ALU.mult, op1=ALU.add), reads=["av", "sv"], writes=["hh1"])
                P.op("dve", lambda e, h=h: e.tensor_tensor_scan(out=rv(h, PL0, LAT), data0=rv(av, PL0, LAT), data1=rv(sv, PL0, LAT),
                                                                initial=h[:, PC0:PC0 + 1], op0=ALU.mult, op1=ALU.add),
                     reads=["av", "sv", "hh1"], writes=["hh1"])
        P.op("pool", lambda e: e.tensor_tensor(out=hh[0][:, N0:N1], in0=hh[0][:, N0:N1], in1=hh[1][:, N0:N1], op=ALU.add),
             reads=["hh0", "hh1"], writes=["hh0"])
        P.op("pool", lambda e, i=i: e.tensor_tensor(out=yb[i][:, N0:N1], in0=hh[0][:, N0:N1], in1=sg[:, N0:N1], op=ALU.mult),
             reads=["hh0", "sg"], writes=["yb%d" % i])
        store_padded(P, g.Y3[1, ct][:, b * S:(b + 1) * S], yb[i], "yb%d" % i, "Y3")


def cst(g, name):
    o, n = CST_LAY[name]
    return g.cst_sb[:, o:o + n]


def phase_gdn(g, l, last):
    nc, P, A = g.nc, g.P, g.A
    P.barrier()
    A.reset()
    import os
    if os.environ.get("GDN_LIMIT"):
        Prog.limit = int(os.environ["GDN_LIMIT"]); Prog.nrec = 0
    NTL = NT // 128
    ba = A.alloc("ba", [NTL, 16])
    bav = g.BA.rearrange("(t p) c -> p t c", p=128)
    for t0 in range(0, NTL, 6):
        P.dma("sp", lambda e, t0=t0: e.dma_start(out=ba[:, t0:t0 + 6, :], in_=bav[:, t0:t0 + 6, :]), writes=["ba"])

    def sc4(name):
        return A.alloc(name, [2, NTL, 4])
    beta = sc4("beta"); nbeta = sc4("nbeta"); gl = sc4("gl"); gc = sc4("gc"); egc = sc4("egc"); egl = sc4("egl"); egt = sc4("egt")
    nea = A.alloc("nea", [8])

    def asdth(v):
        return v.rearrange("p t (d h) -> p d t h", d=2)
    P.op("act", lambda e: e.activation(out=beta, in_=asdth(ba[:, :, 0:8]), func=AF.Sigmoid), reads=["ba"], writes=["beta"])
    P.op("dve", lambda e: e.tensor_scalar_mul(out=nbeta, in0=beta, scalar1=-1.0), reads=["beta"], writes=["nbeta"])
    dtb = prm(g, "dt_bias").rearrange("p (d h) -> p d h", d=2).unsqueeze(2).to_broadcast([128, 2, NTL, 4])
    P.op("dve", lambda e: e.tensor_tensor(out=gl, in0=asdth(ba[:, :, 8:16]), in1=dtb, op=ALU.add), reads=["ba"], writes=["gl"])
    P.op("act", lambda e: e.activation(out=gl, in_=gl, func=AF.Exp), reads=["gl"], writes=["gl"])
    P.op("act", lambda e: e.activation(out=gl, in_=gl, func=AF.Ln, bias=1.0, scale=1.0), reads=["gl"], writes=["gl"])
    P.op("act", lambda e: e.activation(out=nea, in_=prm(g, "a_log"), func=AF.Exp), reads=[], writes=["nea"])
    P.op("dve", lambda e: e.tensor_scalar_mul(out=nea, in0=nea, scalar1=-1.0), reads=["nea"], writes=["nea"])
    neab = nea.rearrange("p (d h) -> p d h", d=2).unsqueeze(2).to_broadcast([128, 2, NTL, 4])
    P.op("dve", lambda e: e.tensor_tensor(out=gl, in0=gl, in1=neab, op=ALU.mult), reads=["gl", "nea"], writes=["gl"])
    for d in range(2):
        ps = g.ps[d][:, 0:NTL * 4]
        P.op("pe", lambda e, d=d, ps=ps: e.matmul(ps, lhsT=cst(g, "tri%d" % d), rhs=gl[:, d].rearrange("p t h -> p (t h)"), start=True, stop=True),
             reads=["gl"], writes=["c0ps%d" % d])
        P.op("dve", lambda e, d=d, ps=ps: e.tensor_copy(out=gc[:, d].rearrange("p t h -> p (t h)"), in_=ps), reads=["c0ps%d" % d], writes=["gc"])
    ps = g.ps[2][:, 0:2 * NTL * 4]
    P.op("pe", lambda e: e.matmul(ps, lhsT=g.ones, rhs=gl.rearrange("p d t h -> p (d t h)"), start=True, stop=True), reads=["gl"], writes=["c0ps2"])
    flat = lambda v: v.rearrange("p d t h -> p (d t h)")
    P.op("act", lambda e: e.activation(out=flat(egt), in_=ps, func=AF.Exp), reads=["c0ps2"], writes=["egt"])
    P.op("dve", lambda e: e.tensor_tensor(out=flat(egl), in0=ps, in1=flat(gc), op=ALU.subtract), reads=["c0ps2", "gc"], writes=["egl"])
    P.op("act", lambda e: e.activation(out=egl, in_=egl, func=AF.Exp), reads=["egl"], writes=["egl"])
    P.op("act", lambda e: e.activation(out=egc, in_=gc, func=AF.Exp), reads=["gc"], writes=["egc"])

    NSLOT = 6
    P.barrier()
    if "gstop0" in g.dbg:
        return
    rope = A.alloc("rope", [2, LAT])
    P.dma("sp", lambda e: e.dma_start(out=rope, in_=g.rope.rearrange("p (a b) -> p a b", a=2)), writes=["rope"])
    raw = A.alloc("craw", [PADW], BF16)
    P.op("pool", lambda e: e.memset(raw, 0.0), writes=["craw"])
    acc = A.alloc("cacc", [PADW])
    P.op("pool", lambda e: e.memset(acc, 0.0), writes=["cacc"])
    sqb = A.alloc("csqb", [PADW], BF16)
    rinv = [A.alloc("crinv%d" % i, [512]) for i in range(2)]
    t1 = [A.alloc("ct1_%d" % i, [512]) for i in range(2)]
    QT = [A.alloc("cQT%d" % i, [S], BF16) for i in range(2)]
    KT = A.alloc("cKT", [S], BF16)
    VT = A.alloc("cVT", [S], BF16)
    Ktok = A.alloc("cKtok", [18, 128], BF16)
    Vtok = A.alloc("cVtok", [18, 128], BF16)
    Od = [A.alloc("cOd%d" % i, [18, 128]) for i in range(2)]
    graw = A.alloc("cgraw", [S], BF16)
    sg = A.alloc("csg", [S])
    onb = A.alloc("conb", [18, 128], BF16)
    ssq = A.alloc("cssq", [18])
    yaT = A.alloc("cyaT", [S], BF16)
    Sst = [A.alloc("cS%d" % i, [128]) for i in range(2)]
    Sbf = [A.alloc("cSbf%d" % i, [128], BF16) for i in range(2)]
    Ubf = [A.alloc("cU%d" % i, [128], BF16) for i in range(2)]
    O2s = [A.alloc("cO2_%d" % i, [128]) for i in range(2)]
    WkT = [A.alloc("cWkT%d" % i, [18, 128], BF16) for i in range(2)]
    Wvb = [A.alloc("cWvb%d" % i, [18, 128]) for i in range(2)]
    QKm = [A.alloc("cQKm%d" % i, [18, 128], BF16) for i in range(2)]
    Ktl = [A.alloc("cKtl%d" % i, [18, 128], BF16) for i in range(2)]
    def two(name, shape, dt=F32):
        return [A.alloc("%s%d" % (name, i), shape, dt) for i in range(NSLOT)]
    Gb = two("cGb", [128]); Em = two("cE", [128]); Dms = two("cDms", [128]); Dmi = two("cDmi", [128])
    ATp = two("cATp", [128], BF16); L0 = two("cL0_", [128], BF16); Tm = two("cT", [128], BF16); TT = two("cTT", [128], BF16)
    Yb = two("cYb", [128], BF16); ZTs = two("cZTs", [128], BF16); Xk = two("cXk", [128], BF16)
    cw = prm(g, "conv_a").rearrange("p (a b) -> p a b", a=12)
    N0, N1 = 2, PADW - 3
    chunks = [(c0, min(c0 + 512, N1)) for c0 in range(N0, N1, 512)]
    state = {"ev": 0, "ch": 0}

    def c1(b, h, qi):
        for which in range(3):
            ct = which * 4 + h
            load_padded(P, raw, g.PT[ct][:, b * S:(b + 1) * S], "craw")
            conv4(P, "dve", acc, raw, cw[:, ct, :], None, ["craw"], "cacc")
            P.op("act", lambda e: e.activation(out=acc[:, N0:N1], in_=acc[:, N0:N1], func=AF.Silu), reads=["cacc"], writes=["cacc"])
            if which == 2:
                P.op("act", lambda e: e.copy(out=VT[:, 0:CTX], in_=acc[:, PC0:PC0 + CTX]), reads=["cacc"], writes=["cVT"])
                P.op("act", lambda e: e.copy(out=VT[:, CTX:S], in_=acc[:, PL0:PL0 + LAT]), reads=["cacc"], writes=["cVT"])
                continue
            P.op("pool", lambda e: e.tensor_tensor(out=sqb[:, N0:N1], in0=acc[:, N0:N1], in1=acc[:, N0:N1], op=ALU.mult), reads=["cacc"], writes=["csqb"])
            for ci, (c0, c1_) in enumerate(chunks):
                bank = 6 + state["ev"] % 2; state["ev"] += 1
                ps = g.ps[bank][:, 0:c1_ - c0]
                ri = rinv[ci % 2][:, 0:c1_ - c0]
                P.op("pe", lambda e, ps=ps, c0=c0, c1_=c1_: e.matmul(ps, lhsT=g.ones_bf, rhs=sqb[:, c0:c1_], start=True, stop=True),
                     reads=["csqb"], writes=["cps%d" % bank])
                rsqrt_act(P, ri, ps, EPS, ["cps%d" % bank], ["crinv%d" % (ci % 2)])
                if which == 0:
                    P.op("dve", lambda e, ri=ri, c0=c0, c1_=c1_: e.scalar_tensor_tensor(out=acc[:, c0:c1_], in0=acc[:, c0:c1_], scalar=float(128.0 ** -0.5),
                                                                                      in1=ri, op0=ALU.mult, op1=ALU.mult),
                         reads=["cacc", "crinv%d" % (ci % 2)], writes=["cacc"])
                else:
                    P.op("dve", lambda e, ri=ri, c0=c0, c1_=c1_: e.tensor_tensor(out=acc[:, c0:c1_], in0=acc[:, c0:c1_], in1=ri, op=ALU.mult),
                         reads=["cacc", "crinv%d" % (ci % 2)], writes=["cacc"])
            for ci in range(4):
                bank = 6 + state["ev"] % 2; state["ev"] += 1
                ps = g.ps[bank]
                a0 = PL0 + ci * 512
                P.op("pe", lambda e, ps=ps, a0=a0: e.matmul(ps, lhsT=cst(g, "prot"), rhs=acc[:, a0:a0 + 512], start=True, stop=True),
                     reads=["cacc"], writes=["cps%d" % bank])
                tt = t1[ci % 2]
                P.op("dve", lambda e, ps=ps, tt=tt, ci=ci: e.tensor_tensor(out=tt, in0=ps, in1=rope[:, 1, ci * 512:(ci + 1) * 512], op=ALU.mult),
                     reads=["cps%d" % bank, "rope"], writes=["ct1_%d" % (ci % 2)])
                P.op("pool", lambda e, a0=a0, ci=ci: e.tensor_tensor(out=acc[:, a0:a0 + 512], in0=acc[:, a0:a0 + 512], in1=rope[:, 0, ci * 512:(ci + 1) * 512], op=ALU.mult),
                     reads=["cacc", "rope", "cps%d" % bank], writes=["cacc"])
                P.op("pool", lambda e, a0=a0, tt=tt: e.tensor_tensor(out=acc[:, a0:a0 + 512], in0=acc[:, a0:a0 + 512], in1=tt, op=ALU.add),
                     reads=["cacc", "ct1_%d" % (ci % 2)], writes=["cacc"])
            dst = QT[qi] if which == 0 else KT
            dn = ("cQT%d" % qi) if which == 0 else "cKT"
            P.op("act", lambda e, dst=dst: e.copy(out=dst[:, 0:CTX], in_=acc[:, PC0:PC0 + CTX]), reads=["cacc"], writes=[dn])
            P.op("act", lambda e, dst=dst: e.copy(out=dst[:, CTX:S], in_=acc[:, PL0:PL0 + LAT]), reads=["cacc"], writes=[dn])
        for src, dstt, sn, dn in ((KT, Ktok, "cKT", "cKtok"), (VT, Vtok, "cVT", "cVtok")):
            for g0 in range(0, 18, 8):
                n = min(8, 18 - g0)
                bank = 6 + state["ev"] % 2; state["ev"] += 1
                pb = g.ps[bank].bitcast(BF16)
                for t in range(n):
                    P.op("pe", lambda e, pb=pb, t=t, g0=g0, src=src: e.transpose(pb[:, t * 128:(t + 1) * 128], src[:, (g0 + t) * 128:(g0 + t + 1) * 128], g.ident_bf),
                         reads=[sn], writes=["cps%d" % bank])
                evac(P, state["ev"], dstt[:, g0:g0 + n, :].rearrange("p a b -> p (a b)"), pb[:, 0:n * 128], ["cps%d" % bank], [dn])

    def chain(b, h, d, t, ip, qi, c):
        T = b * 18 + t
        col = lambda v: v[:, d, T, h:h + 1]
        sl = slice(t * 128, (t + 1) * 128)
        ps = g.ps[c]
        blk = [ps[:, k * 128:(k + 1) * 128] for k in range(4)]
        pn = ["cpA%d_%d" % (c, k) for k in range(4)]
        P.op("pool", lambda e: e.tensor_copy(out=Gb[c], in_=col(gl).to_broadcast([128, 128])), reads=["gl"], writes=["cGb%d" % c])
        yield
        P.op("pe", lambda e: e.matmul(blk[0], lhsT=Gb[c], rhs=cst(g, "tri%d" % d), start=True, stop=True), reads=["cGb%d" % c], writes=[pn[0]])
        P.op("pe", lambda e: e.matmul(blk[1], lhsT=KT[:, sl], rhs=KT[:, sl], start=True, stop=True), reads=["cKT"], writes=[pn[1]])
        P.op("pe", lambda e: e.matmul(blk[2], lhsT=KT[:, sl], rhs=QT[qi][:, sl], start=True, stop=True), reads=["cKT", "cQT%d" % qi], writes=[pn[2]])
        yield
        P.op("dve", lambda e: e.scalar_tensor_tensor(out=Em[c], in0=blk[0], scalar=col(gc), in1=cst(g, "mit%d" % d), op0=ALU.subtract, op1=ALU.mult),
             reads=[pn[0], "gc"], writes=["cE%d" % c])
        yield
        P.op("act", lambda e: e.activation(out=Em[c], in_=Em[c], func=AF.Exp), reads=["cE%d" % c], writes=["cE%d" % c])
        P.op("act", lambda e: e.activation(out=Xk[c], in_=Ktok[:, t, :], func=AF.Copy, scale=col(egc)), reads=["cKtok", "egc"], writes=["cXk%d" % c])
        P.op("act", lambda e: e.activation(out=Ktl[ip][:, t, :], in_=Ktok[:, t, :], func=AF.Copy, scale=col(egl)), reads=["cKtok", "egl"], writes=["cKtl%d_%d" % (ip, t)])
        yield
        P.op("pool", lambda e: e.tensor_tensor(out=Dms[c], in0=Em[c], in1=cst(g, "mst%d" % d), op=ALU.mult), reads=["cE%d" % c], writes=["cDms%d" % c])
        P.op("pool", lambda e: e.tensor_tensor(out=Dmi[c], in0=Em[c], in1=cst(g, "mit%d" % d), op=ALU.mult), reads=["cE%d" % c], writes=["cDmi%d" % c])
        yield
        P.op("dve", lambda e: e.scalar_tensor_tensor(out=ATp[c], in0=blk[1], scalar=col(beta), in1=Dms[c], op0=ALU.mult, op1=ALU.mult),
             reads=[pn[1], "beta", "cDms%d" % c], writes=["cATp%d" % c])
        P.op("dve", lambda e: e.tensor_tensor(out=QKm[ip][:, t, :], in0=blk[2], in1=Dmi[c], op=ALU.mult),
             reads=[pn[2], "cDmi%d" % c], writes=["cQKm%d_%d" % (ip, t)])
        yield
        P.op("pool", lambda e: e.tensor_tensor(out=L0[c], in0=ATp[c], in1=cst(g, "lm0"), op=ALU.mult), reads=["cATp%d" % c], writes=["cL0_%d" % c])
        P.op("pool", lambda e: e.tensor_tensor(out=TT[c], in0=g.ident_bf, in1=L0[c], op=ALU.subtract), reads=["cL0_%d" % c], writes=["cTT%d" % c])
        yield
        pb3 = blk[3].bitcast(BF16)
        P.op("pe", lambda e: e.transpose(pb3[:, 0:128], L0[c], g.ident_bf), reads=["cL0_%d" % c], writes=[pn[3]])
        yield
        P.op("dve", lambda e: e.tensor_tensor(out=Tm[c], in0=g.ident_bf, in1=pb3[:, 0:128], op=ALU.subtract), reads=[pn[3]], writes=["cT%d" % c])
        yield
        for lv in range(1, 7):
            lastlv = (lv == 6)
            P.op("pe", lambda e: e.matmul(blk[0], lhsT=ATp[c], rhs=Tm[c], start=True, stop=True), reads=["cATp%d" % c, "cT%d" % c], writes=[pn[0]])
            yield
            P.op("dve", lambda e, lv=lv: e.tensor_tensor(out=Yb[c], in0=blk[0], in1=cst(g, "lm%d" % lv), op=ALU.mult), reads=[pn[0]], writes=["cYb%d" % c])
            yield
            if not lastlv:
                P.op("pe", lambda e: e.matmul(blk[1], lhsT=TT[c], rhs=Yb[c], start=True, stop=True), reads=["cTT%d" % c, "cYb%d" % c], writes=[pn[1]])
            P.op("pe", lambda e: e.matmul(blk[2], lhsT=Yb[c], rhs=TT[c], start=True, stop=True), reads=["cTT%d" % c, "cYb%d" % c], writes=[pn[2]])
            yield
            P.op("act", lambda e: e.copy(out=ZTs[c], in_=blk[2]), reads=[pn[2]], writes=["cZTs%d" % c])
            if not lastlv:
                P.op("dve", lambda e: e.tensor_tensor(out=Tm[c], in0=Tm[c], in1=blk[1], op=ALU.subtract), reads=["cT%d" % c, pn[1]], writes=["cT%d" % c])
            yield
            P.op("pool", lambda e: e.tensor_tensor(out=TT[c], in0=TT[c], in1=ZTs[c], op=ALU.subtract), reads=["cTT%d" % c, "cZTs%d" % c], writes=["cTT%d" % c])
            yield
        P.op("pe", lambda e: e.matmul(blk[0], lhsT=TT[c], rhs=Vtok[:, t, :], start=True, stop=True), reads=["cTT%d" % c, "cVtok"], writes=[pn[0]])
        P.op("pe", lambda e: e.matmul(blk[1], lhsT=Xk[c], rhs=TT[c], start=True, stop=True), reads=["cTT%d" % c, "cXk%d" % c], writes=[pn[1]])
        yield
        P.op("act", lambda e: e.activation(out=Wvb[ip][:, t, :], in_=blk[0], func=AF.Copy, scale=col(beta)), reads=[pn[0], "beta"], writes=["cWvb%d_%d" % (ip, t)])
        P.op("act", lambda e: e.copy(out=WkT[ip][:, t, :], in_=blk[1]), reads=[pn[1]], writes=["cWkT%d_%d" % (ip, t)])
        yield

    def steps(b, h, d, ip, qi, tiles):
        si = ip
        ps6 = g.ps[6]; ps7 = g.ps[7]
        P.op("pool", lambda e: e.memset(Sst[si], 0.0), writes=["cS%d" % si])
        P.op("pool", lambda e: e.memset(Sbf[si], 0.0), writes=["cSbf%d" % si])
        yield
        for t in tiles:
            T = b * 18 + t
            col = lambda v, T=T: v[:, d, T, h:h + 1]
            sl = slice(t * 128, (t + 1) * 128)
            P.op("pe", lambda e, t=t: e.matmul(ps6[:, 0:128], lhsT=WkT[ip][:, t, :], rhs=Sbf[si], start=True, stop=True),
                 reads=["cWkT%d_%d" % (ip, t), "cSbf%d" % si], writes=["cp6_0"])
            P.op("pe", lambda e, sl=sl: e.matmul(ps7[:, 0:128], lhsT=QT[qi][:, sl], rhs=Sbf[si], start=True, stop=True), reads=["cQT%d" % qi, "cSbf%d" % si], writes=["cp7_1"])
            yield
            P.op("dve", lambda e, t=t, col=col: e.scalar_tensor_tensor(out=Ubf[si], in0=ps6[:, 0:128], scalar=col(nbeta), in1=Wvb[ip][:, t, :], op0=ALU.mult, op1=ALU.add),
                 reads=["cp6_0", "nbeta", "cWvb%d_%d" % (ip, t)], writes=["cU%d" % si])
            yield
            P.op("pe", lambda e, t=t: e.matmul(ps6[:, 256:384], lhsT=QKm[ip][:, t, :], rhs=Ubf[si], start=True, stop=True), reads=["cQKm%d_%d" % (ip, t), "cU%d" % si], writes=["cp6_2"])
            P.op("pe", lambda e, t=t: e.matmul(ps6[:, 384:512], lhsT=Ktl[ip][:, t, :], rhs=Ubf[si], start=True, stop=True), reads=["cKtl%d_%d" % (ip, t), "cU%d" % si], writes=["cp6_3"])
            yield
            P.op("act", lambda e: e.copy(out=O2s[si], in_=ps6[:, 256:384]), reads=["cp6_2"], writes=["cO2_%d" % si])
            P.op("dve", lambda e, col=col: e.scalar_tensor_tensor(out=Sst[si], in0=Sst[si], scalar=col(egt), in1=ps6[:, 384:512], op0=ALU.mult, op1=ALU.add),
                 reads=["cS%d" % si, "egt", "cp6_3"], writes=["cS%d" % si])
            yield
            P.op("act", lambda e: e.copy(out=Sbf[si], in_=Sst[si]), reads=["cS%d" % si], writes=["cSbf%d" % si])
            P.op("dve", lambda e, t=t, col=col: e.scalar_tensor_tensor(out=Od[d][:, t, :], in0=ps7[:, 0:128], scalar=col(egc), in1=O2s[si], op0=ALU.mult, op1=ALU.add),
                 reads=["cp7_1", "egc", "cO2_%d" % si], writes=["cOd%d_%d" % (d, t)])
            yield

    def finalize(b, h):
        odr = ["cOd%d_%d" % (d, t) for d in range(2) for t in range(18)]
        P.dma("sp", lambda e: e.dma_start(out=graw, in_=g.PT[12 + h][:, b * S:(b + 1) * S]), writes=["cgraw"])
        P.op("pool", lambda e: e.tensor_tensor(out=Od[0], in0=Od[0], in1=Od[1], op=ALU.add), reads=odr, writes=["cOsum"] + odr[:18])
        P.op("pool", lambda e: e.tensor_tensor(out=Od[1], in0=Od[0], in1=Od[0], op=ALU.mult), reads=["cOsum"], writes=["cOsq"] + odr[18:])
        P.op("dve", lambda e: e.reduce_sum(out=ssq, in_=Od[1], axis=AX.X), reads=["cOsq"], writes=["cssq"] + odr[18:])
        P.op("act", lambda e: e.activation(out=ssq, in_=ssq, func=AF.Ln, bias=EPS, scale=1.0 / 128), reads=["cssq"], writes=["cssq"])
        P.op("act", lambda e: e.activation(out=ssq, in_=ssq, func=AF.Exp, scale=-0.5), reads=["cssq"], writes=["cssq"])
        P.op("pool", lambda e: e.tensor_tensor(out=onb, in0=Od[0], in1=ssq.unsqueeze(2).to_broadcast([128, 18, 128]), op=ALU.mult),
             reads=["cOsum", "cssq"], writes=["conb"] + odr[:18])
        P.op("act", lambda e: e.activation(out=sg, in_=graw, func=AF.Silu), reads=["cgraw"], writes=["csg"])
        for g0 in range(0, 18, 8):
            n = min(8, 18 - g0)
            state["ev"] += 1
            bank = 6 + (state["ev"] % 2)
            pb = g.ps[bank].bitcast(BF16)
            for t in range(n):
                P.op("pe", lambda e, pb=pb, t=t, g0=g0: e.transpose(pb[:, t * 128:(t + 1) * 128], onb[:, g0 + t, :], g.ident_bf),
                     reads=["conb"], writes=["cps%d" % bank])
            P.op("dve", lambda e, pb=pb, g0=g0, n=n: e.scalar_tensor_tensor(out=yaT[:, g0 * 128:(g0 + n) * 128], in0=pb[:, 0:n * 128], scalar=prm(g, "onorm_a"),
                                                                           in1=sg[:, g0 * 128:(g0 + n) * 128], op0=ALU.mult, op1=ALU.mult),
                 reads=["cps%d" % bank, "csg"], writes=["cyaT"])
        P.dma("sp", lambda e: e.dma_start(out=g.Y3[0, h][:, b * S:(b + 1) * S], in_=yaT), reads=["cyaT"], writes=["Y3"])

    def order(d):
        return list(range(18)) if d == 0 else [1, 0] + list(range(17, 1, -1))
    items = [(b, h, d) for b in range(NB) for h in range(4) for d in range(2)]
    chain_ctr = [0]
    for k in range(len(items) + 1):
        nxt = items[k] if k < len(items) else None
        cur = items[k - 1] if k >= 1 else None
        if nxt is not None and nxt[2] == 0:
            c1(nxt[0], nxt[1], (k // 2) % 2)
        pending = []
        if nxt is not None:
            for t in order(nxt[2]):
                pending.append((nxt[0], nxt[1], nxt[2], t, k % 2, (k // 2) % 2))
        active = []
        stepgen = steps(cur[0], cur[1], cur[2], (k - 1) % 2, ((k - 1) // 2) % 2, order(cur[2])) if cur is not None else None
        while pending or active or stepgen is not None:
            while pending and len(active) < NSLOT:
                used = [a_[1] for a_ in active]
                slot = [c_ for c_ in range(NSLOT) if c_ not in used][0]
                args = pending.pop(0)
                active.append((chain(*args, slot), slot))
            for a_ in list(active):
                try:
                    next(a_[0])
                except StopIteration:
                    active.remove(a_)
            if stepgen is not None:
                try:
                    next(stepgen)
                except StopIteration:
                    stepgen = None
        if cur is not None and cur[2] == 1:
            finalize(cur[0], cur[1])


NEG = -30000.0


def natten_pat(i):
    return 0 if i == 0 else 1 if i == 1 else 3 if i == 14 else 4 if i == 15 else 2


def phase_natten_rr(g, l, last):
    nc, P, A = g.nc, g.P, g.A
    P.barrier()
    A.reset()
    SCALE = 128.0 ** -0.5
    NS = 4
    bias = A.alloc("nbias", [5, 640])
    qT = [A.alloc("nq%d" % i, [S], BF16) for i in range(2)]
    kT = [A.alloc("nk%d" % i, [S], BF16) for i in range(2)]
    vt = [A.alloc("nv%d" % i, [18, 128], BF16) for i in range(2)]
    gr = [A.alloc("ng%d" % i, [S], BF16) for i in range(2)]
    sgt = [A.alloc("nsg%d" % i, [S]) for i in range(2)]
    yc = [A.alloc("nyc%d" % i, [S], BF16) for i in range(2)]
    Sb = [A.alloc("nSb%d" % i, [896]) for i in range(NS)]
    Pe = [A.alloc("nPe%d" % i, [896]) for i in range(NS)]
    Pn = [A.alloc("nPn%d" % i, [896], BF16) for i in range(NS)]
    PTs = [A.alloc("nPT%d" % i, [7, 128], BF16) for i in range(NS)]
    st = [A.alloc("nst%d" % i, [4]) for i in range(NS)]
    items = [(h, b) for h in range(4) for b in range(NB)]
    tiles = list(range(16)) + ([] if last else [16, 17])

    def load(k):
        h, b = items[k]
        i = k % 2
        P.dma("sp", lambda e: e.dma_start(out=qT[i], in_=g.PT[24 + h][:, b * S:(b + 1) * S]), writes=["nq%d" % i])
        P.dma("sp", lambda e: e.dma_start(out=kT[i], in_=g.PT[28 + h][:, b * S:(b + 1) * S]), writes=["nk%d" % i])
        P.dma("sp", lambda e: e.dma_start(out=gr[i], in_=g.PT[36 + h][:, b * S:(b + 1) * S]), writes=["ng%d" % i])
        P.dma("sp", lambda e: e.dma_start(out=vt[i], in_=g.VC[b * S:(b + 1) * S, h * 128:(h + 1) * 128].rearrange("(t p) d -> p t d", p=128)),
              writes=["nv%d" % i])
        P.op("act", lambda e, i=i: e.activation(out=sgt[i], in_=gr[i], func=AF.Silu), reads=["ng%d" % i], writes=["nsg%d" % i])

    def store(k):
        h, b = items[k]
        i = k % 2
        if last:
            P.dma("sp", lambda e: e.dma_start(out=g.Y3[2, h][:, b * S + CTX:(b + 1) * S], in_=yc[i][:, CTX:S]), reads=["nyc%d_%d" % (i, t_) for t_ in tiles], writes=["Y3"])
        else:
            P.dma("sp", lambda e: e.dma_start(out=g.Y3[2, h][:, b * S:(b + 1) * S], in_=yc[i]), reads=["nyc%d_%d" % (i, t_) for t_ in tiles], writes=["Y3"])

    def tile(k, ti, j):
        h, b = items[k]
        i = k % 2
        ctxq = ti >= 16
        if not ctxq:
            q0 = CTX + ti * 128
            sr = min(max(2 * ti - 4, 0), 22)
            k0 = CTX + sr * 64
            nk = 896
            vtiles = [2 + sr // 2 + m for m in range(5)] + [0, 1]
        else:
            q0 = (ti - 16) * 128
            nk = 256
            vtiles = [0, 1]
        bk = 2 * j
        base = bk * 512
        rd = ["nq%d" % i, "nk%d" % i]
        sname = "nS%d" % j
        swr = [sname, "nPTp%d" % j, "no%d" % j]
        if not ctxq:
            P.op("pe", lambda e: e.matmul(g.psall[:, base:base + 512], lhsT=qT[i][:, q0:q0 + 128], rhs=kT[i][:, k0:k0 + 512], start=True, stop=True), reads=rd, writes=swr)
            P.op("pe", lambda e: e.matmul(g.psall[:, base + 512:base + 640], lhsT=qT[i][:, q0:q0 + 128], rhs=kT[i][:, k0 + 512:k0 + 640], start=True, stop=True), reads=rd, writes=swr)
            P.op("pe", lambda e: e.matmul(g.psall[:, base + 640:base + 896], lhsT=qT[i][:, q0:q0 + 128], rhs=kT[i][:, 0:CTX], start=True, stop=True), reads=rd, writes=swr)
            yield
            pat = natten_pat(ti)
            P.op("dve", lambda e: e.scalar_tensor_tensor(out=Sb[j][:, 0:640], in0=g.psall[:, base:base + 640], scalar=SCALE, in1=bias[:, pat, :],
                                                         op0=ALU.mult, op1=ALU.add), reads=[sname, "nbias"], writes=["nSb%d" % j])
            P.op("dve", lambda e: e.tensor_scalar_mul(out=Sb[j][:, 640:896], in0=g.psall[:, base + 640:base + 896], scalar1=SCALE), reads=[sname], writes=["nSb%d" % j])
        else:
            P.op("pe", lambda e: e.matmul(g.psall[:, base:base + 256], lhsT=qT[i][:, q0:q0 + 128], rhs=kT[i][:, 0:CTX], start=True, stop=True), reads=rd, writes=swr)
            yield
            P.op("dve", lambda e: e.tensor_scalar_mul(out=Sb[j][:, 0:256], in0=g.psall[:, base:base + 256], scalar1=SCALE), reads=[sname], writes=["nSb%d" % j])
        sj = st[j]
        P.op("pool", lambda e: e.memset(sj[:, 2:3], 0.0), writes=["nst%d" % j])
        yield
        P.op("dve", lambda e: e.reduce_max(out=sj[:, 0:1], in_=Sb[j][:, 0:nk], axis=AX.X), reads=["nSb%d" % j], writes=["nst%d" % j])
        yield
        P.op("dve", lambda e: e.tensor_scalar_mul(out=sj[:, 1:2], in0=sj[:, 0:1], scalar1=-1.0), reads=["nst%d" % j], writes=["nst%d" % j])
        yield
        P.op("act", lambda e: e.activation(out=Pe[j][:, 0:nk], in_=Sb[j][:, 0:nk], func=AF.Exp, bias=sj[:, 1:2], scale=1.0,
                                           accum_out=sj[:, 2:3]), reads=["nSb%d" % j, "nst%d" % j], writes=["nPe%d" % j, "nst%d" % j])
        yield
        P.op("dve", lambda e: e.reciprocal(out=sj[:, 3:4], in_=sj[:, 2:3]), reads=["nst%d" % j], writes=["nst%d" % j])
        yield
        P.op("act", lambda e: e.activation(out=Pn[j][:, 0:nk], in_=Pe[j][:, 0:nk], func=AF.Copy, scale=sj[:, 3:4]),
             reads=["nPe%d" % j, "nst%d" % j], writes=["nPn%d" % j])
        yield
        nkt = nk // 128
        ptp = g.ps[bk + 1].bitcast(BF16)
        for kt in range(nkt):
            P.op("pe", lambda e, kt=kt: e.transpose(ptp[:, kt * 128:(kt + 1) * 128], Pn[j][:, kt * 128:(kt + 1) * 128], g.ident_bf),
                 reads=["nPn%d" % j], writes=["nPTp%d" % j, sname])
        yield
        P.op("dve", lambda e: e.tensor_copy(out=PTs[j].rearrange("p a b -> p (a b)")[:, 0:nk], in_=ptp[:, 0:nk]), reads=["nPTp%d" % j], writes=["nPT%d" % j])
        yield
        ops = g.ps[bk][:, 0:128]
        for kt in range(nkt):
            P.op("pe", lambda e, kt=kt: e.matmul(ops, lhsT=vt[i][:, vtiles[kt], :], rhs=PTs[j][:, kt, :], start=(kt == 0), stop=(kt == nkt - 1)),
                 reads=["nv%d" % i, "nPT%d" % j], writes=["no%d" % j, sname])
        yield
        P.op("dve", lambda e: e.tensor_tensor(out=yc[i][:, q0:q0 + 128], in0=ops, in1=sgt[i][:, q0:q0 + 128], op=ALU.mult),
             reads=["no%d" % j, "nsg%d" % i], writes=["nyc%d_%d" % (i, ti)])
        yield

    load(0)
    pending = [(k, ti) for k in range(len(items)) for ti in tiles]
    remaining = {k: len(tiles) for k in range(len(items))}
    started = set()
    active = []
    loaded = {0}
    while pending or active:
        while pending and len(active) < NS:
            k, ti = pending[0]
            if k not in loaded:
                if k >= 2 and remaining[k - 2] > 0:
                    break
                load(k); loaded.add(k)
            if k not in started:
                h, b = items[k]
                if b == 0:
                    if k >= 1 and remaining[k - 1] > 0:
                        break
                    P.dma("sp", lambda e, h=h: e.dma_start(out=bias, in_=g.rpbias[l, h].rearrange("p (a b) -> p a b", a=5)), writes=["nbias"])
                started.add(k)
            pending.pop(0)
            used = [a_[1] for a_ in active]
            slot = [c_ for c_ in range(NS) if c_ not in used][0]
            active.append((tile(k, ti, slot), slot, k))
        for a_ in list(active):
            try:
                next(a_[0])
            except StopIteration:
                active.remove(a_)
                kk = a_[2]
                remaining[kk] -= 1
                if remaining[kk] == 0:
                    store(kk)


def phase_natten(g, l, last):
    nc, P, A = g.nc, g.P, g.A
    P.barrier()
    A.reset()
    SCALE = 128.0 ** -0.5
    bias = A.alloc("nbias", [5, 640])
    qT = [A.alloc("nq%d" % i, [S], BF16) for i in range(2)]
    kT = [A.alloc("nk%d" % i, [S], BF16) for i in range(2)]
    vt = [A.alloc("nv%d" % i, [18, 128], BF16) for i in range(2)]
    gr = [A.alloc("ng%d" % i, [S], BF16) for i in range(2)]
    sgt = A.alloc("nsg", [S])
    yc = [A.alloc("nyc%d" % i, [S], BF16) for i in range(2)]
    Sb = [A.alloc("nSb%d" % i, [896]) for i in range(2)]
    Pe = [A.alloc("nPe%d" % i, [896]) for i in range(2)]
    Pn = [A.alloc("nPn%d" % i, [896], BF16) for i in range(2)]
    PTs = [A.alloc("nPT%d" % i, [7, 128], BF16) for i in range(2)]
    st = [A.alloc("nst%d" % i, [4]) for i in range(2)]
    items = [(h, b) for h in range(4) for b in range(NB)]

    def load(k):
        h, b = items[k]
        i = k % 2
        P.dma("sp", lambda e: e.dma_start(out=qT[i], in_=g.PT[24 + h][:, b * S:(b + 1) * S]), writes=["nq%d" % i])
        P.dma("sp", lambda e: e.dma_start(out=kT[i], in_=g.PT[28 + h][:, b * S:(b + 1) * S]), writes=["nk%d" % i])
        P.dma("sp", lambda e: e.dma_start(out=gr[i], in_=g.PT[36 + h][:, b * S:(b + 1) * S]), writes=["ng%d" % i])
        P.dma("sp", lambda e: e.dma_start(out=vt[i], in_=g.VC[b * S:(b + 1) * S, h * 128:(h + 1) * 128].rearrange("(t p) d -> p t d", p=128)),
              writes=["nv%d" % i])

    load(0)
    cnt = 0
    for k, (h, b) in enumerate(items):
        i = k % 2
        if b == 0:
            P.dma("sp", lambda e, h=h: e.dma_start(out=bias, in_=g.rpbias[l, h].rearrange("p (a b) -> p a b", a=5)), writes=["nbias"])
        if k + 1 < len(items):
            load(k + 1)
        P.op("act", lambda e, i=i: e.activation(out=sgt, in_=gr[i], func=AF.Silu), reads=["ng%d" % i], writes=["nsg"])
        tiles = list(range(16)) + ([] if last else [16, 17])
        for ti in tiles:
            j = cnt % 2; cnt += 1
            ctxq = ti >= 16
            if not ctxq:
                q0 = CTX + ti * 128
                sr = min(max(2 * ti - 4, 0), 22)
                k0 = CTX + sr * 64
                nk = 896
                vtiles = [2 + sr // 2 + m for m in range(5)] + [0, 1]
            else:
                q0 = (ti - 16) * 128
                nk = 256
                vtiles = [0, 1]
            bk = 2 * j
            Sps = g.psall[:, bk * 512: bk * 512 + nk]
            rd = ["nq%d" % i, "nk%d" % i]
            if not ctxq:
                P.op("pe", lambda e, i=i, q0=q0, k0=k0, bk=bk: e.matmul(g.psall[:, bk * 512:bk * 512 + 512], lhsT=qT[i][:, q0:q0 + 128],
                                                                       rhs=kT[i][:, k0:k0 + 512], start=True, stop=True), reads=rd, writes=["nS%d" % j])
                P.op("pe", lambda e, i=i, q0=q0, k0=k0, bk=bk: e.matmul(g.psall[:, bk * 512 + 512:bk * 512 + 640], lhsT=qT[i][:, q0:q0 + 128],
                                                                       rhs=kT[i][:, k0 + 512:k0 + 640], start=True, stop=True), reads=rd, writes=["nS%d" % j])
                P.op("pe", lambda e, i=i, q0=q0, bk=bk: e.matmul(g.psall[:, bk * 512 + 640:bk * 512 + 896], lhsT=qT[i][:, q0:q0 + 128],
                                                                rhs=kT[i][:, 0:CTX], start=True, stop=True), reads=rd, writes=["nS%d" % j])
                pat = natten_pat(ti)
                P.op("dve", lambda e, j=j, pat=pat, bk=bk: e.scalar_tensor_tensor(
                    out=Sb[j][:, 0:640], in0=g.psall[:, bk * 512:bk * 512 + 640], scalar=SCALE, in1=bias[:, pat, :],
                    op0=ALU.mult, op1=ALU.add), reads=["nS%d" % j, "nbias"], writes=["nSb%d" % j])
                P.op("act", lambda e, j=j, bk=bk: e.activation(out=Sb[j][:, 640:896], in_=g.psall[:, bk * 512 + 640:bk * 512 + 896],
                                                               func=AF.Copy, scale=SCALE), reads=["nS%d" % j], writes=["nSb%d" % j])
            else:
                P.op("pe", lambda e, i=i, q0=q0, bk=bk: e.matmul(g.psall[:, bk * 512:bk * 512 + 256], lhsT=qT[i][:, q0:q0 + 128],
                                                                rhs=kT[i][:, 0:CTX], start=True, stop=True), reads=rd, writes=["nS%d" % j])
                P.op("act", lambda e, j=j, bk=bk: e.activation(out=Sb[j][:, 0:256], in_=g.psall[:, bk * 512:bk * 512 + 256],
                                                               func=AF.Copy, scale=SCALE), reads=["nS%d" % j], writes=["nSb%d" % j])
            sj = st[j]
            P.op("dve", lambda e, j=j, nk=nk, sj=sj: e.reduce_max(out=sj[:, 0:1], in_=Sb[j][:, 0:nk], axis=AX.X), reads=["nSb%d" % j], writes=["nst%d" % j])
            P.op("dve", lambda e, sj=sj: e.tensor_scalar_mul(out=sj[:, 1:2], in0=sj[:, 0:1], scalar1=-1.0), reads=["nst%d" % j], writes=["nst%d" % j])
            P.op("pool", lambda e, sj=sj: e.memset(sj[:, 2:3], 0.0), writes=["nst%d" % j])
            P.op("act", lambda e, j=j, nk=nk, sj=sj: e.activation(out=Pe[j][:, 0:nk], in_=Sb[j][:, 0:nk], func=AF.Exp, bias=sj[:, 1:2], scale=1.0,
                                                                 accum_out=sj[:, 2:3]), reads=["nSb%d" % j, "nst%d" % j], writes=["nPe%d" % j, "nst%d" % j])
            P.op("dve", lambda e, sj=sj: e.reciprocal(out=sj[:, 3:4], in_=sj[:, 2:3]), reads=["nst%d" % j], writes=["nst%d" % j])
            P.op("act", lambda e, j=j, nk=nk, sj=sj: e.activation(out=Pn[j][:, 0:nk], in_=Pe[j][:, 0:nk], func=AF.Copy, scale=sj[:, 3:4]),
                 reads=["nPe%d" % j, "nst%d" % j], writes=["nPn%d" % j])
            nkt = nk // 128
            ptp = g.ps[4 + j].bitcast(BF16)
            for kt in range(nkt):
                P.op("pe", lambda e, j=j, kt=kt, ptp=ptp: e.transpose(ptp[:, kt * 128:(kt + 1) * 128], Pn[j][:, kt * 128:(kt + 1) * 128], g.ident_bf),
                     reads=["nPn%d" % j], writes=["nQTp%d" % j])
            evac(P, cnt, PTs[j].rearrange("p a b -> p (a b)")[:, 0:nk], ptp[:, 0:nk], ["nQTp%d" % j], ["nPT%d" % j])
            ops = g.ps[6 + j][:, 0:128]
            for kt in range(nkt):
                P.op("pe", lambda e, i=i, j=j, kt=kt, vti=vtiles[kt], ops=ops, nkt=nkt: e.matmul(
                    ops, lhsT=vt[i][:, vti, :], rhs=PTs[j][:, kt, :], start=(kt == 0), stop=(kt == nkt - 1)),
                    reads=["nv%d" % i, "nPT%d" % j], writes=["nqo%d" % j])
            P.op("dve", lambda e, i=i, q0=q0, ops=ops: e.tensor_tensor(out=yc[i][:, q0:q0 + 128], in0=ops, in1=sgt[:, q0:q0 + 128], op=ALU.mult),
                 reads=["nqo%d" % j, "nsg"], writes=["nyc%d" % i])
        if last:
            P.dma("sp", lambda e, i=i, h=h, b=b: e.dma_start(out=g.Y3[2, h][:, b * S + CTX:(b + 1) * S], in_=yc[i][:, CTX:S]), reads=["nyc%d" % i], writes=["Y3"])
        else:
            P.dma("sp", lambda e, i=i, h=h, b=b: e.dma_start(out=g.Y3[2, h][:, b * S:(b + 1) * S], in_=yc[i]), reads=["nyc%d" % i], writes=["Y3"])


def phase_merge(g, l, last):
    nc, P, A = g.nc, g.P, g.A
    P.barrier()
    A.reset()
    wbr = A.alloc("wbr", [12, D], BF16)
    wo = A.alloc("wo", [8, D], BF16)
    stg = [A.alloc("mstg%d" % i, [4, D]) for i in range(2)]
    srcs = [g.w_br[l, jb].rearrange("(kc p) n -> p kc n", p=128) for jb in range(3)] + \
           [g.w_out[l].rearrange("(kc p) n -> p kc n", p=128)[:, 0:4, :], g.w_out[l].rearrange("(kc p) n -> p kc n", p=128)[:, 4:8, :]]
    dsts = [wbr[:, 0:4, :], wbr[:, 4:8, :], wbr[:, 8:12, :], wo[:, 0:4, :], wo[:, 4:8, :]]
    for k in range(5):
        i = k % 2
        P.dma("sp", lambda e, k=k, i=i: e.dma_start(out=stg[i], in_=srcs[k]), writes=["mstg%d" % i])
        P.op("pool", lambda e, k=k, i=i: e.tensor_copy(out=dsts[k], in_=stg[i]), reads=["mstg%d" % i], writes=["mw%d" % k])
    wreads = ["mw%d" % k for k in range(5)]
    yin = [A.alloc("myin%d" % i, [12, 512], BF16) for i in range(2)]
    gl = [A.alloc("mgl%d" % i, [3, 512], BF16) for i in range(2)]
    sgm = [A.alloc("msg%d" % i, [3, 512]) for i in range(2)]
    tj = [A.alloc("mtj%d" % i, [3, 512]) for i in range(2)]
    mT = A.alloc("mT", [8, 512], BF16)
    yT = A.alloc("myT", [8, 512])
    ysq = A.alloc("mysq", [8, 512], BF16)
    xin = A.alloc("mxin", [8, 512])
    rs = A.alloc("mrs", [512])
    xsrc = (g.xT_in if l == 0 else g.xs).rearrange("(kc p) t -> p kc t", p=128)
    xdst = g.xs.rearrange("(kc p) t -> p kc t", p=128)
    odst = g.outT.rearrange("(kc p) t -> p kc t", p=128)
    NTT = NT // 512
    gsrc = g.PT[40:64].rearrange("(j t) p n -> t p j n", j=3)
    ev = 0

    def loady(T):
        i = T % 2
        P.dma("sp", lambda e: e.dma_start(out=yin[i], in_=g.Y3[:, :, :, T * 512:(T + 1) * 512].rearrange("j k p n -> p (j k) n")),
              reads=["Y3"], writes=["myin%d" % i])

    loady(0)
    gcnt = 0
    for T in range(NTT):
        i = T % 2
        if T + 1 < NTT:
            loady(T + 1)
        P.dma("sp", lambda e, T=T: e.dma_start(out=xin, in_=xsrc[:, :, T * 512:(T + 1) * 512]), reads=["xs"], writes=["mxin"])
        for dt in range(8):
            gi = gcnt % 2; gcnt += 1
            P.dma("sp", lambda e, dt=dt, T=T, gi=gi: e.dma_start(out=gl[gi], in_=gsrc[dt][:, :, T * 512:(T + 1) * 512]), writes=["mgl%d" % gi])
            P.op("act", lambda e, gi=gi: e.activation(out=sgm[gi], in_=gl[gi], func=AF.Sigmoid), reads=["mgl%d" % gi], writes=["msg%d" % gi])
            for jb in range(3):
                bank = (ev % 6); ev += 1
                ps = g.ps[bank]
                for kc in range(4):
                    P.op("pe", lambda e, ps=ps, jb=jb, kc=kc, dt=dt, i=i: e.matmul(
                        ps, lhsT=wbr[:, jb * 4 + kc, dt * 128:(dt + 1) * 128], rhs=yin[i][:, jb * 4 + kc, :], start=(kc == 0), stop=(kc == 3)),
                        reads=wreads + ["myin%d" % i], writes=["mps%d" % bank])
                P.op("dve", lambda e, ps=ps, jb=jb, gi=gi: e.tensor_tensor(out=tj[gi][:, jb, :], in0=ps, in1=sgm[gi][:, jb, :], op=ALU.mult),
                     reads=["mps%d" % bank, "msg%d" % gi], writes=["mtj%d_%d" % (gi, jb)])
            P.op("pool", lambda e, gi=gi: e.tensor_tensor(out=tj[gi][:, 0, :], in0=tj[gi][:, 0, :], in1=tj[gi][:, 1, :], op=ALU.add),
                 reads=["mtj%d_0" % gi, "mtj%d_1" % gi], writes=["mtj%d_0" % gi])
            P.op("pool", lambda e, gi=gi, dt=dt: e.tensor_tensor(out=mT[:, dt, :], in0=tj[gi][:, 0, :], in1=tj[gi][:, 2, :], op=ALU.add),
                 reads=["mtj%d_0" % gi, "mtj%d_2" % gi], writes=["mT%d" % dt])
        mreads = ["mT%d" % dt for dt in range(8)]
        for d2 in range(8):
            bank = (ev % 6); ev += 1
            ps = g.ps[bank]
            for kc in range(8):
                P.op("pe", lambda e, ps=ps, kc=kc, d2=d2: e.matmul(ps, lhsT=wo[:, kc, d2 * 128:(d2 + 1) * 128], rhs=mT[:, kc, :],
                                                                   start=(kc == 0), stop=(kc == 7)), reads=wreads + mreads, writes=["mps%d" % bank])
            P.op("act", lambda e, ps=ps, d2=d2: e.copy(out=yT[:, d2, :], in_=ps), reads=["mps%d" % bank], writes=["myT%d" % d2])
            P.op("act", lambda e, d2=d2: e.activation(out=ysq[:, d2, :], in_=yT[:, d2, :], func=AF.Square), reads=["myT%d" % d2], writes=["mysq%d" % d2])
        ssp = g.ps[6]
        for d2 in range(8):
            P.op("pe", lambda e, d2=d2: e.matmul(ssp, lhsT=g.ones_bf, rhs=ysq[:, d2, :], start=(d2 == 0), stop=(d2 == 7)),
                 reads=["mysq%d" % d2], writes=["mss"])
        rsqrt_act(P, rs, ssp, float(D * EPS), ["mss"], ["mrs"])
        for half in range(2):
            pc = 2 * T + half
            j = seg_j(pc)
            isctx = (pc % 9 == 0)
            if last and isctx:
                continue
            c0, c1 = half * 256, half * 256 + 256
            for d2 in range(8):
                P.op("dve", lambda e, d2=d2, j=j, c0=c0, c1=c1: e.scalar_tensor_tensor(
                    out=yT[:, d2, c0:c1], in0=yT[:, d2, c0:c1], scalar=g.modG[:, d2, j:j + 1], in1=rs[:, c0:c1], op0=ALU.mult, op1=ALU.mult),
                    reads=["myT%d" % d2, "mrs"], writes=["myT%d" % d2])
            P.op("pool", lambda e, c0=c0, c1=c1: e.tensor_tensor(out=xin[:, :, c0:c1], in0=xin[:, :, c0:c1], in1=yT[:, :, c0:c1], op=ALU.add),
                 reads=["mxin"] + ["myT%d" % d2 for d2 in range(8)], writes=["mxin"])
            if last:
                b = pc // 9
                q = pc % 9 - 1
                oc = b * LAT + q * 256
                P.dma("sp", lambda e, c0=c0, c1=c1, oc=oc: e.dma_start(out=odst[:, :, oc:oc + 256], in_=xin[:, :, c0:c1]), reads=["mxin"], writes=["out"])
            else:
                P.dma("sp", lambda e, c0=c0, c1=c1, pc=pc: e.dma_start(out=xdst[:, :, pc * 256:(pc + 1) * 256], in_=xin[:, :, c0:c1]),
                      reads=["mxin"], writes=["xs_w"])


def host_lruw(inp):
    out = np.zeros((DEPTH, 128, 16, 128), np.float32)
    for gi, name in enumerate(("lru_wa", "lru_wx")):
        w = inp[name]
        for d in range(2):
            for ct in range(4):
                for hb in range(2):
                    out[:, hb * 64:(hb + 1) * 64, gi * 8 + d * 4 + ct, hb * 64:(hb + 1) * 64] = w[:, d, ct * 2 + hb]
    return out.reshape(DEPTH, 128, 16 * 128)


def host_rpbias(inp):
    rpb = inp["rpb"]
    out = np.full((DEPTH, 4, 128, 5, 640), NEG, np.float32)
    cq = np.arange(64)
    kc = np.arange(64)
    win = np.clip(cq - 8, 0, 48)
    col_ok = (kc[None, :] >= win[:, None]) & (kc[None, :] < win[:, None] + 16)
    dc = np.clip(kc[None, :] - cq[:, None], -15, 15) + 15
    for pat, ti in enumerate((0, 1, 2, 14, 15)):
        sr = min(max(2 * ti - 4, 0), 22)
        for a in range(2):
            r = 2 * ti + a
            r0 = min(max(r - 4, 0), 24)
            for m in range(10):
                kr = sr + m
                if r0 <= kr < r0 + 8:
                    dr = kr - r + 7
                    vals = rpb[:, :, dr, :][:, :, dc]
                    blk = np.where(col_ok[None, None], vals, NEG)
                    out[:, :, a * 64:(a + 1) * 64, pat, m * 64:(m + 1) * 64] = blk
    return out.reshape(DEPTH, 4, 128, 5 * 640)


def prep_inputs(inp, nl=DEPTH):
    w_in = inp["w_in"]
    w_big = np.ascontiguousarray(np.concatenate([w_in[:, :, :2048], w_in[:, :, 2064:]], axis=2))
    w_small = np.ascontiguousarray(w_in[:, :, 2048:2064])
    prm = host_params(inp)
    cst = host_consts()
    lruw = host_lruw(inp)
    rpbias = host_rpbias(inp)
    rope = host_rope().reshape(128, 2 * LAT)
    maps = []
    nb_total = inp["x"].shape[0]
    for core in range(nb_total // NB):
        xs = []
        cs = []
        for b in range(core * NB, (core + 1) * NB):
            xs.append(inp["ctx"][b].T)
            xs.append(inp["x"][b].T)
            cs.append(inp["c"][b])
        cs.append(inp["c_ctx"])
        xT = np.ascontiguousarray(np.concatenate(xs, axis=1))
        cT = np.ascontiguousarray(np.stack(cs, axis=1).reshape(8, 128, 3).transpose(1, 0, 2).reshape(128, 24))
        maps.append({"xT": xT, "cT": cT, "w_mod": inp["w_mod"][:nl], "w_big": w_big[:nl], "w_small": w_small[:nl],
                     "prm": prm[:nl], "cst": cst, "rope": rope, "lruw": lruw[:nl], "rpbias": rpbias[:nl],
                     "w_br": inp["w_branch"][:nl], "w_out": inp["w_out"][:nl]})
    return maps


def kernel(**inputs):
    inp = {k: np.asarray(v) for k, v in inputs.items()}
    maps = prep_inputs(inp)
    nc = build()
    res = run_bass_kernel_spmd(nc, maps, core_ids=list(range(len(maps))))
    outs = []
    for r in res.results:
        oT = np.asarray(r["outT"])
        for b in range(NB):
            outs.append(oT[:, b * LAT:(b + 1) * LAT].T)
    return np.ascontiguousarray(np.stack(outs, axis=0)).astype(np.float32)
```
